# Optimizing a Trainium2 kernel written in Bass

```python
import math
import jax
import jax.numpy as jnp
from jax import lax
import numpy as np

D_MODEL = 1024
BATCH = 4
SEQ = 8192
DEPTH = 4

GRID_W = 64
CTX_LEN = 256
N_EVEN = (DEPTH + 1) // 2
N_ODD = DEPTH // 2

GLA_HEADS = 4
GLA_DK = 64
GLA_DV = 128
GLA_GATE_RANK = 16
GLA_TAU = 16.0
GLA_CHUNK = 64

DIFF_HEADS = 4
DIFF_DQK = 64
DIFF_DV = 2 * DIFF_DQK
Q_BLOCK = 128
ROPE_BASE = 10000.0

SSD_D_INNER = 2 * D_MODEL
SSD_HEAD_DIM = 64
SSD_HEADS = SSD_D_INNER // SSD_HEAD_DIM
SSD_GROUPS = 4
SSD_STATE = 128
SSD_CONV = 5
SSD_CHUNK = 128
SSD_CONV_CH = SSD_D_INNER + 2 * SSD_GROUPS * SSD_STATE
SSD_IN = SSD_D_INNER + SSD_CONV_CH + 2 * SSD_HEADS

FFN_HIDDEN = 256 * (-(-8 * D_MODEL // (3 * 256)))

EV_SIZES = (GLA_HEADS * GLA_DK, GLA_HEADS * GLA_DK, GLA_HEADS * GLA_DV, GLA_HEADS * GLA_DV,
            GLA_GATE_RANK, GLA_GATE_RANK,
            DIFF_HEADS * 2 * DIFF_DQK, DIFF_HEADS * 2 * DIFF_DQK, DIFF_HEADS * DIFF_DV)
EV_IN = sum(EV_SIZES)
EV_MIX = GLA_HEADS * GLA_DV + DIFF_HEADS * DIFF_DV

ALPHA = (2 * DEPTH) ** 0.25
BETA = (8 * DEPTH) ** -0.25
LN_EPS = 1e-6
RMS_EPS = 1e-6
F32 = jnp.float32

kernel_name = 'hybrid_gla_diffattn_ssd_deepnorm_prefix'


def _split(t, sizes):
    out, start = [], 0
    for s in sizes:
        out.append(t[..., start:start + s])
        start += s
    return out


def layer_norm(x, g, b):
    xf = x.astype(F32)
    mu = jnp.mean(xf, -1, keepdims=True)
    var = jnp.mean(jnp.square(xf - mu), -1, keepdims=True)
    return ((xf - mu) * lax.rsqrt(var + LN_EPS) * g + b).astype(x.dtype)


def rms_norm(x, g):
    xf = x.astype(F32)
    return (xf * lax.rsqrt(jnp.mean(xf * xf, -1, keepdims=True) + RMS_EPS) * g).astype(x.dtype)


def group_rms_norm(y, g):
    yf = y.astype(F32)
    yg = yf.reshape(yf.shape[:-1] + (SSD_GROUPS, -1))
    yg = yg * lax.rsqrt(jnp.mean(yg * yg, -1, keepdims=True) + RMS_EPS)
    return (yg.reshape(yf.shape) * g).astype(y.dtype)


def to_heads(t, n):
    b, s, _ = t.shape
    return t.reshape(b, s, n, -1).transpose(0, 2, 1, 3)


def from_heads(t):
    b, h, s, d = t.shape
    return t.transpose(0, 2, 1, 3).reshape(b, s, h * d)


def swiglu(h, w_in, w_out):
    gate, up = jnp.split(h @ w_in, 2, axis=-1)
    return (jax.nn.silu(gate) * up) @ w_out


def rope_angles(n_tokens):
    rows = n_tokens // GRID_W
    row = jnp.repeat(jnp.arange(rows, dtype=F32), GRID_W)
    col = jnp.tile(jnp.arange(GRID_W, dtype=F32), rows)
    n_freq = DIFF_DQK // 4
    inv = ROPE_BASE ** (-jnp.arange(n_freq, dtype=F32) / n_freq)
    return row[:, None] * inv, col[:, None] * inv


def _rotate(t, ang):
    t1, t2 = jnp.split(t, 2, axis=-1)
    cos, sin = jnp.cos(ang), jnp.sin(ang)
    return jnp.concatenate([t1 * cos - t2 * sin, t2 * cos + t1 * sin], axis=-1)


def axial_rope(t, ang_row, ang_col):
    tr, tc = jnp.split(t.astype(F32), 2, axis=-1)
    return jnp.concatenate([_rotate(tr, ang_row), _rotate(tc, ang_col)], axis=-1).astype(t.dtype)


def centred_dwconv(t, w):
    pad = SSD_CONV // 2
    return lax.conv_general_dilated(t, w[:, None, :].astype(t.dtype), window_strides=(1,),
                                    padding=[(pad, pad)], dimension_numbers=('NWC', 'WIO', 'NWC'),
                                    feature_group_count=t.shape[-1])


def gla_chunked(q, k, v, log_a, s0):
    bsz, h, t, _ = k.shape
    n, c = t // GLA_CHUNK, GLA_CHUNK
    k, v, log_a = [z.astype(F32).reshape(bsz, h, n, c, -1) for z in (k, v, log_a)]
    b = jnp.cumsum(log_a, axis=3)
    b_last = b[:, :, :, -1:]
    chunk_kv = jnp.einsum('bhnck,bhncv->bhnkv', k * jnp.exp(b_last - b), v)
    decay = jnp.exp(b_last[:, :, :, 0])

    def step(s, inp):
        d, kv = inp
        return d[..., None] * s + kv, s

    s_final, s_start = lax.scan(step, s0.astype(F32),
                                (jnp.moveaxis(decay, 2, 0), jnp.moveaxis(chunk_kv, 2, 0)))
    if q is None:
        return None, s_final
    s_start = jnp.moveaxis(s_start, 0, 2)
    q_dec = q.astype(F32).reshape(bsz, h, n, c, -1) * jnp.exp(b)
    mask = jnp.tril(jnp.ones((c, c), dtype=bool))
    att = jnp.where(mask, jnp.einsum('bhnik,bhnjk->bhnij', q_dec, k * jnp.exp(-b)), 0.0)
    o = jnp.einsum('bhnij,bhnjv->bhniv', att, v) + jnp.einsum('bhnck,bhnkv->bhncv', q_dec, s_start)
    return o.reshape(bsz, h, t, -1), s_final


def gla_mixer(parts_c, parts_x, w_gate2, b_gate, norm_w, need_ctx):
    def prep(parts):
        q, k, v, g, r_f, r_b = parts
        la_f = jax.nn.log_sigmoid((r_f @ w_gate2[0] + b_gate[0]).astype(F32)) / GLA_TAU
        la_b = jax.nn.log_sigmoid((r_b @ w_gate2[1] + b_gate[1]).astype(F32)) / GLA_TAU
        return (to_heads(q, GLA_HEADS) * GLA_DK ** -0.5, to_heads(k, GLA_HEADS), to_heads(v, GLA_HEADS),
                to_heads(la_f, GLA_HEADS), to_heads(la_b, GLA_HEADS), g)

    qc, kc, vc, lfc, lbc, gc = prep(parts_c)
    qx, kx, vx, lfx, lbx, gx = prep(parts_x)
    flip = lambda t: jnp.flip(t, axis=2)
    s0 = jnp.zeros(kc.shape[:2] + (GLA_DK, GLA_DV), F32)
    o_cf, s_f = gla_chunked(qc if need_ctx else None, kc, vc, lfc, s0)
    o_cb, s_b = gla_chunked(flip(qc) if need_ctx else None, flip(kc), flip(vc), flip(lbc), s0)
    o_xf, _ = gla_chunked(qx, kx, vx, lfx, s_f)
    o_xb, _ = gla_chunked(flip(qx), flip(kx), flip(vx), flip(lbx), s_b)

    def finish(o, g):
        return from_heads(rms_norm(o, norm_w)).astype(g.dtype) * jax.nn.silu(g)

    out_x = finish(o_xf + flip(o_xb), gx)
    out_c = finish(o_cf + flip(o_cb), gc) if need_ctx else None
    return out_c, out_x


def diff_attend(q, k, v, lam):
    s = jnp.einsum('bhmqd,bhmkd->bhmqk', q, k).astype(F32) * DIFF_DQK ** -0.5
    p = jax.nn.softmax(s, axis=-1)
    w = p[:, :, 0] - lam * p[:, :, 1]
    return jnp.einsum('bhqk,bhkd->bhqd', w.astype(v.dtype), v)


def diff_mixer(parts_c, parts_x, lam_p, norm_w, layer, ang_row, ang_col, need_ctx):
    def prep(q, k, v):
        b, s, _ = q.shape
        q = q.reshape(b, s, DIFF_HEADS, 2, DIFF_DQK).transpose(0, 2, 3, 1, 4)
        k = k.reshape(b, s, DIFF_HEADS, 2, DIFF_DQK).transpose(0, 2, 3, 1, 4)
        return q, k, to_heads(v, DIFF_HEADS)

    qc, kc, vc = prep(*parts_c)
    qx, kx, vx = prep(*parts_x)
    qx = axial_rope(qx, ang_row, ang_col)
    kx = axial_rope(kx, ang_row, ang_col)
    lam_init = 0.8 - 0.6 * math.exp(-0.3 * layer)
    lp = lam_p.astype(F32)
    lam = jnp.exp(jnp.sum(lp[0] * lp[1])) - jnp.exp(jnp.sum(lp[2] * lp[3])) + lam_init
    k_all = jnp.concatenate([kc, kx], axis=3)
    v_all = jnp.concatenate([vc, vx], axis=2)
    b, h, _, t, d = qx.shape
    nb = t // Q_BLOCK
    q_blocks = jnp.moveaxis(qx.reshape(b, h, 2, nb, Q_BLOCK, d), 3, 0)
    o_blocks = lax.map(lambda qb: diff_attend(qb, k_all, v_all, lam), q_blocks)
    o_x = jnp.moveaxis(o_blocks, 0, 2).reshape(b, h, t, DIFF_DV)

    def finish(o):
        return from_heads(rms_norm(o, norm_w) * (1.0 - lam_init))

    out_x = finish(o_x)
    out_c = finish(diff_attend(qc, kc, vc, lam)) if need_ctx else None
    return out_c, out_x


def even_mixer(hc, hx, w_in, w_out, gla_w_gate2, gla_b_gate, gla_norm_w, diff_lambda, diff_norm_w,
               layer, ang_row, ang_col, need_ctx):
    pc = _split(hc @ w_in, EV_SIZES)
    px = _split(hx @ w_in, EV_SIZES)
    gla_c, gla_x = gla_mixer(pc[:6], px[:6], gla_w_gate2, gla_b_gate, gla_norm_w, need_ctx)
    dif_c, dif_x = diff_mixer(pc[6:], px[6:], diff_lambda, diff_norm_w, layer, ang_row, ang_col, need_ctx)
    out_x = jnp.concatenate([gla_x, dif_x], axis=-1) @ w_out
    out_c = jnp.concatenate([gla_c, dif_c], axis=-1) @ w_out if need_ctx else None
    return out_c, out_x


def ssd_chunked(x, dt, a, bm, cm, s0, with_out):
    bsz, t, h, p = x.shape
    g, s = bm.shape[-2:]
    e, c = h // g, SSD_CHUNK
    n = t // c
    x = x.astype(F32).reshape(bsz, n, c, g, e, p)
    dt = dt.astype(F32).reshape(bsz, n, c, g, e)
    bm = bm.astype(F32).reshape(bsz, n, c, g, s)
    cm = cm.astype(F32).reshape(bsz, n, c, g, s)
    acs = jnp.cumsum(dt * a.astype(F32).reshape(g, e), axis=2)
    a_last = acs[:, :, -1]
    xdt = x * dt[..., None]
    chunk_state = jnp.einsum('bncgs,bncge,bncgep->bngeps', bm, jnp.exp(a_last[:, :, None] - acs), xdt)

    def step(st, inp):
        d, cs = inp
        return jnp.exp(d)[..., None, None] * st + cs, st

    s_final, s_start = lax.scan(step, s0.astype(F32),
                                (jnp.moveaxis(a_last, 1, 0), jnp.moveaxis(chunk_state, 1, 0)))
    if not with_out:
        return None, s_final
    s_start = jnp.moveaxis(s_start, 0, 1)
    acs_t = jnp.moveaxis(acs, 2, -1)
    mask = jnp.tril(jnp.ones((c, c), dtype=bool))
    decay = jnp.exp(jnp.where(mask, acs_t[..., :, None] - acs_t[..., None, :], -jnp.inf))
    cb = jnp.einsum('bnigs,bnjgs->bngij', cm, bm)
    y_diag = jnp.einsum('bngij,bngeij,bnjgep->bnigep', cb, decay, xdt)
    y_off = jnp.einsum('bncgs,bngeps,bncge->bncgep', cm, s_start, jnp.exp(acs))
    return (y_diag + y_off).reshape(bsz, t, h, p), s_final


def odd_mixer(hc, hx, w_in, conv_w, conv_b, dt_bias, a_log, d_skip, norm_w, w_out, need_ctx):
    def prep(hh):
        b, s, _ = hh.shape
        z, xbc, dt = _split(hh @ w_in, (SSD_D_INNER, SSD_CONV_CH, 2 * SSD_HEADS))
        xbc = jax.nn.silu(centred_dwconv(xbc, conv_w) + conv_b)
        xs, bm, cm = _split(xbc, (SSD_D_INNER, SSD_GROUPS * SSD_STATE, SSD_GROUPS * SSD_STATE))
        dt = jax.nn.softplus(dt.astype(F32).reshape(b, s, 2, SSD_HEADS) + dt_bias.astype(F32))
        return (z, xs.reshape(b, s, SSD_HEADS, SSD_HEAD_DIM), bm.reshape(b, s, SSD_GROUPS, SSD_STATE),
                cm.reshape(b, s, SSD_GROUPS, SSD_STATE), dt[:, :, 0], dt[:, :, 1])

    zc, xc, bc, cc, dfc, dbc = prep(hc)
    zx, xx, bx, cx, dfx, dbx = prep(hx)
    a = -jnp.exp(a_log.astype(F32))
    flip = lambda t: jnp.flip(t, axis=1)
    s0 = jnp.zeros((hc.shape[0], SSD_GROUPS, SSD_HEADS // SSD_GROUPS, SSD_HEAD_DIM, SSD_STATE), F32)
    y_cf, s_f = ssd_chunked(xc, dfc, a[0], bc, cc, s0, need_ctx)
    y_cb, s_b = ssd_chunked(flip(xc), flip(dbc), a[1], flip(bc), flip(cc), s0, need_ctx)
    y_xf, _ = ssd_chunked(xx, dfx, a[0], bx, cx, s_f, True)
    y_xb, _ = ssd_chunked(flip(xx), flip(dbx), a[1], flip(bx), flip(cx), s_b, True)

    def finish(yf, yb, xs, z):
        y = yf + flip(yb) + d_skip.astype(F32)[:, None] * xs.astype(F32)
        b, s = y.shape[:2]
        y = y.reshape(b, s, SSD_D_INNER) * jax.nn.silu(z.astype(F32))
        return group_rms_norm(y, norm_w).astype(z.dtype) @ w_out

    out_x = finish(y_xf, y_xb, xx, zx)
    out_c = finish(y_cf, y_cb, xc, zc) if need_ctx else None
    return out_c, out_x


def setup_inputs(seed: int = 0) -> dict:
    key = jax.random.key(seed)
    keys = iter(jax.random.split(key, 40))

    def nrm(shape, scale):
        return scale * jax.random.normal(next(keys), shape, F32)

    def gain(shape):
        return 1.0 + nrm(shape, 0.02)

    D = D_MODEL
    dt0 = jnp.exp(jax.random.uniform(next(keys), (N_ODD, 2, SSD_HEADS), F32, math.log(1e-3), math.log(1e-1)))
    dt_bias = dt0 + jnp.log(-jnp.expm1(-dt0))
    a_log = jnp.log(jax.random.uniform(next(keys), (N_ODD, 2, SSD_HEADS), F32, 1.0, 16.0))
    return {
        'x': nrm((BATCH, SEQ, D), 1.0),
        'c': nrm((BATCH, D), 1.0),
        'ctx': nrm((BATCH, CTX_LEN, D), 1.0),
        'c_ctx': nrm((D,), 1.0),
        'mod_w': nrm((DEPTH, D, 6 * D), D ** -0.5),
        'mod_b': nrm((DEPTH, 6 * D), 0.01),
        'ln_g': gain((DEPTH, 2, D)),
        'ln_b': nrm((DEPTH, 2, D), 0.02),
        'ffn_w_in': nrm((DEPTH, D, 2 * FFN_HIDDEN), D ** -0.5),
        'ffn_w_out': nrm((DEPTH, FFN_HIDDEN, D), BETA * FFN_HIDDEN ** -0.5),
        'ev_w_in': nrm((N_EVEN, D, EV_IN), D ** -0.5),
        'ev_w_out': nrm((N_EVEN, EV_MIX, D), BETA * EV_MIX ** -0.5),
        'gla_w_gate2': nrm((N_EVEN, 2, GLA_GATE_RANK, GLA_HEADS * GLA_DK), GLA_GATE_RANK ** -0.5),
        'gla_b_gate': nrm((N_EVEN, 2, GLA_HEADS * GLA_DK), 0.1),
        'gla_norm_w': gain((N_EVEN, GLA_DV)),
        'diff_lambda': nrm((N_EVEN, 4, DIFF_DQK), 0.1),
        'diff_norm_w': gain((N_EVEN, DIFF_DV)),
        'ssd_w_in': nrm((N_ODD, D, SSD_IN), D ** -0.5),
        'ssd_conv_w': nrm((N_ODD, SSD_CONV, SSD_CONV_CH), SSD_CONV ** -0.5),
        'ssd_conv_b': nrm((N_ODD, SSD_CONV_CH), 0.02),
        'ssd_dt_bias': dt_bias,
        'ssd_a_log': a_log,
        'ssd_d': gain((N_ODD, SSD_HEADS)),
        'ssd_norm_w': gain((N_ODD, SSD_D_INNER)),
        'ssd_w_out': nrm((N_ODD, SSD_D_INNER, D), BETA * SSD_D_INNER ** -0.5),
    }


def reference(x, c, ctx, c_ctx, mod_w, mod_b, ln_g, ln_b, ffn_w_in, ffn_w_out, ev_w_in, ev_w_out,
              gla_w_gate2, gla_b_gate, gla_norm_w, diff_lambda, diff_norm_w, ssd_w_in, ssd_conv_w,
              ssd_conv_b, ssd_dt_bias, ssd_a_log, ssd_d, ssd_norm_w, ssd_w_out):
    ang_row, ang_col = rope_angles(x.shape[1])
    for layer in range(DEPTH):
        need_ctx = layer < DEPTH - 1
        mod_x = jax.nn.silu(c) @ mod_w[layer] + mod_b[layer]
        mod_c = jax.nn.silu(c_ctx) @ mod_w[layer] + mod_b[layer]
        shm_x, scm_x, gm_x, shf_x, scf_x, gf_x = [m[:, None, :] for m in jnp.split(mod_x, 6, axis=-1)]
        shm_c, scm_c, gm_c, shf_c, scf_c, gf_c = jnp.split(mod_c, 6, axis=-1)
        hx = x * (1.0 + scm_x) + shm_x
        hc = ctx * (1.0 + scm_c) + shm_c
        if layer % 2 == 0:
            e = layer // 2
            mix_c, mix_x = even_mixer(hc, hx, ev_w_in[e], ev_w_out[e], gla_w_gate2[e], gla_b_gate[e],
                                      gla_norm_w[e], diff_lambda[e], diff_norm_w[e], layer,
                                      ang_row, ang_col, need_ctx)
        else:
            o = layer // 2
            mix_c, mix_x = odd_mixer(hc, hx, ssd_w_in[o], ssd_conv_w[o], ssd_conv_b[o], ssd_dt_bias[o],
                                     ssd_a_log[o], ssd_d[o], ssd_norm_w[o], ssd_w_out[o], need_ctx)
        x = layer_norm(ALPHA * x + gm_x * mix_x, ln_g[layer, 0], ln_b[layer, 0])
        x = layer_norm(ALPHA * x + gf_x * swiglu(x * (1.0 + scf_x) + shf_x, ffn_w_in[layer], ffn_w_out[layer]),
                       ln_g[layer, 1], ln_b[layer, 1])
        if need_ctx:
            ctx = layer_norm(ALPHA * ctx + gm_c * mix_c, ln_g[layer, 0], ln_b[layer, 0])
            ctx = layer_norm(ALPHA * ctx + gf_c * swiglu(ctx * (1.0 + scf_c) + shf_c, ffn_w_in[layer],
                                                         ffn_w_out[layer]),
                             ln_g[layer, 1], ln_b[layer, 1])
    return x
```

```python
import contextlib
import math
import numpy as np
import concourse.bass as bass
import concourse.mybir as mybir
from concourse.bass_utils import run_bass_kernel_spmd

F32 = mybir.dt.float32
BF16 = mybir.dt.bfloat16
AF = mybir.ActivationFunctionType
ALU = mybir.AluOpType
AX = mybir.AxisListType

ENGS = ('pe', 'act', 'dve', 'pool', 'sp')
DQS = ('sp', 'pool', 'act')
SEM_LIMIT = 30000
N_EPOCH = 12
N_DMASEM = 8


class Res:
    __slots__ = ('w', 'r')

    def __init__(self):
        self.w = None
        self.r = []


class T:
    def __init__(self, ap, res=None):
        self.ap = ap
        self.res = res if res is not None else Res()

    def __getitem__(self, idx):
        return self.ap[idx]


def _res(t):
    return t.res if isinstance(t, T) else t


class _Rec:
    def __getattr__(self, name):
        def f(*a, **k):
            self.call = (name, a, k)
        return f


class Prog:
    def __init__(self, nc, same_engine_sync=True):
        self.nc = nc
        self.same = same_engine_sync
        self.streams = {e: [] for e in ENGS}
        self.sems = {e: [nc.alloc_semaphore(name=f"s_{e}_{i}") for i in range(N_EPOCH)] for e in ENGS}
        self.ops = {e: [] for e in ENGS}
        self.seen = {e: {} for e in ENGS}
        self.dsems = {q: [nc.alloc_semaphore(name=f"d_{q}_{i}") for i in range(N_DMASEM)] for q in DQS}
        self.dcnt = {q: 0 for q in DQS}
        self.nops = 0

    def _need(self, eng, ev):
        kind, src, val = ev
        if kind == 'c':
            if src == eng and (eng == 'pe' or not self.same):
                return
            if self.seen[eng].get(('c', src), 0) >= val:
                return
            self.seen[eng][('c', src)] = val
            self.ops[src][val - 1]['signal'] = True
            self.streams[eng].append({'waitc': (src, val)})
        else:
            key = (kind, src[0], src[1])
            if self.seen[eng].get(key, 0) >= val:
                return
            self.seen[eng][key] = val
            self.streams[eng].append({'wait': (self.dsems[src[0]][src[1]], val)})

    def _collect(self, reads, writes):
        evs = []
        for t in reads:
            r = _res(t)
            if r.w is not None:
                evs.append(r.w)
        for t in writes:
            r = _res(t)
            if r.w is not None:
                evs.append(r.w)
            evs.extend(r.r)
        return evs

    def _mark(self, ev, reads, writes):
        for t in reads:
            _res(t).r.append(ev)
        for t in writes:
            r = _res(t)
            r.w = ev
            r.r = []

    def op(self, eng, fn, reads=(), writes=()):
        for ev in self._collect(reads, writes):
            self._need(eng, ev)
        r = _Rec()
        fn(r)
        rec = {'call': r.call, 'signal': False}
        self.streams[eng].append(rec)
        self.ops[eng].append(rec)
        ev = ('c', eng, len(self.ops[eng]))
        self._mark(ev, reads, writes)
        self.nops += 1
        return ev

    def dma(self, q, out, in_, reads=(), writes=(), **kw):
        evs = self._collect(reads, writes)
        k = self.dcnt[q]
        self.dcnt[q] += 1
        si, rnd = k % N_DMASEM, k // N_DMASEM
        if rnd > 0:
            evs.append(('d', (q, si), 16 * rnd))
        for ev in evs:
            self._need(q, ev)
        rec = {'dma': (out, in_, kw), 'sig': (self.dsems[q][si], 16)}
        self.streams[q].append(rec)
        ev = ('d', (q, si), 16 * (rnd + 1))
        self._mark(ev, reads, writes)
        self.nops += 1
        return ev

    def barrier(self):
        evs = []
        for e in ENGS:
            if self.ops[e]:
                evs.append(('c', e, len(self.ops[e])))
        for q in DQS:
            for si in range(N_DMASEM):
                n = (self.dcnt[q] - si + N_DMASEM - 1) // N_DMASEM if self.dcnt[q] > si else 0
                if n > 0:
                    evs.append(('d', (q, si), 16 * n))
        for e in ENGS:
            for ev in evs:
                if ev[0] == 'c' and ev[1] == e:
                    continue
                self._need(e, ev)

    def _resolve(self):
        for e in ENGS:
            epoch, c = 0, 0
            for rec in self.ops[e]:
                if rec['signal']:
                    c += 1
                    if c > SEM_LIMIT:
                        epoch += 1
                        c = 1
                        assert epoch < N_EPOCH, "out of semaphore epochs"
                    rec['semval'] = (self.sems[e][epoch], c)

    def _replay(self, eng_obj, stream):
        for rec in stream:
            if 'waitc' in rec:
                src, idx = rec['waitc']
                s, v = self.ops[src][idx - 1]['semval']
                eng_obj.wait_ge(s, v)
            elif 'wait' in rec:
                s, v = rec['wait']
                eng_obj.wait_ge(s, v)
            elif 'dma' in rec:
                out, in_, kw = rec['dma']
                ins = eng_obj.dma_start(out=out, in_=in_, **kw)
                s, v = rec['sig']
                ins.then_inc(s, v)
            else:
                name, a, k = rec['call']
                ins = getattr(eng_obj, name)(*a, **k)
                if rec['signal']:
                    ins.then_inc(rec['semval'][0], 1)

    def emit(self):
        self._resolve()
        with self.nc.Block() as block:
            @block.tensor
            def _(e):
                self._replay(e, self.streams['pe'])

            @block.scalar
            def _(e):
                self._replay(e, self.streams['act'])

            @block.vector
            def _(e):
                self._replay(e, self.streams['dve'])

            @block.gpsimd
            def _(e):
                self._replay(e, self.streams['pool'])

            @block.sync
            def _(e):
                self._replay(e, self.streams['sp'])


CTX = 256
D = 1024
FFH = 2816
NJ = FFH // 128
ALPHA = (2 * 4) ** 0.25
LN_EPS = 1e-6
RMS_EPS = 1e-6
GRID_W = 64
ROPE_BASE = 10000.0
GROUPS = [[0, 4], [1, 5], [2, 6], [3, 7]]
EVC = 1568
SDC = 2592

C_IDENT, C_TRIF, C_TRIB, C_STRIF, C_STRIB, C_MASKF, C_MASKB, C_PERM, C_ONES, C_MEAN, C_UF, C_UB, C_SLF, C_SLB = range(14)
NCMAT = 14


def _const_mats():
    j = np.arange(128)[:, None]
    i = np.arange(128)[None, :]
    m = np.zeros((128, NCMAT, 128), np.float32)
    m[:, C_IDENT] = (j == i)
    m[:, C_TRIF] = (j <= i) * (-1.0 / 16)
    m[:, C_TRIB] = (j >= i) * (-1.0 / 16)
    m[:, C_STRIF] = (j > i) * (-1.0 / 16)
    m[:, C_STRIB] = (j < i) * (-1.0 / 16)
    m[:, C_MASKF] = (j <= i)
    m[:, C_MASKB] = (j >= i)
    m[:, C_PERM] = (j == (i ^ 16))
    m[:, C_ONES] = 1.0
    m[:, C_MEAN] = 1.0 / 1024
    m[:, C_UF] = (j <= i)
    m[:, C_UB] = (j >= i)
    m[:, C_SLF] = (j > i)
    m[:, C_SLB] = (j < i)
    return m


def _rope_tables(TL):
    t = np.arange(TL)
    row = (t // GRID_W).astype(np.float32)
    col = (t % GRID_W).astype(np.float32)
    nf = 16
    inv = (ROPE_BASE ** (-np.arange(nf, dtype=np.float32) / nf)).astype(np.float32)
    ar = row[None, :] * inv[:, None]
    ac = col[None, :] * inv[:, None]
    cos64 = np.concatenate([np.cos(ar), np.cos(ar), np.cos(ac), np.cos(ac)], 0)
    sin64 = np.concatenate([-np.sin(ar), np.sin(ar), -np.sin(ac), np.sin(ac)], 0)
    cos = np.concatenate([cos64, cos64], 0).astype(np.float32)
    sin = np.concatenate([sin64, sin64], 0).astype(np.float32)
    return np.ascontiguousarray(cos), np.ascontiguousarray(sin)


def _tiles(total, first, step):
    out = [(0, first)]
    s = first
    while s < total:
        n = min(step, total - s)
        out.append((s, n))
        s += n
    return out


def build(TL, DEPTH, dump=()):
    S = CTX + TL
    Sh = S // 2
    assert Sh % 128 == 0
    NCH = S // 128
    own512 = _tiles(Sh, 256, 512)
    own256 = _tiles(Sh, 256, 256)
    gtiles = [(t, l0, n) for t in range(2) for (l0, n) in own512]
    ctiles = [(c, t, l0, n) for c, (l0, n) in enumerate(own512) for t in range(2)]
    nc = bass.Bass("TRN2", target_bir_lowering=False)
    P = Prog(nc)

    def din(name, shape, dt=F32):
        return nc.dram_tensor(name, list(shape), dt, kind="ExternalInput").ap()

    def dscr(name, shape, dt):
        kind = "ExternalOutput" if name in dump else "Internal"
        return nc.dram_tensor(name, list(shape), dt, kind=kind).ap()

    xin = din("xin", [8, 128, Sh])
    csil = din("csil", [128, 8, 2])
    mctx = din("mctx", [128, 1])
    mod_w = din("mod_w", [4, D, 6 * D])
    mod_bT = din("mod_bT", [128, 4, 48])
    ln_gT = din("ln_gT", [128, 4, 2, 8])
    ln_bT = din("ln_bT", [128, 4, 2, 8])
    ffn_w_in = din("ffn_w_in", [4, D, 2 * FFH])
    ffn_w_out = din("ffn_w_out", [4, FFH, D])
    ev_w_in = din("ev_w_in", [2, D, EVC])
    ev_w_out = din("ev_w_out", [2, 512, D])
    gla_wg = din("gla_wg", [2, 17, 2, 128])
    gla_nw = din("gla_nw", [128, 2])
    diff_nw = din("diff_nw", [128, 2])
    diff_lam = din("diff_lam", [1, 2, 256])
    ssd_w_in = din("ssd_w_in", [2, D, SDC])
    ssd_w_out = din("ssd_w_out", [2, 1024, D])
    ssd_cw = din("ssd_cw", [2, 128, 5, 12])
    ssd_cb = din("ssd_cb", [2, 128, 12])
    ssd_dtb = din("ssd_dtb", [2, 128, 32])
    ssd_alog = din("ssd_alog", [2, 128, 32])
    ssd_dsk = din("ssd_dsk", [2, 128, 16])
    ssd_nw = din("ssd_nw", [2, 128, 1024])
    cmat_d = din("cmat", [128, NCMAT, 128])
    cos_d = din("ropecos", [128, TL])
    sin_d = din("ropesin", [128, TL])
    yout = nc.dram_tensor("yout", [8, 128, Sh], F32, kind="ExternalOutput").ap()

    X2 = dscr("X2", [8, 128, Sh], F32)
    Hloc = dscr("Hloc", [8, 128, Sh], BF16)
    Hg = dscr("Hg", [8, 2, 128, Sh], BF16)
    Ppc = [dscr(f"Pp{c}", [2, 8, 128, n], BF16) for c, (l0, n) in enumerate(own512)]
    Moc = [dscr(f"Mo{c}", [8, 128, n], BF16) for c, (l0, n) in enumerate(own512)]
    QG = dscr("QG", [128, S], BF16)
    KG = dscr("KG", [128, S], BF16)
    GG = dscr("GG", [256, S], BF16)
    RR = dscr("RR", [32, S], BF16)
    QD = dscr("QD", [256, S], BF16)
    KD = dscr("KD", [256, S], BF16)
    KGt = dscr("KGt", [S, 128], BF16)
    VGt = dscr("VGt", [S, 256], BF16)
    VDt = dscr("VDt", [S, 256], BF16)
    OG = dscr("OG", [256, S], F32)
    DT = dscr("DT", [256, S], BF16)
    XBC = dscr("XBC", [1536, S], BF16)
    ZS = dscr("ZS", [S, 1024], BF16)
    DTt = dscr("DTt", [S, 32], F32)
    BTd = dscr("BTd", [256, S], BF16)
    CTd = dscr("CTd", [256, S], BF16)
    XSt = dscr("XSt", [S, 1024], BF16)
    Btd = dscr("Btd", [S, 256], BF16)
    YF = dscr("YF", [S, 1024], F32)
    YNT = dscr("YNT", [1024, S], BF16)

    def fm(x3):
        return x3.rearrange("k p s -> p k s")

    with contextlib.ExitStack() as gs:
        cnt = [0]

        def gsb(shape, dt, name=None):
            cnt[0] += 1
            return T(gs.enter_context(nc.sbuf_tensor(name or f"g{cnt[0]}", list(shape), dt)))

        psum = gs.enter_context(nc.psum_tensor("psum", [128, 8, 512], F32))
        PB = [T(psum[:, i, :]) for i in range(8)]

        class Phase:
            def __enter__(self):
                self.es = contextlib.ExitStack()
                return self

            def sb(self, shape, dt, name=None):
                cnt[0] += 1
                return T(self.es.enter_context(nc.sbuf_tensor(name or f"t{cnt[0]}", list(shape), dt)))

            def __exit__(self, *a):
                P.barrier()
                self.es.close()
                return False

        ccres = Res()
        ppres = [Res() for _ in own512]

        def rs_chunk(c):
            P.op('pool', lambda e: e.collective_compute("ReduceScatter", ALU.add, replica_groups=GROUPS,
                                                        ins=[Ppc[c].rearrange("t k p s -> (t k p) s")],
                                                        outs=[Moc[c].rearrange("k p s -> (k p) s")]),
                 reads=[ppres[c]], writes=[ccres])

        def mo_view(l0, n):
            for c, (c0, cn) in enumerate(own512):
                if c0 <= l0 and l0 + n <= c0 + cn:
                    return Moc[c].rearrange("k p s -> p k s")[:, :, l0 - c0:l0 - c0 + n]
            raise AssertionError((l0, n))

        def all_gather(src2d, dst2d):
            P.barrier()
            P.op('pool', lambda e: e.collective_compute("AllGather", ALU.bypass, replica_groups=GROUPS,
                                                        ins=[src2d], outs=[dst2d]), writes=[ccres])
            P.barrier()

        def reduce_scatter(src2d, dst2d):
            P.barrier()
            P.op('pool', lambda e: e.collective_compute("ReduceScatter", ALU.add, replica_groups=GROUPS,
                                                        ins=[src2d], outs=[dst2d]), writes=[ccres])
            P.barrier()

        cmb = gsb([128, NCMAT, 128], BF16, "cmb")
        modT = gsb([128, 4, 48, 2], F32, "modT")
        mod1p = gsb([128, 4, 48, 2], F32, "mod1p")
        lng = gsb([128, 4, 2, 8], F32, "lng")
        lnb = gsb([128, 4, 2, 8], F32, "lnb")
        gnw = gsb([128, 2], F32, "gnw")
        dnw = gsb([128, 2], F32, "dnw")
        dnws = gsb([128, 2], F32, "dnws")
        nlam = gsb([128, 2], F32, "nlam")
        mcx = gsb([128, 1], F32, "mcx")

        def load_cmf(ph):
            t = ph.sb([128, NCMAT, 128], F32)
            P.dma('sp', t[:], cmat_d, writes=[t])
            return t
        P.dma('pool', cmb[:], cmat_d, writes=[cmb])
        P.dma('sp', lng[:], ln_gT, writes=[lng])
        P.dma('sp', lnb[:], ln_bT, writes=[lnb])
        P.dma('sp', gnw[:], gla_nw, writes=[gnw])
        P.dma('sp', dnw[:], diff_nw, writes=[dnw])
        P.dma('sp', mcx[:], mctx, writes=[mcx])

        def lam_init(layer):
            return 0.8 - 0.6 * math.exp(-0.3 * layer)

        with Phase() as ph:
            cmf = load_cmf(ph)
            sc = ph.sb([128, 8, 2], F32)
            modb = ph.sb([128, 4, 48], F32)
            mdiff = ph.sb([128, 4, 48], F32)
            P.dma('sp', sc[:], csil, writes=[sc])
            P.dma('sp', modb[:], mod_bT, writes=[modb])
            P.op('act', lambda e: e.activation(out=sc[:], in_=sc[:], func=AF.Silu), reads=[sc], writes=[sc])
            wb = [ph.sb([128, 8, 1024], F32) for _ in range(2)]
            it = 0
            for l in range(DEPTH):
                for cb in range(6):
                    wt = wb[it % 2]
                    pb = PB[it % 2]
                    it += 1
                    for k in range(8):
                        P.dma('sp' if k % 2 == 0 else 'act', wt[:, k, :],
                              mod_w[l, k * 128:(k + 1) * 128, cb * 1024:(cb + 1) * 1024], writes=[wt])
                    for m in range(8):
                        for k in range(8):
                            P.op('pe', lambda e, m=m, k=k, wt=wt, pb=pb: e.matmul(
                                pb[:, 2 * m:2 * m + 2], lhsT=wt[:, k, m * 128:(m + 1) * 128], rhs=sc[:, k, :],
                                start=(k == 0), stop=(k == 7)), reads=[wt, sc], writes=[pb])
                    P.op('dve', lambda e, l=l, cb=cb, pb=pb: e.tensor_tensor(
                        out=modT[:, l, cb * 8:(cb + 1) * 8, :],
                        in0=pb[:, 0:16].rearrange("p (m j) -> p m j", j=2),
                        in1=modb[:, l, cb * 8:(cb + 1) * 8].unsqueeze(2).to_broadcast([128, 8, 2]),
                        op=ALU.add), reads=[pb, modb], writes=[modT])
            P.op('dve', lambda e: e.tensor_tensor(out=mdiff[:], in0=modT[:, :, :, 1], in1=modT[:, :, :, 0], op=ALU.subtract),
                 reads=[modT], writes=[mdiff])
            P.op('dve', lambda e: e.scalar_tensor_tensor(out=modT[:, :, :, 1], in0=mdiff[:], scalar=mcx[:, 0:1], in1=modT[:, :, :, 0],
                                                         op0=ALU.mult, op1=ALU.add), reads=[mdiff, mcx, modT], writes=[modT])
            P.op('dve', lambda e: e.tensor_scalar_add(out=mod1p[:], in0=modT[:], scalar1=1.0),
                 reads=[modT], writes=[mod1p])
            dl = ph.sb([1, 2, 256], F32)
            pr = ph.sb([1, 2, 2, 64], F32)
            sm = ph.sb([1, 4], F32)
            lv = ph.sb([1, 2], F32)
            P.dma('sp', dl[:], diff_lam, writes=[dl])
            dlv = dl[:].rearrange("o e (a b d) -> o e a b d", a=2, b=2)
            P.op('dve', lambda e: e.tensor_tensor(out=pr[:], in0=dlv[:, :, :, 0, :], in1=dlv[:, :, :, 1, :], op=ALU.mult),
                 reads=[dl], writes=[pr])
            P.op('dve', lambda e: e.tensor_reduce(out=sm[:], in_=pr[:].rearrange("o e a d -> o (e a) d"),
                                                  axis=AX.X, op=ALU.add), reads=[pr], writes=[sm])
            P.op('act', lambda e: e.activation(out=sm[:], in_=sm[:], func=AF.Exp), reads=[sm], writes=[sm])
            smv = sm[:].rearrange("o (e a) -> o e a", a=2)
            P.op('dve', lambda e: e.tensor_tensor(out=lv[:], in0=smv[:, :, 1], in1=smv[:, :, 0], op=ALU.subtract),
                 reads=[sm], writes=[lv])
            for ee in range(2):
                P.op('dve', lambda e, ee=ee: e.tensor_scalar_add(out=lv[:, ee:ee + 1], in0=lv[:, ee:ee + 1],
                                                                 scalar1=-lam_init(2 * ee)), reads=[lv], writes=[lv])
            P.op('pe', lambda e: e.matmul(PB[2][:, 0:2], lhsT=cmf[0:1, C_ONES, :], rhs=lv[:], start=True, stop=True),
                 reads=[cmf, lv], writes=[PB[2]])
            P.op('dve', lambda e: e.tensor_copy(out=nlam[:], in_=PB[2][:, 0:2]), reads=[PB[2]], writes=[nlam])
            for ee in range(2):
                P.op('dve', lambda e, ee=ee: e.tensor_scalar_mul(out=dnws[:, ee:ee + 1], in0=dnw[:, ee:ee + 1],
                                                                 scalar1=1.0 - lam_init(2 * ee)), reads=[dnw], writes=[dnws])

        def load_w(ph, src, K, N):
            kc = K // 128
            w = ph.sb([128, kc, N], BF16)
            for k in range(kc):
                P.dma('pool', w[:, k, :], src[k * 128:(k + 1) * 128, :], writes=[w])
            return w

        def modulate(xt, ht, n, l, which_shift, which_scale, j):
            for k in range(8):
                P.op('act', lambda e, k=k: e.activation(
                    out=ht[:, k, :n], in_=xt[:, k, :n], func=AF.Identity,
                    scale=mod1p[:, l, which_scale * 8 + k, j:j + 1], bias=modT[:, l, which_shift * 8 + k, j:j + 1]),
                    reads=[xt, mod1p, modT], writes=[ht])

        def layer_norm(ph_bufs, v, n, l, which):
            vb, vsq, mean_sb, rstd = ph_bufs
            P.op('act', lambda e: e.activation(out=vb[:, 0:8, :n], in_=v[:, :, :n], func=AF.Copy), reads=[v], writes=[vb])
            P.op('pool', lambda e: e.tensor_tensor(out=vsq[:, 0:8, :n], in0=v[:, :, :n], in1=v[:, :, :n], op=ALU.mult),
                 reads=[v], writes=[vsq])
            for k in range(8):
                P.op('pe', lambda e, k=k: e.matmul(PB[6][:, :n], lhsT=cmb[:, C_MEAN, :], rhs=vb[:, k, :n],
                                                   start=(k == 0), stop=(k == 7)), reads=[cmb, vb], writes=[PB[6]])
            for k in range(8):
                P.op('pe', lambda e, k=k: e.matmul(PB[7][:, :n], lhsT=cmb[:, C_MEAN, :], rhs=vsq[:, k, :n],
                                                   start=(k == 0), stop=(k == 7)), reads=[cmb, vsq], writes=[PB[7]])
            P.op('act', lambda e: e.activation(out=mean_sb[:, :n], in_=PB[6][:, :n], func=AF.Copy),
                 reads=[PB[6]], writes=[mean_sb])
            P.op('dve', lambda e: e.tensor_tensor(out=rstd[:, :n], in0=mean_sb[:, :n], in1=mean_sb[:, :n], op=ALU.mult),
                 reads=[mean_sb], writes=[rstd])
            P.op('dve', lambda e: e.tensor_tensor(out=rstd[:, :n], in0=PB[7][:, :n], in1=rstd[:, :n], op=ALU.subtract),
                 reads=[PB[7], rstd], writes=[rstd])
            P.op('act', lambda e: e.activation(out=rstd[:, :n], in_=rstd[:, :n], func=AF.Sqrt, bias=LN_EPS, scale=1.0),
                 reads=[rstd], writes=[rstd])
            P.op('dve', lambda e: e.reciprocal(out=rstd[:, :n], in_=rstd[:, :n]), reads=[rstd], writes=[rstd])
            P.op('dve', lambda e: e.tensor_tensor(out=v[:, :, :n], in0=v[:, :, :n],
                                                  in1=mean_sb[:, :n].unsqueeze(1).to_broadcast([128, 8, n]),
                                                  op=ALU.subtract), reads=[v, mean_sb], writes=[v])
            P.op('pool', lambda e: e.tensor_tensor(out=v[:, :, :n], in0=v[:, :, :n],
                                                   in1=rstd[:, :n].unsqueeze(1).to_broadcast([128, 8, n]),
                                                   op=ALU.mult), reads=[v, rstd], writes=[v])
            for k in range(8):
                P.op('act', lambda e, k=k: e.activation(out=v[:, k, :n], in_=v[:, k, :n], func=AF.Identity,
                                                        scale=lng[:, l, which, k:k + 1], bias=lnb[:, l, which, k:k + 1]),
                     reads=[v, lng, lnb], writes=[v])

        def modh_phase(l, Xsrc):
            with Phase() as ph:
                xb = [ph.sb([128, 8, 512], F32) for _ in range(2)]
                hb = [ph.sb([128, 8, 512], BF16) for _ in range(2)]
                for ti, (l0, n) in enumerate(own512):
                    j = 1 if l0 == 0 else 0
                    xt, ht = xb[ti % 2], hb[ti % 2]
                    P.dma('sp', xt[:, :, :n], fm(Xsrc)[:, :, l0:l0 + n], writes=[xt])
                    modulate(xt, ht, n, l, 0, 1, j)
                    P.dma('act', fm(Hloc)[:, :, l0:l0 + n], ht[:, :, :n], reads=[ht])
            P.barrier()
            for k in range(8):
                P.op('pool', lambda e, k=k: e.collective_compute("AllGather", ALU.bypass, replica_groups=GROUPS,
                                                                 ins=[Hloc[k]], outs=[Hg[k].rearrange("t p s -> (t p) s")]),
                     writes=[ccres])
            P.barrier()

        def layer_norm2(v, n, l, which, sgt, mean_sb, rstd):
            for k in range(8):
                tt = sgt[k % 2]
                P.op('act', lambda e, k=k, tt=tt: e.activation(out=tt[:, :n], in_=v[:, k, :n], func=AF.Square), reads=[v], writes=[tt])
                P.op('pe', lambda e, k=k, tt=tt: e.matmul(PB[7][:, :n], lhsT=cmb[:, C_MEAN, :], rhs=tt[:, :n],
                                                          start=(k == 0), stop=(k == 7)), reads=[cmb, tt], writes=[PB[7]])
            for k in range(8):
                tt = sgt[k % 2]
                P.op('act', lambda e, k=k, tt=tt: e.activation(out=tt[:, :n], in_=v[:, k, :n], func=AF.Copy), reads=[v], writes=[tt])
                P.op('pe', lambda e, k=k, tt=tt: e.matmul(PB[6][:, :n], lhsT=cmb[:, C_MEAN, :], rhs=tt[:, :n],
                                                          start=(k == 0), stop=(k == 7)), reads=[cmb, tt], writes=[PB[6]])
            P.op('act', lambda e: e.activation(out=mean_sb[:, :n], in_=PB[6][:, :n], func=AF.Copy),
                 reads=[PB[6]], writes=[mean_sb])
            P.op('dve', lambda e: e.tensor_tensor(out=rstd[:, :n], in0=mean_sb[:, :n], in1=mean_sb[:, :n], op=ALU.mult),
                 reads=[mean_sb], writes=[rstd])
            P.op('dve', lambda e: e.tensor_tensor(out=rstd[:, :n], in0=PB[7][:, :n], in1=rstd[:, :n], op=ALU.subtract),
                 reads=[PB[7], rstd], writes=[rstd])
            P.op('act', lambda e: e.activation(out=rstd[:, :n], in_=rstd[:, :n], func=AF.Sqrt, bias=LN_EPS, scale=1.0),
                 reads=[rstd], writes=[rstd])
            P.op('dve', lambda e: e.reciprocal(out=rstd[:, :n], in_=rstd[:, :n]), reads=[rstd], writes=[rstd])
            P.op('dve', lambda e: e.tensor_tensor(out=v[:, :, :n], in0=v[:, :, :n],
                                                  in1=mean_sb[:, :n].unsqueeze(1).to_broadcast([128, 8, n]),
                                                  op=ALU.subtract), reads=[v, mean_sb], writes=[v])
            P.op('dve', lambda e: e.tensor_tensor(out=v[:, 0:5, :n], in0=v[:, 0:5, :n],
                                                  in1=rstd[:, :n].unsqueeze(1).to_broadcast([128, 5, n]),
                                                  op=ALU.mult), reads=[v, rstd], writes=[v])
            P.op('pool', lambda e: e.tensor_tensor(out=v[:, 5:8, :n], in0=v[:, 5:8, :n],
                                                   in1=rstd[:, :n].unsqueeze(1).to_broadcast([128, 3, n]),
                                                   op=ALU.mult), reads=[v, rstd], writes=[v])
            for k in range(8):
                P.op('act', lambda e, k=k: e.activation(out=v[:, k, :n], in_=v[:, k, :n], func=AF.Identity,
                                                        scale=lng[:, l, which, k:k + 1], bias=lnb[:, l, which, k:k + 1]),
                     reads=[v, lng, lnb], writes=[v])

        def ffn_phase(l, Xsrc, Xdst, win, wout):
            with Phase() as ph:
                xt = ph.sb([128, 8, 512], F32)
                x1a = ph.sb([128, 8, 512], F32)
                hb = ph.sb([128, 8, 512], BF16)
                actb = ph.sb([128, NJ, 512], BF16)
                sg = [ph.sb([128, 512], BF16) for _ in range(2)]
                mean_sb = ph.sb([128, 512], F32)
                rstd = ph.sb([128, 512], F32)

                def prologue(ti):
                    l0, n = own512[ti]
                    j = 1 if l0 == 0 else 0
                    mo = hb
                    P.dma('sp', xt[:, :, :n], fm(Xsrc)[:, :, l0:l0 + n], writes=[xt])
                    P.dma('sp', mo[:, :, :n], mo_view(l0, n), writes=[mo])
                    P.op('act', lambda e: e.activation(out=xt[:, :, :n], in_=xt[:, :, :n], func=AF.Copy, scale=ALPHA),
                         reads=[xt], writes=[xt])
                    for m in range(8):
                        P.op('dve', lambda e, m=m: e.scalar_tensor_tensor(
                            out=xt[:, m, :n], in0=mo[:, m, :n], scalar=modT[:, l, 2 * 8 + m, j:j + 1], in1=xt[:, m, :n],
                            op0=ALU.mult, op1=ALU.add), reads=[mo, modT, xt], writes=[xt])
                    layer_norm2(xt, n, l, 0, sg, mean_sb, rstd)
                    modulate(xt, hb, n, l, 3, 4, j)

                prologue(0)
                for ti, (l0, n) in enumerate(own512):
                    j = 1 if l0 == 0 else 0
                    for jb in range(NJ):
                        pg, pu = PB[(jb % 2) * 2], PB[(jb % 2) * 2 + 1]
                        for k in range(8):
                            P.op('pe', lambda e, k=k, jb=jb, pg=pg: e.matmul(
                                pg[:, :n], lhsT=win[:, k, jb * 128:(jb + 1) * 128], rhs=hb[:, k, :n],
                                start=(k == 0), stop=(k == 7)), reads=[win, hb], writes=[pg])
                        for k in range(8):
                            P.op('pe', lambda e, k=k, jb=jb, pu=pu: e.matmul(
                                pu[:, :n], lhsT=win[:, k, FFH + jb * 128:FFH + (jb + 1) * 128], rhs=hb[:, k, :n],
                                start=(k == 0), stop=(k == 7)), reads=[win, hb], writes=[pu])
                        sgt = sg[jb % 2]
                        P.op('act', lambda e, pg=pg, sgt=sgt: e.activation(out=sgt[:, :n], in_=pg[:, :n], func=AF.Silu),
                             reads=[pg], writes=[sgt])
                        P.op('dve', lambda e, pu=pu, sgt=sgt, jb=jb: e.tensor_tensor(
                            out=actb[:, jb, :n], in0=pu[:, :n], in1=sgt[:, :n], op=ALU.mult),
                            reads=[pu, sgt], writes=[actb])
                    P.op('act', lambda e: e.activation(out=x1a[:, :, :n], in_=xt[:, :, :n], func=AF.Copy, scale=ALPHA),
                         reads=[xt], writes=[x1a])
                    if ti + 1 < len(own512):
                        prologue(ti + 1)
                    for m in range(8):
                        po = PB[4 + m % 2]
                        for jb in range(NJ):
                            P.op('pe', lambda e, m=m, jb=jb, po=po: e.matmul(
                                po[:, :n], lhsT=wout[:, jb, m * 128:(m + 1) * 128], rhs=actb[:, jb, :n],
                                start=(jb == 0), stop=(jb == NJ - 1)), reads=[wout, actb], writes=[po])
                        P.op('dve', lambda e, m=m, po=po: e.scalar_tensor_tensor(
                            out=x1a[:, m, :n], in0=po[:, :n], scalar=modT[:, l, 5 * 8 + m, j:j + 1], in1=x1a[:, m, :n],
                            op0=ALU.mult, op1=ALU.add), reads=[po, modT, x1a], writes=[x1a])
                    layer_norm2(x1a, n, l, 1, sg, mean_sb, rstd)
                    P.dma('act', fm(Xdst)[:, :, l0:l0 + n], x1a[:, :, :n], reads=[x1a])

        def load_h(ht, t, l0, n):
            P.dma('sp', ht[:, :, :n], Hg[:, t].rearrange("k p s -> p k s")[:, :, l0:l0 + n], writes=[ht])

        def even_proj_phase(ee):
            with Phase() as ph:
                w = load_w(ph, ev_w_in[ee], D, EVC)
                hb = [ph.sb([128, 8, 512], BF16) for _ in range(2)]
                st_gq = [ph.sb([128, 512], BF16) for _ in range(2)]
                st_gk = [ph.sb([128, 512], BF16) for _ in range(2)]
                st_g = [ph.sb([128, 2, 512], BF16) for _ in range(2)]
                st_r = [ph.sb([32, 512], BF16) for _ in range(2)]
                st_dq = [ph.sb([128, 2, 512], BF16) for _ in range(2)]
                st_dk = [ph.sb([128, 2, 512], BF16) for _ in range(2)]
                st_t = [ph.sb([128, 4, 640], BF16) for _ in range(2)]
                cs = [ph.sb([128, 512], F32) for _ in range(2)]
                sn = [ph.sb([128, 512], F32) for _ in range(2)]
                qb = [ph.sb([128, 512], BF16) for _ in range(2)]
                t1 = [ph.sb([128, 512], F32) for _ in range(2)]
                t2 = [ph.sb([128, 512], F32) for _ in range(2)]
                blocks = ([('gq', 0, 0, 128), ('gk', 0, 128, 128)] + [('g', i, 512 + i * 128, 128) for i in range(2)]
                          + [('r', 0, 768, 32)] + [('dq', i, 800 + i * 128, 128) for i in range(2)]
                          + [('dk', i, 1056 + i * 128, 128) for i in range(2)])
                bi = 0
                ri = 0
                for ti, (t, l0, n) in enumerate(gtiles):
                    s0 = t * Sh + l0
                    isctx = s0 < CTX
                    b2 = ti % 2
                    ht = hb[b2]
                    load_h(ht, t, l0, n)
                    if not isctx:
                        P.dma('sp', cs[b2][:, :n], cos_d[:, s0 - CTX:s0 - CTX + n], writes=[cs[b2]])
                        P.dma('sp', sn[b2][:, :n], sin_d[:, s0 - CTX:s0 - CTX + n], writes=[sn[b2]])
                    for (kind, idx, c0, M) in blocks:
                        pb = PB[bi % 4]
                        bi += 1
                        for k in range(8):
                            P.op('pe', lambda e, k=k, c0=c0, M=M, pb=pb: e.matmul(
                                pb[0:M, :n], lhsT=w[:, k, c0:c0 + M], rhs=ht[:, k, :n], start=(k == 0), stop=(k == 7)),
                                reads=[w, ht], writes=[pb])
                        if kind == 'gq':
                            dst = st_gq[b2]
                            P.op('act', lambda e, dst=dst, pb=pb: e.activation(
                                out=dst[:, :n], in_=pb[:, :n], func=AF.Copy, scale=0.125), reads=[pb], writes=[dst])
                        elif kind == 'gk':
                            dst = st_gk[b2]
                            P.op('dve', lambda e, dst=dst, pb=pb: e.tensor_copy(out=dst[:, :n], in_=pb[:, :n]),
                                 reads=[pb], writes=[dst])
                        elif kind == 'g':
                            dst = st_g[b2]
                            P.op('act', lambda e, dst=dst, idx=idx, pb=pb: e.activation(
                                out=dst[:, idx, :n], in_=pb[:, :n], func=AF.Silu), reads=[pb], writes=[dst])
                        elif kind == 'r':
                            dst = st_r[b2]
                            P.op('dve', lambda e, dst=dst, pb=pb: e.tensor_copy(out=dst[:, :n], in_=pb[0:32, :n]),
                                 reads=[pb], writes=[dst])
                        else:
                            dst = (st_dq if kind == 'dq' else st_dk)[b2]
                            if isctx:
                                P.op('dve', lambda e, dst=dst, idx=idx, pb=pb: e.tensor_copy(out=dst[:, idx, :n], in_=pb[:, :n]),
                                     reads=[pb], writes=[dst])
                            else:
                                r2 = ri % 2
                                ri += 1
                                pp = PB[4 + r2]
                                P.op('act', lambda e, r2=r2, pb=pb: e.activation(out=qb[r2][:, :n], in_=pb[:, :n], func=AF.Copy),
                                     reads=[pb], writes=[qb[r2]])
                                P.op('pe', lambda e, r2=r2, pp=pp: e.matmul(pp[:, :n], lhsT=cmb[:, C_PERM, :], rhs=qb[r2][:, :n],
                                                                          start=True, stop=True), reads=[cmb, qb[r2]], writes=[pp])
                                P.op('pool', lambda e, r2=r2: e.tensor_tensor(out=t1[r2][:, :n], in0=qb[r2][:, :n], in1=cs[b2][:, :n],
                                                                             op=ALU.mult), reads=[qb[r2], cs[b2]], writes=[t1[r2]])
                                P.op('dve', lambda e, r2=r2, pp=pp: e.tensor_tensor(out=t2[r2][:, :n], in0=pp[:, :n], in1=sn[b2][:, :n],
                                                                                   op=ALU.mult), reads=[pp, sn[b2]], writes=[t2[r2]])
                                P.op('pool', lambda e, r2=r2, dst=dst, idx=idx: e.tensor_tensor(
                                    out=dst[:, idx, :n], in0=t1[r2][:, :n], in1=t2[r2][:, :n], op=ALU.add),
                                    reads=[t1[r2], t2[r2]], writes=[dst])
                    stt = st_t[b2]
                    for sub in range(n // 128):
                        for (c0, N, o0, pbi) in ((128, 128, 0, 6), (256, 256, 128, 7), (1312, 256, 384, 6)):
                            pb = PB[pbi]
                            for k in range(8):
                                P.op('pe', lambda e, k=k, c0=c0, N=N, pb=pb, sub=sub: e.matmul(
                                    pb[:, :N], lhsT=ht[:, k, sub * 128:(sub + 1) * 128], rhs=w[:, k, c0:c0 + N],
                                    start=(k == 0), stop=(k == 7)), reads=[w, ht], writes=[pb])
                            if pbi == 7:
                                P.op('act', lambda e, pb=pb, sub=sub, o0=o0, N=N: e.activation(
                                    out=stt[:, sub, o0:o0 + N], in_=pb[:, :N], func=AF.Copy), reads=[pb], writes=[stt])
                            else:
                                P.op('dve', lambda e, pb=pb, sub=sub, o0=o0, N=N: e.tensor_copy(
                                    out=stt[:, sub, o0:o0 + N], in_=pb[:, :N]), reads=[pb], writes=[stt])
                    P.dma('pool', QG[:, s0:s0 + n], st_gq[b2][:, :n], reads=[st_gq[b2]])
                    P.dma('pool', KG[:, s0:s0 + n], st_gk[b2][:, :n], reads=[st_gk[b2]])
                    P.dma('pool', GG.rearrange("(i p) s -> p i s", p=128)[:, :, s0:s0 + n], st_g[b2][:, :, :n], reads=[st_g[b2]])
                    P.dma('pool', RR[:, s0:s0 + n], st_r[b2][:, :n], reads=[st_r[b2]])
                    P.dma('pool', QD.rearrange("(i p) s -> p i s", p=128)[:, :, s0:s0 + n], st_dq[b2][:, :, :n], reads=[st_dq[b2]])
                    P.dma('pool', KD.rearrange("(i p) s -> p i s", p=128)[:, :, s0:s0 + n], st_dk[b2][:, :, :n], reads=[st_dk[b2]])
                    c0i, nsub = s0 // 128, n // 128
                    P.dma('pool', KGt.rearrange("(c p) f -> p c f", p=128)[:, c0i:c0i + nsub, :], stt[:, :nsub, 0:128], reads=[stt])
                    P.dma('pool', VGt.rearrange("(c p) f -> p c f", p=128)[:, c0i:c0i + nsub, :], stt[:, :nsub, 128:384], reads=[stt])
                    P.dma('pool', VDt.rearrange("(c p) f -> p c f", p=128)[:, c0i:c0i + nsub, :], stt[:, :nsub, 384:640], reads=[stt])

        def gla_phase(ee, d):
            order = list(range(NCH)) if d == 0 else [1, 0] + list(range(NCH - 1, 1, -1))
            TRI, STRI, MASK = (C_TRIF, C_STRIF, C_MASKF) if d == 0 else (C_TRIB, C_STRIB, C_MASKB)
            li = 127 if d == 0 else 0
            H = 2
            with Phase() as ph:
                cmf = load_cmf(ph)
                wg = ph.sb([17, 128], BF16)
                P.dma('pool', wg[:], gla_wg[ee, :, d, :], writes=[wg])
                mask4 = ph.sb([128, H, 128], F32)
                P.op('dve', lambda e: e.tensor_copy(out=mask4[:], in_=cmf[:, MASK:MASK + 1, :].to_broadcast([128, H, 128])),
                     reads=[cmf], writes=[mask4])
                Sf = ph.sb([64, H, 128], F32)
                Sb = ph.sb([64, H, 128], BF16)
                P.op('dve', lambda e: e.memset(Sf[:], 0.0), writes=[Sf])
                P.op('dve', lambda e: e.memset(Sb[:], 0.0), writes=[Sb])
                NB = 3
                rt = [ph.sb([17, 128], BF16) for _ in range(NB)]
                for tt in rt:
                    P.op('dve', lambda e, tt=tt: e.memset(tt[:], 1.0), writes=[tt])
                qT = [ph.sb([64, H, 128], BF16) for _ in range(NB)]
                kT = [ph.sb([64, H, 128], BF16) for _ in range(NB)]
                ktok = [ph.sb([128, 128], BF16) for _ in range(NB)]
                vtok = [ph.sb([128, 256], BF16) for _ in range(NB)]
                ofw = [ph.sb([128, H, 128], F32) for _ in range(NB)]
                ez = [ph.sb([128, 128], F32) for _ in range(NB)]
                spb = [ph.sb([128, 128], BF16) for _ in range(NB)]
                Ep = [ph.sb([64, H, 128], F32) for _ in range(NB)]
                Em = [ph.sb([64, H, 128], F32) for _ in range(NB)]
                Ee = [ph.sb([128, 128], F32) for _ in range(NB)]
                qd = [ph.sb([64, H, 128], BF16) for _ in range(NB)]
                kd = [ph.sb([64, H, 128], BF16) for _ in range(NB)]
                kl = [ph.sb([128, 128], BF16) for _ in range(NB)]
                attm = [ph.sb([128, H, 128], BF16) for _ in range(NB)]
                ot = [ph.sb([128, H, 128], F32) for _ in range(NB)]
                OGv = OG.rearrange("(h p) s -> p h s", p=128)
                def stage_a(ci):
                    c = order[ci]
                    b = ci % NB
                    p0 = (ci % 2) * 4
                    tok = slice(c * 128, (c + 1) * 128)
                    P.dma('sp', rt[b][0:16, :], RR[16 * d:16 * d + 16, tok], writes=[rt[b]])
                    P.dma('sp', qT[b][:], QG.rearrange("(h p) s -> p h s", p=64)[:, :, tok], writes=[qT[b]])
                    P.dma('sp', kT[b][:], KG.rearrange("(h p) s -> p h s", p=64)[:, :, tok], writes=[kT[b]])
                    P.dma('sp', ktok[b][:], KGt[tok, :], writes=[ktok[b]])
                    P.dma('sp', vtok[b][:], VGt[tok, :], writes=[vtok[b]])
                    if d == 1:
                        P.dma('sp', ofw[b][:], OGv[:, :, tok], writes=[ofw[b]])
                    pz, pbt, patt, po = PB[p0], PB[p0 + 1], PB[p0 + 2], PB[p0 + 3]
                    P.op('pe', lambda e, b=b: e.matmul(pz[:, 0:128], lhsT=rt[b][:], rhs=wg[:], start=True, stop=True),
                         reads=[rt[b], wg], writes=[pz])
                    P.op('act', lambda e, b=b: e.activation(out=ez[b][:], in_=pz[:, 0:128], func=AF.Exp, scale=-1.0),
                         reads=[pz], writes=[ez[b]])
                    P.op('act', lambda e, b=b: e.activation(out=spb[b][:], in_=ez[b][:], func=AF.Ln, bias=1.0, scale=1.0),
                         reads=[ez[b]], writes=[spb[b]])
                    for h in range(H):
                        P.op('pe', lambda e, b=b, h=h: e.matmul(pbt[0:64, h * 128:(h + 1) * 128], lhsT=spb[b][:, h * 64:(h + 1) * 64],
                                                               rhs=cmb[:, TRI, :], start=True, stop=True),
                             reads=[spb[b], cmb], writes=[pbt])
                    P.op('pe', lambda e, b=b: e.matmul(pz[:, 128:256], lhsT=cmb[:, STRI, :], rhs=spb[b][:], start=True, stop=True),
                         reads=[spb[b], cmb], writes=[pz])
                    pb1v = pbt[0:64, 0:H * 128].rearrange("p (h t) -> p h t", h=H)
                    P.op('act', lambda e, b=b: e.activation(out=Ep[b][:], in_=pb1v, func=AF.Exp), reads=[pbt], writes=[Ep[b]])
                    P.op('act', lambda e, b=b: e.activation(out=Em[b][:], in_=pb1v, func=AF.Exp, scale=-1.0),
                         reads=[pbt], writes=[Em[b]])
                    P.op('act', lambda e, b=b: e.activation(out=Ee[b][:], in_=pz[:, 128:256], func=AF.Exp),
                         reads=[pz], writes=[Ee[b]])
                    P.op('dve', lambda e, b=b: e.tensor_tensor(out=qd[b][:], in0=Ep[b][:], in1=qT[b][:], op=ALU.mult),
                         reads=[Ep[b], qT[b]], writes=[qd[b]])
                    P.op('dve', lambda e, b=b: e.tensor_tensor(out=kd[b][:], in0=Em[b][:], in1=kT[b][:], op=ALU.mult),
                         reads=[Em[b], kT[b]], writes=[kd[b]])
                    P.op('pool', lambda e, b=b: e.tensor_tensor(out=kl[b][:], in0=Ee[b][:], in1=ktok[b][:], op=ALU.mult),
                         reads=[Ee[b], ktok[b]], writes=[kl[b]])
                    for h in range(H):
                        P.op('pe', lambda e, b=b, h=h: e.matmul(patt[:, h * 128:(h + 1) * 128], lhsT=kd[b][:, h, :], rhs=qd[b][:, h, :],
                                                               start=True, stop=True), reads=[kd[b], qd[b]], writes=[patt])
                    P.op('dve', lambda e, b=b: e.tensor_tensor(out=attm[b][:], in0=patt[:, 0:H * 128].rearrange("p (h t) -> p h t", h=H),
                                                              in1=mask4[:], op=ALU.mult), reads=[patt, mask4], writes=[attm[b]])

                def stage_b(ci):
                    c = order[ci]
                    b = ci % NB
                    p0 = (ci % 2) * 4
                    tok = slice(c * 128, (c + 1) * 128)
                    pz, po = PB[p0], PB[p0 + 3]
                    for h in range(H):
                        P.op('pe', lambda e, b=b, h=h: e.matmul(po[:, h * 128:(h + 1) * 128], lhsT=vtok[b][:, h * 128:(h + 1) * 128],
                                                               rhs=attm[b][:, h, :], start=True, stop=False),
                             reads=[vtok[b], attm[b]], writes=[po])
                        P.op('pe', lambda e, b=b, h=h: e.matmul(po[:, h * 128:(h + 1) * 128], lhsT=Sb[:, h, :],
                                                               rhs=qd[b][:, h, :], start=False, stop=True),
                             reads=[Sb, qd[b]], writes=[po])
                    pb4v = po[:, 0:H * 128].rearrange("p (h t) -> p h t", h=H)
                    if d == 0:
                        P.op('act', lambda e, b=b: e.activation(out=ot[b][:], in_=pb4v, func=AF.Copy), reads=[po], writes=[ot[b]])
                    else:
                        P.op('pool' if False else 'dve', lambda e, b=b: e.tensor_tensor(out=ot[b][:], in0=pb4v, in1=ofw[b][:], op=ALU.add),
                             reads=[po, ofw[b]], writes=[ot[b]])
                    P.dma('act', OGv[:, :, tok], ot[b][:], reads=[ot[b]])
                    for h in range(H):
                        P.op('pe', lambda e, b=b, h=h: e.matmul(pz[0:64, 256 + h * 128:256 + (h + 1) * 128], lhsT=kl[b][:, h * 64:(h + 1) * 64],
                                                               rhs=vtok[b][:, h * 128:(h + 1) * 128], start=True, stop=True),
                             reads=[kl[b], vtok[b]], writes=[pz])
                    P.op('dve', lambda e, b=b: e.tensor_tensor(out=Sf[:], in0=Sf[:],
                                                              in1=Ep[b][:, :, li:li + 1].to_broadcast([64, H, 128]), op=ALU.mult),
                         reads=[Sf, Ep[b]], writes=[Sf])
                    P.op('dve', lambda e: e.tensor_tensor(out=Sf[:], in0=Sf[:], in1=pz[0:64, 256:512].rearrange("p (h t) -> p h t", h=H),
                                                          op=ALU.add), reads=[Sf, pz], writes=[Sf])
                    P.op('act', lambda e: e.activation(out=Sb[:], in_=Sf[:], func=AF.Copy), reads=[Sf], writes=[Sb])

                stage_a(0)
                for ci in range(len(order)):
                    if ci + 1 < len(order):
                        stage_a(ci + 1)
                    stage_b(ci)

        def attn_phase(ee):
            with Phase() as ph:
                cmf = load_cmf(ph)
                kTh = [ph.sb([128, S], BF16) for _ in range(2)]
                vh = [ph.sb([128, NCH, 128], BF16) for _ in range(2)]
                qTt = [ph.sb([128, 512], BF16) for _ in range(2)]
                pt = [ph.sb([128, 2, 512], BF16) for _ in range(3)]
                acc = [ph.sb([128, 512], F32) for _ in range(2)]
                r1 = ph.sb([128, 512], F32)
                r2 = ph.sb([128, 512], F32)
                o1 = ph.sb([128, 512], F32)
                oa = ph.sb([128, 512], F32)
                ob_ = ph.sb([128, 512], F32)
                osq = ph.sb([128, 512], BF16)
                dto = [ph.sb([128, 512], BF16) for _ in range(2)]
                SB = [(0, 1), (2, 3)]
                qi = 0
                for h in range(2):
                    P.dma('sp', kTh[h][:], KD[h * 128:(h + 1) * 128, :], writes=[kTh[h]])
                    P.dma('act', vh[h][:], VDt.rearrange("(c p) f -> p c f", p=128)[:, :, h * 128:(h + 1) * 128], writes=[vh[h]])
                for h in range(2):
                    kT_, v_ = kTh[h], vh[h]
                    for (t, l0, n) in gtiles:
                        s0 = t * Sh + l0
                        q2 = qi % 2
                        qi += 1
                        qt = qTt[q2]
                        P.dma('sp', qt[:, :n], QD[h * 128:(h + 1) * 128, s0:s0 + n], writes=[qt])
                        nkb = 2 if s0 < CTX else NCH

                        def smm(kb):
                            for m in range(2):
                                pb = PB[SB[kb % 2][m]]
                                P.op('pe', lambda e, m=m, pb=pb, kb=kb: e.matmul(
                                    pb[:, :n], lhsT=kT_[m * 64:(m + 1) * 64, kb * 128:(kb + 1) * 128],
                                    rhs=qt[m * 64:(m + 1) * 64, :n], start=True, stop=True), reads=[kT_, qt], writes=[pb])
                        smm(0)
                        if nkb > 1:
                            smm(1)
                        for kb in range(nkb):
                            p3 = kb % 3
                            b0 = SB[kb % 2][0]
                            P.op('act', lambda e, b0=b0, p3=p3: e.activation(
                                out=pt[p3][:, :, :n], in_=psum[:, b0:b0 + 2, :n], func=AF.Exp, scale=0.125),
                                reads=[PB[b0], PB[b0 + 1]], writes=[pt[p3]])
                            if kb + 2 < nkb:
                                smm(kb + 2)
                            for m in range(2):
                                P.op('pe', lambda e, m=m, p3=p3, kb=kb: e.matmul(
                                    PB[4 + m][:, :n], lhsT=v_[:, kb, :], rhs=pt[p3][:, m, :n],
                                    start=(kb == 0), stop=(kb == nkb - 1)), reads=[v_, pt[p3]], writes=[PB[4 + m]])
                            P.op('pe', lambda e, p3=p3, kb=kb: e.matmul(
                                PB[7][:, :n], lhsT=cmb[:, C_ONES, :], rhs=pt[p3][:, 0, :n],
                                start=(kb == 0), stop=(kb == nkb - 1)), reads=[cmb, pt[p3]], writes=[PB[7]])
                            hn = n // 2
                            for ai, (eng, c0, c1) in enumerate((('dve', 0, hn), ('pool', hn, n))):
                                if kb == 0:
                                    P.op(eng, lambda e, p3=p3, c0=c0, c1=c1, ai=ai: e.tensor_copy(out=acc[ai][:, c0:c1], in_=pt[p3][:, 1, c0:c1]),
                                         reads=[pt[p3]], writes=[acc[ai]])
                                else:
                                    P.op(eng, lambda e, p3=p3, c0=c0, c1=c1, ai=ai: e.tensor_tensor(
                                        out=acc[ai][:, c0:c1], in0=acc[ai][:, c0:c1], in1=pt[p3][:, 1, c0:c1], op=ALU.add),
                                        reads=[pt[p3], acc[ai]], writes=[acc[ai]])
                        for ai, (c0, c1) in enumerate(((0, n // 2), (n // 2, n))):
                            P.op('pe', lambda e, ai=ai, c0=c0, c1=c1: e.matmul(PB[6][:, c0:c1], lhsT=cmf[:, C_ONES, :], rhs=acc[ai][:, c0:c1],
                                                                              start=True, stop=True), reads=[cmf, acc[ai]], writes=[PB[6]])
                        P.op('act', lambda e: e.activation(out=r1[:, :n], in_=PB[7][:, :n], func=AF.Copy), reads=[PB[7]], writes=[r1])
                        P.op('dve', lambda e: e.tensor_copy(out=oa[:, :n], in_=PB[4][:, :n]), reads=[PB[4]], writes=[oa])
                        P.op('act', lambda e: e.activation(out=ob_[:, :n], in_=PB[5][:, :n], func=AF.Copy), reads=[PB[5]], writes=[ob_])
                        P.op('dve', lambda e: e.reciprocal(out=r1[:, :n], in_=r1[:, :n]), reads=[r1], writes=[r1])
                        P.op('dve', lambda e: e.reciprocal(out=r2[:, :n], in_=PB[6][:, :n]), reads=[PB[6]], writes=[r2])
                        P.op('dve', lambda e: e.tensor_tensor(out=r1[:, :n], in0=oa[:, :n], in1=r1[:, :n], op=ALU.mult),
                             reads=[oa, r1], writes=[r1])
                        P.op('pool', lambda e: e.tensor_tensor(out=r2[:, :n], in0=ob_[:, :n], in1=r2[:, :n], op=ALU.mult),
                             reads=[ob_, r2], writes=[r2])
                        P.op('dve', lambda e: e.scalar_tensor_tensor(out=o1[:, :n], in0=r2[:, :n], scalar=nlam[:, ee:ee + 1],
                                                                     in1=r1[:, :n], op0=ALU.mult, op1=ALU.add),
                             reads=[r1, r2, nlam], writes=[o1])
                        P.op('pool', lambda e: e.tensor_tensor(out=osq[:, :n], in0=o1[:, :n], in1=o1[:, :n], op=ALU.mult),
                             reads=[o1], writes=[osq])
                        P.op('pe', lambda e: e.matmul(PB[6][:, :n], lhsT=cmb[:, C_ONES, :], rhs=osq[:, :n], start=True, stop=True),
                             reads=[cmb, osq], writes=[PB[6]])
                        P.op('act', lambda e: e.activation(out=r1[:, :n], in_=PB[6][:, :n], func=AF.Sqrt, bias=RMS_EPS, scale=1.0 / 128),
                             reads=[PB[6]], writes=[r1])
                        P.op('dve', lambda e: e.reciprocal(out=r1[:, :n], in_=r1[:, :n]), reads=[r1], writes=[r1])
                        dd = dto[q2]
                        P.op('dve', lambda e, dd=dd: e.scalar_tensor_tensor(out=dd[:, :n], in0=o1[:, :n], scalar=dnws[:, ee:ee + 1],
                                                                            in1=r1[:, :n], op0=ALU.mult, op1=ALU.mult),
                             reads=[o1, r1, dnws], writes=[dd])
                        P.dma('sp', DT[h * 128:(h + 1) * 128, s0:s0 + n], dd[:, :n], reads=[dd])

        def even_out_phase(ee):
            with Phase() as ph:
                wo = load_w(ph, ev_w_out[ee], 512, D)
                og = [ph.sb([128, 2, 512], F32) for _ in range(2)]
                gg = [ph.sb([128, 2, 512], BF16) for _ in range(2)]
                mix = [ph.sb([128, 4, 512], BF16) for _ in range(2)]
                sq = ph.sb([128, 2, 512], BF16)
                rs = ph.sb([128, 2, 512], F32)
                ob = [ph.sb([128, 8, 512], BF16) for _ in range(2)]
                for ti, (c, t, l0, n) in enumerate(ctiles):
                    s0 = t * Sh + l0
                    b2 = ti % 2
                    P.dma('sp', og[b2][:, :, :n], OG.rearrange("(h p) s -> p h s", p=128)[:, :, s0:s0 + n], writes=[og[b2]])
                    P.dma('sp', gg[b2][:, :, :n], GG.rearrange("(h p) s -> p h s", p=128)[:, :, s0:s0 + n], writes=[gg[b2]])
                    P.dma('sp', mix[b2][:, 2:4, :n], DT.rearrange("(h p) s -> p h s", p=128)[:, :, s0:s0 + n], writes=[mix[b2]])
                    P.op('dve', lambda e, b2=b2: e.tensor_tensor(out=sq[:, :, :n], in0=og[b2][:, :, :n], in1=og[b2][:, :, :n], op=ALU.mult),
                         reads=[og[b2]], writes=[sq])
                    for h in range(2):
                        P.op('pe', lambda e, h=h: e.matmul(PB[h][:, :n], lhsT=cmb[:, C_ONES, :], rhs=sq[:, h, :n], start=True, stop=True),
                             reads=[cmb, sq], writes=[PB[h]])
                    P.op('act', lambda e: e.activation(out=rs[:, :, :n], in_=psum[:, 0:2, :n], func=AF.Sqrt, bias=RMS_EPS, scale=1.0 / 128),
                         reads=[PB[0], PB[1]], writes=[rs])
                    P.op('dve', lambda e: e.reciprocal(out=rs[:, :, :n], in_=rs[:, :, :n]), reads=[rs], writes=[rs])
                    P.op('dve', lambda e, b2=b2: e.tensor_tensor(out=rs[:, :, :n], in0=rs[:, :, :n], in1=og[b2][:, :, :n], op=ALU.mult),
                         reads=[rs, og[b2]], writes=[rs])
                    P.op('dve', lambda e, b2=b2: e.scalar_tensor_tensor(out=mix[b2][:, 0:2, :n], in0=rs[:, :, :n], scalar=gnw[:, ee:ee + 1],
                                                                       in1=gg[b2][:, :, :n], op0=ALU.mult, op1=ALU.mult),
                         reads=[rs, gnw, gg[b2]], writes=[mix[b2]])
                    o_ = ob[b2]
                    for m in range(8):
                        po = PB[4 + m % 4]
                        for k in range(4):
                            P.op('pe', lambda e, m=m, k=k, po=po, b2=b2: e.matmul(
                                po[:, :n], lhsT=wo[:, k, m * 128:(m + 1) * 128], rhs=mix[b2][:, k, :n],
                                start=(k == 0), stop=(k == 3)), reads=[wo, mix[b2]], writes=[po])
                        if m % 2 == 0:
                            P.op('act', lambda e, m=m, po=po: e.activation(out=o_[:, m, :n], in_=po[:, :n], func=AF.Copy),
                                 reads=[po], writes=[o_])
                        else:
                            P.op('dve', lambda e, m=m, po=po: e.tensor_copy(out=o_[:, m, :n], in_=po[:, :n]), reads=[po], writes=[o_])
                    P.dma('act', Ppc[c][t].rearrange("k p s -> p k s"), o_[:, :, :n], reads=[o_], writes=[ppres[c]])
                    if t == 1:
                        rs_chunk(c)

        def odd_proj_phase(oo):
            with Phase() as ph:
                w = load_w(ph, ssd_w_in[oo], D, SDC)
                hb = [ph.sb([128, 8, 512], BF16) for _ in range(2)]
                st_x = [ph.sb([128, 12, 512], BF16) for _ in range(2)]
                st_z = [ph.sb([128, 4, 1024], BF16) for _ in range(2)]
                st_dt = [ph.sb([128, 4, 32], F32) for _ in range(2)]
                dtb = ph.sb([128, 32], F32)
                P.dma('sp', dtb[:], ssd_dtb[oo], writes=[dtb])
                bi = 0
                for ti, (t, l0, n) in enumerate(gtiles):
                    s0 = t * Sh + l0
                    b2 = ti % 2
                    ht = hb[b2]
                    stx, stz, stdt = st_x[b2], st_z[b2], st_dt[b2]
                    load_h(ht, t, l0, n)
                    for i in range(12):
                        pb = PB[bi % 4]
                        bi += 1
                        for k in range(8):
                            P.op('pe', lambda e, k=k, i=i, pb=pb: e.matmul(
                                pb[:, :n], lhsT=w[:, k, 1024 + i * 128:1024 + (i + 1) * 128], rhs=ht[:, k, :n],
                                start=(k == 0), stop=(k == 7)), reads=[w, ht], writes=[pb])
                        if i % 2 == 0:
                            P.op('act', lambda e, i=i, pb=pb: e.activation(out=stx[:, i, :n], in_=pb[:, :n], func=AF.Copy),
                                 reads=[pb], writes=[stx])
                        else:
                            P.op('dve', lambda e, i=i, pb=pb: e.tensor_copy(out=stx[:, i, :n], in_=pb[:, :n]),
                                 reads=[pb], writes=[stx])
                    nsub = n // 128
                    for sub in range(nsub):
                        for zb in range(2):
                            pb = PB[4 + zb % 2]
                            for k in range(8):
                                P.op('pe', lambda e, k=k, zb=zb, pb=pb, sub=sub: e.matmul(
                                    pb[:, :512], lhsT=ht[:, k, sub * 128:(sub + 1) * 128], rhs=w[:, k, zb * 512:(zb + 1) * 512],
                                    start=(k == 0), stop=(k == 7)), reads=[w, ht], writes=[pb])
                            P.op('act', lambda e, zb=zb, pb=pb, sub=sub: e.activation(
                                out=stz[:, sub, zb * 512:(zb + 1) * 512], in_=pb[:, :512], func=AF.Silu), reads=[pb], writes=[stz])
                        for k in range(8):
                            P.op('pe', lambda e, k=k, sub=sub: e.matmul(
                                PB[6][:, 0:32], lhsT=ht[:, k, sub * 128:(sub + 1) * 128], rhs=w[:, k, 2560:2592],
                                start=(k == 0), stop=(k == 7)), reads=[w, ht], writes=[PB[6]])
                        P.op('dve', lambda e, sub=sub: e.tensor_tensor(out=stdt[:, sub, :], in0=PB[6][:, 0:32], in1=dtb[:], op=ALU.add),
                             reads=[PB[6], dtb], writes=[stdt])
                    P.op('act', lambda e: e.activation(out=stdt[:, :nsub, :], in_=stdt[:, :nsub, :], func=AF.Exp),
                         reads=[stdt], writes=[stdt])
                    P.op('act', lambda e: e.activation(out=stdt[:, :nsub, :], in_=stdt[:, :nsub, :], func=AF.Ln, bias=1.0, scale=1.0),
                         reads=[stdt], writes=[stdt])
                    c0i = s0 // 128
                    P.dma('pool', XBC.rearrange("(i p) s -> p i s", p=128)[:, :, s0:s0 + n], stx[:, :, :n], reads=[stx])
                    P.dma('pool', ZS.rearrange("(c p) f -> p c f", p=128)[:, c0i:c0i + nsub, :], stz[:, :nsub, :], reads=[stz])
                    P.dma('pool', DTt.rearrange("(c p) f -> p c f", p=128)[:, c0i:c0i + nsub, :], stdt[:, :nsub, :], reads=[stdt])

        def conv_phase(oo):
            NBK = 12
            with Phase() as ph:
                cmf = load_cmf(ph)
                cw = ph.sb([128, 5, NBK], F32)
                cb = ph.sb([128, NBK], F32)
                P.dma('sp', cw[:], ssd_cw[oo], writes=[cw])
                P.dma('sp', cb[:], ssd_cb[oo], writes=[cb])
                Dg = ph.sb([128, NBK, 5, 128], BF16)
                for i in range(NBK):
                    for jj in range(5):
                        P.op('dve' if (i + jj) % 2 == 0 else 'pool', lambda e, i=i, jj=jj: e.tensor_scalar_mul(
                            out=Dg[:, i, jj, :], in0=cmf[:, C_IDENT, :], scalar1=cw[:, jj, i:i + 1]), reads=[cmf, cw], writes=[Dg])
                xp = [ph.sb([128, NBK, 516], BF16) for _ in range(2)]
                xc = [ph.sb([128, NBK, 512], BF16) for _ in range(2)]
                stt = [ph.sb([128, 4, 1280], BF16) for _ in range(2)]
                bi = 0
                for ti, (t, l0, n) in enumerate(gtiles):
                    s0 = t * Sh + l0
                    b2 = ti % 2
                    x_p, x_c = xp[b2], xc[b2]
                    lo = 0 if (s0 == 0 or s0 == CTX) else 2
                    hi = 0 if (s0 + n == CTX or s0 + n == S) else 2
                    if lo == 0:
                        P.op('pool', lambda e, x_p=x_p: e.memset(x_p[:, :, 0:2], 0.0), writes=[x_p])
                    if hi == 0:
                        P.op('pool', lambda e, x_p=x_p: e.memset(x_p[:, :, n + 2:n + 4], 0.0), writes=[x_p])
                    P.dma('sp', x_p[:, :, 2 - lo:n + 2 + hi], XBC.rearrange("(i p) s -> p i s", p=128)[:, :, s0 - lo:s0 + n + hi],
                          writes=[x_p])
                    for i in range(NBK):
                        pb = PB[bi % 4]
                        bi += 1
                        for jj in range(5):
                            P.op('pe', lambda e, i=i, jj=jj, pb=pb: e.matmul(
                                pb[:, :n], lhsT=Dg[:, i, jj, :], rhs=x_p[:, i, jj:jj + n], start=(jj == 0), stop=(jj == 4)),
                                reads=[Dg, x_p], writes=[pb])
                        P.op('act', lambda e, i=i, pb=pb: e.activation(out=x_c[:, i, :n], in_=pb[:, :n], func=AF.Silu,
                                                                      bias=cb[:, i:i + 1], scale=1.0), reads=[pb, cb], writes=[x_c])
                    P.dma('pool', BTd.rearrange("(i p) s -> p i s", p=128)[:, :, s0:s0 + n], x_c[:, 8:10, :n], reads=[x_c])
                    P.dma('pool', CTd.rearrange("(i p) s -> p i s", p=128)[:, :, s0:s0 + n], x_c[:, 10:12, :n], reads=[x_c])
                    st = stt[b2]
                    nsub = n // 128
                    qq = 0
                    for sub in range(nsub):
                        for (i0, ni) in ((0, 4), (4, 4), (8, 2)):
                            pb = PB[4 + qq % 4]
                            qq += 1
                            for u in range(ni):
                                i = i0 + u
                                P.op('pe', lambda e, i=i, u=u, pb=pb, sub=sub: e.matmul(
                                    pb[:, u * 128:(u + 1) * 128], lhsT=x_c[:, i, sub * 128:(sub + 1) * 128], rhs=cmb[:, C_IDENT, :],
                                    start=True, stop=True), reads=[x_c, cmb], writes=[pb])
                            if qq % 2 == 0:
                                P.op('dve', lambda e, i0=i0, ni=ni, pb=pb, sub=sub: e.tensor_copy(
                                    out=st[:, sub, i0 * 128:(i0 + ni) * 128], in_=pb[:, 0:ni * 128]), reads=[pb], writes=[st])
                            else:
                                P.op('act', lambda e, i0=i0, ni=ni, pb=pb, sub=sub: e.activation(
                                    out=st[:, sub, i0 * 128:(i0 + ni) * 128], in_=pb[:, 0:ni * 128], func=AF.Copy), reads=[pb], writes=[st])
                    c0i = s0 // 128
                    P.dma('pool', XSt.rearrange("(c p) f -> p c f", p=128)[:, c0i:c0i + nsub, :], st[:, :nsub, 0:1024], reads=[st])
                    P.dma('pool', Btd.rearrange("(c p) f -> p c f", p=128)[:, c0i:c0i + nsub, :], st[:, :nsub, 1024:1280], reads=[st])

        def ssd_phase(oo, d):
            order = list(range(NCH)) if d == 0 else [1, 0] + list(range(NCH - 1, 1, -1))
            U, SL, MASK = (C_UF, C_SLF, C_MASKF) if d == 0 else (C_UB, C_SLB, C_MASKB)
            NH, NG = 16, 2
            with Phase() as ph:
                cmf = load_cmf(ph)
                abc = ph.sb([128, 32], F32)
                P.dma('sp', abc[:], ssd_alog[oo], writes=[abc])
                P.op('act', lambda e: e.activation(out=abc[:], in_=abc[:], func=AF.Exp), reads=[abc], writes=[abc])
                P.op('dve', lambda e: e.tensor_scalar_mul(out=abc[:], in0=abc[:], scalar1=-1.0), reads=[abc], writes=[abc])
                dsk = ph.sb([128, NH], F32)
                nwb = ph.sb([128, 1024], F32)
                if d == 1:
                    P.dma('sp', dsk[:], ssd_dsk[oo], writes=[dsk])
                    P.dma('sp', nwb[:], ssd_nw[oo], writes=[nwb])
                mask4 = ph.sb([128, NG, 128], F32)
                P.op('dve', lambda e: e.tensor_copy(out=mask4[:], in_=cmf[:, MASK:MASK + 1, :].to_broadcast([128, NG, 128])),
                     reads=[cmf], writes=[mask4])
                Sf = [ph.sb([128, 8, 64], F32) for _ in range(NG)]
                Sb = [ph.sb([128, 8, 64], BF16) for _ in range(NG)]
                for g in range(NG):
                    P.op('dve', lambda e, g=g: e.memset(Sf[g][:], 0.0), writes=[Sf[g]])
                    P.op('pool', lambda e, g=g: e.memset(Sb[g][:], 0.0), writes=[Sb[g]])
                NB = 3
                xtok = [ph.sb([128, NH, 64], BF16) for _ in range(NB)]
                btok = [ph.sb([128, 256], BF16) for _ in range(NB)]
                bT = [ph.sb([128, NG, 128], BF16) for _ in range(NB)]
                cT = [ph.sb([128, NG, 128], BF16) for _ in range(NB)]
                dtt = [ph.sb([128, 32], F32) for _ in range(NB)]
                dA = [ph.sb([128, NH], F32) for _ in range(NB)]
                ex = [ph.sb([128, 3 * NH], F32) for _ in range(NB)]
                wv = [ph.sb([128, NH], F32) for _ in range(NB)]
                xdt = [ph.sb([128, NH, 64], BF16) for _ in range(NB)]
                xw = [ph.sb([128, NH, 64], BF16) for _ in range(NB)]
                rhsD = [ph.sb([128, NH, 128], BF16) for _ in range(NB)]
                cbm = [ph.sb([128, NG, 128], F32) for _ in range(NB)]
                expD = [ph.sb([128, 8, 128], F32) for _ in range(2)]
                G = [ph.sb([128, 8, 128], BF16) for _ in range(2)]
                tmp = [ph.sb([128, 8, 64], F32) for _ in range(2)]
                y = [ph.sb([128, NH, 64], F32) for _ in range(NB)]
                if d == 1:
                    yfw = [ph.sb([128, NH, 64], F32) for _ in range(NB)]
                    zs = [ph.sb([128, 1024], BF16) for _ in range(NB)]
                    t3 = ph.sb([128, NH, 64], F32)
                    sq = ph.sb([128, NG, 512], F32)
                    ss = ph.sb([128, NG], F32)
                    yn = ph.sb([128, 1024], BF16)
                    ynT = [ph.sb([128, 8, 128], BF16) for _ in range(2)]
                gi = [0]

                def stage_a(ci):
                    c = order[ci]
                    b = ci % NB
                    tok = slice(c * 128, (c + 1) * 128)
                    P.dma('sp', xtok[b][:], XSt[tok, :].rearrange("t (h p) -> t h p", p=64), writes=[xtok[b]])
                    P.dma('sp', btok[b][:], Btd[tok, :], writes=[btok[b]])
                    P.dma('sp', bT[b][:], BTd.rearrange("(g p) s -> p g s", p=128)[:, :, tok], writes=[bT[b]])
                    P.dma('sp', cT[b][:], CTd.rearrange("(g p) s -> p g s", p=128)[:, :, tok], writes=[cT[b]])
                    P.dma('sp', dtt[b][:], DTt[tok, :], writes=[dtt[b]])
                    if d == 1:
                        P.dma('sp', yfw[b][:], YF[tok, :].rearrange("t (h p) -> t h p", p=64), writes=[yfw[b]])
                        P.dma('sp', zs[b][:], ZS[tok, :], writes=[zs[b]])
                    dtd = dtt[b][:, NH * d:NH * d + NH]
                    P.op('dve', lambda e, b=b: e.tensor_tensor(out=dA[b][:], in0=dtd, in1=abc[:, NH * d:NH * d + NH], op=ALU.mult),
                         reads=[dtt[b], abc], writes=[dA[b]])
                    P.op('pe', lambda e, b=b: e.matmul(PB[0][:, 0:NH], lhsT=cmf[:, U, :], rhs=dA[b][:], start=True, stop=True),
                         reads=[cmf, dA[b]], writes=[PB[0]])
                    P.op('pe', lambda e, b=b: e.matmul(PB[0][:, NH:2 * NH], lhsT=cmf[:, SL, :], rhs=dA[b][:], start=True, stop=True),
                         reads=[cmf, dA[b]], writes=[PB[0]])
                    P.op('pe', lambda e, b=b: e.matmul(PB[0][:, 2 * NH:3 * NH], lhsT=cmf[:, C_ONES, :], rhs=dA[b][:], start=True, stop=True),
                         reads=[cmf, dA[b]], writes=[PB[0]])
                    P.op('act', lambda e, b=b: e.activation(out=ex[b][:], in_=PB[0][:, 0:3 * NH], func=AF.Exp), reads=[PB[0]], writes=[ex[b]])
                    P.op('dve', lambda e, b=b: e.tensor_tensor(out=wv[b][:], in0=dtd, in1=ex[b][:, NH:2 * NH], op=ALU.mult),
                         reads=[dtt[b], ex[b]], writes=[wv[b]])
                    P.op('pool', lambda e, b=b: e.tensor_tensor(out=xdt[b][:], in0=xtok[b][:], in1=dtd.unsqueeze(2).to_broadcast([128, NH, 64]),
                                                               op=ALU.mult), reads=[xtok[b], dtt[b]], writes=[xdt[b]])
                    P.op('dve', lambda e, b=b: e.tensor_tensor(out=xw[b][:], in0=xtok[b][:], in1=wv[b][:].unsqueeze(2).to_broadcast([128, NH, 64]),
                                                              op=ALU.mult), reads=[xtok[b], wv[b]], writes=[xw[b]])
                    for hh, eng in ((0, 'dve'), (8, 'pool')):
                        P.op(eng, lambda e, b=b, hh=hh: e.tensor_tensor(
                            out=rhsD[b][:, hh:hh + 8, :], in0=cmf[:, U:U + 1, :].to_broadcast([128, 8, 128]),
                            in1=dA[b][:, hh:hh + 8].unsqueeze(2).to_broadcast([128, 8, 128]), op=ALU.mult),
                            reads=[cmf, dA[b]], writes=[rhsD[b]])
                    for g in range(NG):
                        P.op('pe', lambda e, b=b, g=g: e.matmul(PB[1][:, g * 128:(g + 1) * 128], lhsT=bT[b][:, g, :], rhs=cT[b][:, g, :],
                                                               start=True, stop=True), reads=[bT[b], cT[b]], writes=[PB[1]])
                    P.op('dve', lambda e, b=b: e.tensor_tensor(out=cbm[b][:], in0=PB[1][:, 0:NG * 128].rearrange("p (g t) -> p g t", g=NG),
                                                              in1=mask4[:], op=ALU.mult), reads=[PB[1], mask4], writes=[cbm[b]])

                def stage_b(ci):
                    c = order[ci]
                    b = ci % NB
                    tok = slice(c * 128, (c + 1) * 128)
                    g2s = []
                    for g in range(NG):
                        g2 = gi[0] % 2
                        gi[0] += 1
                        g2s.append(g2)
                        d0 = 2 if g2 == 0 else 6
                        for k2 in range(2):
                            P.op('pe', lambda e, b=b, g=g, k2=k2, d0=d0: e.matmul(
                                PB[d0 + k2][:, :], lhsT=cmb[:, SL, :],
                                rhs=rhsD[b][:, g * 8 + k2 * 4:g * 8 + k2 * 4 + 4, :], start=True, stop=True),
                                reads=[cmb, rhsD[b]], writes=[PB[d0 + k2]])
                        P.op('act', lambda e, g2=g2, d0=d0: e.activation(out=expD[g2][:], in_=psum[:, d0:d0 + 2, :].rearrange("p a (h t) -> p (a h) t", h=4),
                                                                        func=AF.Exp), reads=[PB[d0], PB[d0 + 1]], writes=[expD[g2]])
                        P.op('pool' if g % 2 == 0 else 'dve', lambda e, b=b, g=g, g2=g2: e.tensor_tensor(
                            out=G[g2][:], in0=expD[g2][:], in1=cbm[b][:, g:g + 1, :].to_broadcast([128, 8, 128]), op=ALU.mult),
                            reads=[expD[g2], cbm[b]], writes=[G[g2]])
                    for g in range(NG):
                        g2 = g2s[g]
                        P.op('pe', lambda e, b=b, g=g: e.matmul(PB[4][:, :], lhsT=cT[b][:, g, :], rhs=Sb[g][:], start=True, stop=True),
                             reads=[cT[b], Sb[g]], writes=[PB[4]])
                        for h8 in range(8):
                            P.op('pe', lambda e, b=b, g=g, g2=g2, h8=h8: e.matmul(
                                PB[5][:, h8 * 64:(h8 + 1) * 64], lhsT=G[g2][:, h8, :], rhs=xdt[b][:, g * 8 + h8, :],
                                start=True, stop=True), reads=[G[g2], xdt[b]], writes=[PB[5]])
                        P.op('dve', lambda e, b=b, g=g, g2=g2: e.tensor_tensor(
                            out=tmp[g2][:], in0=PB[4][:, :].rearrange("p (h q) -> p h q", q=64),
                            in1=ex[b][:, g * 8:(g + 1) * 8].unsqueeze(2).to_broadcast([128, 8, 64]), op=ALU.mult),
                            reads=[PB[4], ex[b]], writes=[tmp[g2]])
                        P.op('dve', lambda e, b=b, g=g, g2=g2: e.tensor_tensor(
                            out=y[b][:, g * 8:(g + 1) * 8, :], in0=PB[5][:, :].rearrange("p (h q) -> p h q", q=64),
                            in1=tmp[g2][:], op=ALU.add), reads=[PB[5], tmp[g2]], writes=[y[b]])
                        P.op('pe', lambda e, b=b, g=g: e.matmul(PB[4][:, :], lhsT=btok[b][:, g * 128:(g + 1) * 128],
                                                               rhs=xw[b][:, g * 8:(g + 1) * 8, :], start=True, stop=True),
                             reads=[btok[b], xw[b]], writes=[PB[4]])
                        P.op('pool', lambda e, b=b, g=g: e.tensor_tensor(
                            out=Sf[g][:], in0=Sf[g][:], in1=ex[b][:, 2 * NH + g * 8:2 * NH + (g + 1) * 8].unsqueeze(2).to_broadcast([128, 8, 64]),
                            op=ALU.mult), reads=[Sf[g], ex[b]], writes=[Sf[g]])
                        P.op('dve', lambda e, g=g: e.tensor_tensor(out=Sf[g][:], in0=Sf[g][:],
                                                                  in1=PB[4][:, :].rearrange("p (h q) -> p h q", q=64), op=ALU.add),
                             reads=[Sf[g], PB[4]], writes=[Sf[g]])
                        P.op('act', lambda e, g=g: e.activation(out=Sb[g][:], in_=Sf[g][:], func=AF.Copy), reads=[Sf[g]], writes=[Sb[g]])

                def stage_c(ci):
                    c = order[ci]
                    b = ci % NB
                    tok = slice(c * 128, (c + 1) * 128)
                    if d == 0:
                        P.dma('act', YF[tok, :].rearrange("t (h p) -> t h p", p=64), y[b][:], reads=[y[b]])
                    else:
                        yb = y[b]
                        P.op('dve', lambda e, b=b: e.tensor_tensor(out=yb[:], in0=yb[:], in1=yfw[b][:], op=ALU.add),
                             reads=[yb, yfw[b]], writes=[yb])
                        P.op('pool', lambda e, b=b: e.tensor_tensor(out=t3[:], in0=xtok[b][:],
                                                                   in1=dsk[:].unsqueeze(2).to_broadcast([128, NH, 64]), op=ALU.mult),
                             reads=[xtok[b], dsk], writes=[t3])
                        P.op('dve', lambda e: e.tensor_tensor(out=yb[:], in0=yb[:], in1=t3[:], op=ALU.add), reads=[yb, t3], writes=[yb])
                        ybf = yb[:].rearrange("p h q -> p (h q)")
                        P.op('pool', lambda e, b=b: e.tensor_tensor(out=ybf, in0=ybf, in1=zs[b][:], op=ALU.mult),
                             reads=[yb, zs[b]], writes=[yb])
                        sqf = sq[:].rearrange("p g q -> p (g q)")
                        P.op('pool', lambda e: e.tensor_tensor(out=sqf, in0=ybf, in1=ybf, op=ALU.mult), reads=[yb], writes=[sq])
                        P.op('dve', lambda e: e.tensor_reduce(out=ss[:], in_=sq[:], axis=AX.X, op=ALU.add), reads=[sq], writes=[ss])
                        P.op('act', lambda e: e.activation(out=ss[:], in_=ss[:], func=AF.Ln, bias=RMS_EPS, scale=1.0 / 512),
                             reads=[ss], writes=[ss])
                        P.op('act', lambda e: e.activation(out=ss[:], in_=ss[:], func=AF.Exp, scale=-0.5), reads=[ss], writes=[ss])
                        P.op('dve', lambda e: e.tensor_tensor(out=sq[:], in0=ybf.rearrange("p (g q) -> p g q", g=NG),
                                                              in1=ss[:].unsqueeze(2).to_broadcast([128, NG, 512]), op=ALU.mult),
                             reads=[yb, ss], writes=[sq])
                        P.op('pool', lambda e: e.tensor_tensor(out=yn[:], in0=sqf, in1=nwb[:], op=ALU.mult), reads=[sq, nwb], writes=[yn])
                        yt2 = ynT[ci % 2]
                        for q4 in range(2):
                            pb = PB[1] if q4 == 0 else PB[0]
                            for u in range(4):
                                i = q4 * 4 + u
                                P.op('pe', lambda e, i=i, u=u, pb=pb: e.matmul(pb[:, u * 128:(u + 1) * 128], lhsT=yn[:, i * 128:(i + 1) * 128],
                                                                              rhs=cmb[:, C_IDENT, :], start=True, stop=True),
                                     reads=[yn, cmb], writes=[pb])
                            P.op('act', lambda e, q4=q4, pb=pb: e.activation(out=yt2[:, q4 * 4:(q4 + 1) * 4, :],
                                                                            in_=pb[:, :].rearrange("p (u t) -> p u t", u=4), func=AF.Copy),
                                 reads=[pb], writes=[yt2])
                        P.dma('act', YNT.rearrange("(i p) s -> p i s", p=128)[:, :, tok], yt2[:], reads=[yt2])

                nord = len(order)
                stage_a(0)
                if nord > 1:
                    stage_a(1)
                stage_b(0)
                for ci in range(nord):
                    if ci + 2 < nord:
                        stage_a(ci + 2)
                    if ci + 1 < nord:
                        stage_b(ci + 1)
                    stage_c(ci)

        def odd_out_phase(oo):
            with Phase() as ph:
                wo = load_w(ph, ssd_w_out[oo], 1024, D)
                mix = [ph.sb([128, 8, 512], BF16) for _ in range(2)]
                ob = [ph.sb([128, 8, 512], BF16) for _ in range(2)]
                for ti, (c, t, l0, n) in enumerate(ctiles):
                    s0 = t * Sh + l0
                    b2 = ti % 2
                    P.dma('sp', mix[b2][:, :, :n], YNT.rearrange("(i p) s -> p i s", p=128)[:, :, s0:s0 + n], writes=[mix[b2]])
                    o_ = ob[b2]
                    for m in range(8):
                        po = PB[4 + m % 4]
                        for k in range(8):
                            P.op('pe', lambda e, m=m, k=k, po=po, b2=b2: e.matmul(
                                po[:, :n], lhsT=wo[:, k, m * 128:(m + 1) * 128], rhs=mix[b2][:, k, :n],
                                start=(k == 0), stop=(k == 7)), reads=[wo, mix[b2]], writes=[po])
                        if m % 2 == 0:
                            P.op('act', lambda e, m=m, po=po: e.activation(out=o_[:, m, :n], in_=po[:, :n], func=AF.Copy),
                                 reads=[po], writes=[o_])
                        else:
                            P.op('dve', lambda e, m=m, po=po: e.tensor_copy(out=o_[:, m, :n], in_=po[:, :n]), reads=[po], writes=[o_])
                    P.dma('act', Ppc[c][t].rearrange("k p s -> p k s"), o_[:, :, :n], reads=[o_], writes=[ppres[c]])
                    if t == 1:
                        rs_chunk(c)

        Xcur = xin
        for l in range(DEPTH):
            Xlast = yout if l == DEPTH - 1 else X2
            modh_phase(l, Xcur)
            if l % 2 == 0:
                ee = l // 2
                even_proj_phase(ee)
                gla_phase(ee, 0)
                gla_phase(ee, 1)
                attn_phase(ee)
            else:
                oo = l // 2
                odd_proj_phase(oo)
                conv_phase(oo)
                ssd_phase(oo, 0)
                ssd_phase(oo, 1)
            with Phase() as phw:
                win = load_w(phw, ffn_w_in[l], D, 2 * FFH)
                wout = load_w(phw, ffn_w_out[l], FFH, D)
                if l % 2 == 0:
                    even_out_phase(l // 2)
                else:
                    odd_out_phase(l // 2)
                ffn_phase(l, Xcur, Xlast, win, wout)
            Xcur = Xlast
        P.barrier()
        P.emit()
    return nc


def _prep_inputs(inputs, TL):
    x = np.asarray(inputs['x'], np.float32)
    B = x.shape[0]
    S = CTX + TL
    Sh = S // 2
    g = lambda k: np.asarray(inputs[k], np.float32)
    f32 = lambda a: np.ascontiguousarray(np.asarray(a, np.float32))
    shared = {
        'mod_w': f32(g('mod_w')),
        'mod_bT': f32(g('mod_b').reshape(4, 48, 128).transpose(2, 0, 1)),
        'ln_gT': f32(g('ln_g').reshape(4, 2, 8, 128).transpose(3, 0, 1, 2)),
        'ln_bT': f32(g('ln_b').reshape(4, 2, 8, 128).transpose(3, 0, 1, 2)),
        'ffn_w_in': f32(g('ffn_w_in')),
        'ffn_w_out': f32(g('ffn_w_out')),
        'gla_nw': f32(g('gla_norm_w').T),
        'diff_nw': f32(g('diff_norm_w').T),
        'diff_lam': f32(g('diff_lambda').reshape(1, 2, 256)),
        'cmat': _const_mats(),
    }
    cos, sin = _rope_tables(TL)
    shared['ropecos'] = cos
    shared['ropesin'] = sin
    per_rank = []
    evw, evo = g('ev_w_in'), g('ev_w_out')
    sw, so = g('ssd_w_in'), g('ssd_w_out')
    cwf, cbf = g('ssd_conv_w'), g('ssd_conv_b')
    for r in range(2):
        pr = {}
        cols = np.concatenate([
            np.arange(0, 256)[r * 128:(r + 1) * 128],
            256 + np.arange(256)[r * 128:(r + 1) * 128],
            512 + np.arange(512)[r * 256:(r + 1) * 256],
            1024 + np.arange(512)[r * 256:(r + 1) * 256],
            np.arange(1536, 1568),
            1568 + np.arange(512)[r * 256:(r + 1) * 256],
            2080 + np.arange(512)[r * 256:(r + 1) * 256],
            2592 + np.arange(512)[r * 256:(r + 1) * 256],
        ])
        pr['ev_w_in'] = f32(evw[:, :, cols])
        rows = np.concatenate([np.arange(512)[r * 256:(r + 1) * 256], 512 + np.arange(512)[r * 256:(r + 1) * 256]])
        pr['ev_w_out'] = f32(evo[:, rows, :])
        wg = np.zeros((2, 17, 2, 128), np.float32)
        wg[:, 0:16] = g('gla_w_gate2').transpose(0, 2, 1, 3)[..., r * 128:(r + 1) * 128]
        wg[:, 16] = g('gla_b_gate')[..., r * 128:(r + 1) * 128]
        pr['gla_wg'] = wg
        hs = slice(r * 16, (r + 1) * 16)
        zc = np.arange(2048)[r * 1024:(r + 1) * 1024]
        xc = 2048 + np.arange(2048)[r * 1024:(r + 1) * 1024]
        bc = 4096 + np.arange(512)[r * 256:(r + 1) * 256]
        cc = 4608 + np.arange(512)[r * 256:(r + 1) * 256]
        dc = np.concatenate([5120 + np.arange(32)[hs], 5152 + np.arange(32)[hs]])
        pr['ssd_w_in'] = f32(sw[:, :, np.concatenate([zc, xc, bc, cc, dc])])
        pr['ssd_w_out'] = f32(so[:, r * 1024:(r + 1) * 1024, :])
        cch = np.concatenate([xc, bc, cc]) - 2048
        pr['ssd_cw'] = f32(cwf[:, :, cch].reshape(2, 5, 12, 128).transpose(0, 3, 1, 2))
        pr['ssd_cb'] = f32(cbf[:, cch].reshape(2, 12, 128).transpose(0, 2, 1))
        rep = lambda a: f32(np.broadcast_to(np.asarray(a, np.float32)[:, None, :], (2, 128, np.asarray(a).shape[-1])))
        pr['ssd_dtb'] = rep(g('ssd_dt_bias')[:, :, hs].reshape(2, 32))
        pr['ssd_alog'] = rep(g('ssd_a_log')[:, :, hs].reshape(2, 32))
        pr['ssd_dsk'] = rep(g('ssd_d')[:, hs])
        pr['ssd_nw'] = rep(g('ssd_norm_w')[:, r * 1024:(r + 1) * 1024])
        pr['mctx'] = np.full((128, 1), 1.0 if r == 0 else 0.0, np.float32)
        per_rank.append(pr)
    maps = []
    for core in range(8):
        b, r = core % 4, core // 4
        b = b % B
        xc_ = np.concatenate([g('ctx')[b], x[b]], 0)
        m = dict(shared)
        m.update(per_rank[r])
        m['xin'] = np.ascontiguousarray(xc_[r * Sh:(r + 1) * Sh].T.reshape(8, 128, Sh))
        cs = np.stack([g('c')[b], g('c_ctx')], -1)
        m['csil'] = np.ascontiguousarray(cs.reshape(8, 128, 2).transpose(1, 0, 2))
        maps.append(m)
    return maps


def run(inputs, TL, DEPTH, dump=()):
    nc = build(TL, DEPTH, dump)
    maps = _prep_inputs(inputs, TL)
    return run_bass_kernel_spmd(nc, maps, core_ids=list(range(8)))


def _gather(res, B, TL):
    S = CTX + TL
    Sh = S // 2
    out = np.empty((B, TL, D), np.float32)
    for b in range(B):
        full = np.concatenate([res.results[b + 4 * r]['yout'].reshape(D, Sh) for r in range(2)], axis=1)
        out[b] = full[:, CTX:].T
    return out


def kernel(**inputs):
    x = np.asarray(inputs['x'])
    B, TL, _ = x.shape
    res = run(inputs, TL, 4)
    return _gather(res, B, TL)
```

```python
import contextlib
import math
import numpy as np
import concourse.bass as bass
import concourse.mybir as mybir
from concourse.bass_utils import run_bass_kernel_spmd

F32 = mybir.dt.float32
BF16 = mybir.dt.bfloat16
AF = mybir.ActivationFunctionType
ALU = mybir.AluOpType
AX = mybir.AxisListType

ENGS = ('pe', 'act', 'dve', 'pool', 'sp')
DQS = ('sp', 'pool', 'act')
SEM_LIMIT = 30000
N_EPOCH = 12
N_DMASEM = 8


class Res:
    __slots__ = ('w', 'r')

    def __init__(self):
        self.w = None
        self.r = []


class T:
    def __init__(self, ap, res=None):
        self.ap = ap
        self.res = res if res is not None else Res()

    def __getitem__(self, idx):
        return self.ap[idx]


def _res(t):
    return t.res if isinstance(t, T) else t


class _Rec:
    def __getattr__(self, name):
        def f(*a, **k):
            self.call = (name, a, k)
        return f


class Prog:
    def __init__(self, nc, same_engine_sync=True):
        self.nc = nc
        self.same = same_engine_sync
        self.streams = {e: [] for e in ENGS}
        self.sems = {e: [nc.alloc_semaphore(name=f"s_{e}_{i}") for i in range(N_EPOCH)] for e in ENGS}
        self.ops = {e: [] for e in ENGS}
        self.seen = {e: {} for e in ENGS}
        self.dsems = {q: [nc.alloc_semaphore(name=f"d_{q}_{i}") for i in range(N_DMASEM)] for q in DQS}
        self.dcnt = {q: 0 for q in DQS}
        self.nops = 0

    def _need(self, eng, ev):
        kind, src, val = ev
        if kind == 'c':
            if src == eng and (eng == 'pe' or not self.same):
                return
            if self.seen[eng].get(('c', src), 0) >= val:
                return
            self.seen[eng][('c', src)] = val
            self.ops[src][val - 1]['signal'] = True
            self.streams[eng].append({'waitc': (src, val)})
        else:
            key = (kind, src[0], src[1])
            if self.seen[eng].get(key, 0) >= val:
                return
            self.seen[eng][key] = val
            self.streams[eng].append({'wait': (self.dsems[src[0]][src[1]], val)})

    def _collect(self, reads, writes):
        evs = []
        for t in reads:
            r = _res(t)
            if r.w is not None:
                evs.append(r.w)
        for t in writes:
            r = _res(t)
            if r.w is not None:
                evs.append(r.w)
            evs.extend(r.r)
        return evs

    def _mark(self, ev, reads, writes):
        for t in reads:
            _res(t).r.append(ev)
        for t in writes:
            r = _res(t)
            r.w = ev
            r.r = []

    def op(self, eng, fn, reads=(), writes=()):
        for ev in self._collect(reads, writes):
            self._need(eng, ev)
        r = _Rec()
        fn(r)
        rec = {'call': r.call, 'signal': False}
        self.streams[eng].append(rec)
        self.ops[eng].append(rec)
        ev = ('c', eng, len(self.ops[eng]))
        self._mark(ev, reads, writes)
        self.nops += 1
        return ev

    def dma(self, q, out, in_, reads=(), writes=(), **kw):
        evs = self._collect(reads, writes)
        k = self.dcnt[q]
        self.dcnt[q] += 1
        si, rnd = k % N_DMASEM, k // N_DMASEM
        if rnd > 0:
            evs.append(('d', (q, si), 16 * rnd))
        for ev in evs:
            self._need(q, ev)
        rec = {'dma': (out, in_, kw), 'sig': (self.dsems[q][si], 16)}
        self.streams[q].append(rec)
        ev = ('d', (q, si), 16 * (rnd + 1))
        self._mark(ev, reads, writes)
        self.nops += 1
        return ev

    def barrier(self):
        evs = []
        for e in ENGS:
            if self.ops[e]:
                evs.append(('c', e, len(self.ops[e])))
        for q in DQS:
            for si in range(N_DMASEM):
                n = (self.dcnt[q] - si + N_DMASEM - 1) // N_DMASEM if self.dcnt[q] > si else 0
                if n > 0:
                    evs.append(('d', (q, si), 16 * n))
        for e in ENGS:
            for ev in evs:
                if ev[0] == 'c' and ev[1] == e:
                    continue
                self._need(e, ev)

    def _resolve(self):
        for e in ENGS:
            epoch, c = 0, 0
            for rec in self.ops[e]:
                if rec['signal']:
                    c += 1
                    if c > SEM_LIMIT:
                        epoch += 1
                        c = 1
                        assert epoch < N_EPOCH, "out of semaphore epochs"
                    rec['semval'] = (self.sems[e][epoch], c)

    def _replay(self, eng_obj, stream):
        for rec in stream:
            if 'waitc' in rec:
                src, idx = rec['waitc']
                s, v = self.ops[src][idx - 1]['semval']
                eng_obj.wait_ge(s, v)
            elif 'wait' in rec:
                s, v = rec['wait']
                eng_obj.wait_ge(s, v)
            elif 'dma' in rec:
                out, in_, kw = rec['dma']
                ins = eng_obj.dma_start(out=out, in_=in_, **kw)
                s, v = rec['sig']
                ins.then_inc(s, v)
            else:
                name, a, k = rec['call']
                ins = getattr(eng_obj, name)(*a, **k)
                if rec['signal']:
                    ins.then_inc(rec['semval'][0], 1)

    def emit(self):
        self._resolve()
        with self.nc.Block() as block:
            @block.tensor
            def _(e):
                self._replay(e, self.streams['pe'])

            @block.scalar
            def _(e):
                self._replay(e, self.streams['act'])

            @block.vector
            def _(e):
                self._replay(e, self.streams['dve'])

            @block.gpsimd
            def _(e):
                self._replay(e, self.streams['pool'])

            @block.sync
            def _(e):
                self._replay(e, self.streams['sp'])


CTX = 256
D = 1024
FFH = 2816
NJ = FFH // 128
ALPHA = (2 * 4) ** 0.25
LN_EPS = 1e-6
RMS_EPS = 1e-6
GRID_W = 64
ROPE_BASE = 10000.0
GROUPS = [[0, 4], [1, 5], [2, 6], [3, 7]]
EVC = 1568
SDC = 2592

C_IDENT, C_TRIF, C_TRIB, C_STRIF, C_STRIB, C_MASKF, C_MASKB, C_PERM, C_ONES, C_MEAN, C_UF, C_UB, C_SLF, C_SLB = range(14)
NCMAT = 14


def _const_mats():
    j = np.arange(128)[:, None]
    i = np.arange(128)[None, :]
    m = np.zeros((128, NCMAT, 128), np.float32)
    m[:, C_IDENT] = (j == i)
    m[:, C_TRIF] = (j <= i) * (-1.0 / 16)
    m[:, C_TRIB] = (j >= i) * (-1.0 / 16)
    m[:, C_STRIF] = (j > i) * (-1.0 / 16)
    m[:, C_STRIB] = (j < i) * (-1.0 / 16)
    m[:, C_MASKF] = (j <= i)
    m[:, C_MASKB] = (j >= i)
    m[:, C_PERM] = (j == (i ^ 16))
    m[:, C_ONES] = 1.0
    m[:, C_MEAN] = 1.0 / 1024
    m[:, C_UF] = (j <= i)
    m[:, C_UB] = (j >= i)
    m[:, C_SLF] = (j > i)
    m[:, C_SLB] = (j < i)
    return m


def _rope_tables(TL):
    t = np.arange(TL)
    row = (t // GRID_W).astype(np.float32)
    col = (t % GRID_W).astype(np.float32)
    nf = 16
    inv = (ROPE_BASE ** (-np.arange(nf, dtype=np.float32) / nf)).astype(np.float32)
    ar = row[None, :] * inv[:, None]
    ac = col[None, :] * inv[:, None]
    cos64 = np.concatenate([np.cos(ar), np.cos(ar), np.cos(ac), np.cos(ac)], 0)
    sin64 = np.concatenate([-np.sin(ar), np.sin(ar), -np.sin(ac), np.sin(ac)], 0)
    cos = np.concatenate([cos64, cos64], 0).astype(np.float32)
    sin = np.concatenate([sin64, sin64], 0).astype(np.float32)
    return np.ascontiguousarray(cos), np.ascontiguousarray(sin)


def _tiles(total, first, step):
    out = [(0, first)]
    s = first
    while s < total:
        n = min(step, total - s)
        out.append((s, n))
        s += n
    return out


def build(TL, DEPTH, dump=()):
    S = CTX + TL
    Sh = S // 2
    assert Sh % 128 == 0
    NCH = S // 128
    own512 = _tiles(Sh, 256, 512)
    own256 = _tiles(Sh, 256, 256)
    gtiles = [(t, l0, n) for t in range(2) for (l0, n) in own512]
    ctiles = [(c, t, l0, n) for c, (l0, n) in enumerate(own512) for t in range(2)]
    nc = bass.Bass("TRN2", target_bir_lowering=False)
    P = Prog(nc)

    def din(name, shape, dt=F32):
        return nc.dram_tensor(name, list(shape), dt, kind="ExternalInput").ap()

    def dscr(name, shape, dt):
        kind = "ExternalOutput" if name in dump else "Internal"
        return nc.dram_tensor(name, list(shape), dt, kind=kind).ap()

    xin = din("xin", [8, 128, Sh])
    csil = din("csil", [128, 8, 2])
    mctx = din("mctx", [128, 1])
    mod_w = din("mod_w", [4, D, 6 * D])
    mod_bT = din("mod_bT", [128, 4, 48])
    ln_gT = din("ln_gT", [128, 4, 2, 8])
    ln_bT = din("ln_bT", [128, 4, 2, 8])
    ffn_w_in = din("ffn_w_in", [4, D, 2 * FFH])
    ffn_w_out = din("ffn_w_out", [4, FFH, D])
    ev_w_in = din("ev_w_in", [2, D, EVC])
    ev_w_out = din("ev_w_out", [2, 512, D])
    gla_wg = din("gla_wg", [2, 17, 2, 128])
    gla_nw = din("gla_nw", [128, 2])
    diff_nw = din("diff_nw", [128, 2])
    diff_lam = din("diff_lam", [1, 2, 256])
    ssd_w_in = din("ssd_w_in", [2, D, SDC])
    ssd_w_out = din("ssd_w_out", [2, 1024, D])
    ssd_cw = din("ssd_cw", [2, 128, 5, 12])
    ssd_cb = din("ssd_cb", [2, 128, 12])
    ssd_dtb = din("ssd_dtb", [2, 128, 32])
    ssd_alog = din("ssd_alog", [2, 128, 32])
    ssd_dsk = din("ssd_dsk", [2, 128, 16])
    ssd_nw = din("ssd_nw", [2, 128, 1024])
    cmat_d = din("cmat", [128, NCMAT, 128])
    cos_d = din("ropecos", [128, TL])
    sin_d = din("ropesin", [128, TL])
    yout = nc.dram_tensor("yout", [8, 128, Sh], F32, kind="ExternalOutput").ap()

    X2 = dscr("X2", [8, 128, Sh], F32)
    Hloc = dscr("Hloc", [8, 128, Sh], BF16)
    Hg = dscr("Hg", [8, 2, 128, Sh], BF16)
    Ppc = [dscr(f"Pp{c}", [2, 8, 128, n], BF16) for c, (l0, n) in enumerate(own512)]
    Moc = [dscr(f"Mo{c}", [8, 128, n], BF16) for c, (l0, n) in enumerate(own512)]
    QG = dscr("QG", [128, S], BF16)
    KG = dscr("KG", [128, S], BF16)
    GG = dscr("GG", [256, S], BF16)
    RR = dscr("RR", [32, S], BF16)
    QD = dscr("QD", [256, S], BF16)
    KD = dscr("KD", [256, S], BF16)
    KGt = dscr("KGt", [S, 128], BF16)
    VGt = dscr("VGt", [S, 256], BF16)
    VDt = dscr("VDt", [S, 256], BF16)
    OG = dscr("OG", [256, S], F32)
    DT = dscr("DT", [256, S], BF16)
    XBC = dscr("XBC", [1536, S], BF16)
    ZS = dscr("ZS", [S, 1024], BF16)
    DTt = dscr("DTt", [S, 32], F32)
    BTd = dscr("BTd", [256, S], BF16)
    CTd = dscr("CTd", [256, S], BF16)
    XSt = dscr("XSt", [S, 1024], BF16)
    Btd = dscr("Btd", [S, 256], BF16)
    YF = dscr("YF", [S, 1024], F32)
    YNT = dscr("YNT", [1024, S], BF16)

    def fm(x3):
        return x3.rearrange("k p s -> p k s")

    with contextlib.ExitStack() as gs:
        cnt = [0]

        def gsb(shape, dt, name=None):
            cnt[0] += 1
            return T(gs.enter_context(nc.sbuf_tensor(name or f"g{cnt[0]}", list(shape), dt)))

        psum = gs.enter_context(nc.psum_tensor("psum", [128, 8, 512], F32))
        PB = [T(psum[:, i, :]) for i in range(8)]

        class Phase:
            def __enter__(self):
                self.es = contextlib.ExitStack()
                return self

            def sb(self, shape, dt, name=None):
                cnt[0] += 1
                return T(self.es.enter_context(nc.sbuf_tensor(name or f"t{cnt[0]}", list(shape), dt)))

            def __exit__(self, *a):
                P.barrier()
                self.es.close()
                return False

        ccres = Res()
        ppres = [Res() for _ in own512]

        def rs_chunk(c):
            P.op('pool', lambda e: e.collective_compute("ReduceScatter", ALU.add, replica_groups=GROUPS,
                                                        ins=[Ppc[c].rearrange("t k p s -> (t k p) s")],
                                                        outs=[Moc[c].rearrange("k p s -> (k p) s")]),
                 reads=[ppres[c]], writes=[ccres])

        def mo_view(l0, n):
            for c, (c0, cn) in enumerate(own512):
                if c0 <= l0 and l0 + n <= c0 + cn:
                    return Moc[c].rearrange("k p s -> p k s")[:, :, l0 - c0:l0 - c0 + n]
            raise AssertionError((l0, n))

        def all_gather(src2d, dst2d):
            P.barrier()
            P.op('pool', lambda e: e.collective_compute("AllGather", ALU.bypass, replica_groups=GROUPS,
                                                        ins=[src2d], outs=[dst2d]), writes=[ccres])
            P.barrier()

        def reduce_scatter(src2d, dst2d):
            P.barrier()
            P.op('pool', lambda e: e.collective_compute("ReduceScatter", ALU.add, replica_groups=GROUPS,
                                                        ins=[src2d], outs=[dst2d]), writes=[ccres])
            P.barrier()

        cmb = gsb([128, NCMAT, 128], BF16, "cmb")
        modT = gsb([128, 4, 48, 2], F32, "modT")
        mod1p = gsb([128, 4, 48, 2], F32, "mod1p")
        lng = gsb([128, 4, 2, 8], F32, "lng")
        lnb = gsb([128, 4, 2, 8], F32, "lnb")
        gnw = gsb([128, 2], F32, "gnw")
        dnw = gsb([128, 2], F32, "dnw")
        dnws = gsb([128, 2], F32, "dnws")
        nlam = gsb([128, 2], F32, "nlam")
        mcx = gsb([128, 1], F32, "mcx")

        def load_cmf(ph):
            t = ph.sb([128, NCMAT, 128], F32)
            P.dma('sp', t[:], cmat_d, writes=[t])
            return t
        P.dma('pool', cmb[:], cmat_d, writes=[cmb])
        P.dma('sp', lng[:], ln_gT, writes=[lng])
        P.dma('sp', lnb[:], ln_bT, writes=[lnb])
        P.dma('sp', gnw[:], gla_nw, writes=[gnw])
        P.dma('sp', dnw[:], diff_nw, writes=[dnw])
        P.dma('sp', mcx[:], mctx, writes=[mcx])

        def lam_init(layer):
            return 0.8 - 0.6 * math.exp(-0.3 * layer)

        with Phase() as ph:
            cmf = load_cmf(ph)
            sc = ph.sb([128, 8, 2], F32)
            modb = ph.sb([128, 4, 48], F32)
            mdiff = ph.sb([128, 4, 48], F32)
            P.dma('sp', sc[:], csil, writes=[sc])
            P.dma('sp', modb[:], mod_bT, writes=[modb])
            P.op('act', lambda e: e.activation(out=sc[:], in_=sc[:], func=AF.Silu), reads=[sc], writes=[sc])
            wb = [ph.sb([128, 8, 1024], F32) for _ in range(2)]
            it = 0
            for l in range(DEPTH):
                for cb in range(6):
                    wt = wb[it % 2]
                    pb = PB[it % 2]
                    it += 1
                    for k in range(8):
                        P.dma('sp' if k % 2 == 0 else 'act', wt[:, k, :],
                              mod_w[l, k * 128:(k + 1) * 128, cb * 1024:(cb + 1) * 1024], writes=[wt])
                    for m in range(8):
                        for k in range(8):
                            P.op('pe', lambda e, m=m, k=k, wt=wt, pb=pb: e.matmul(
                                pb[:, 2 * m:2 * m + 2], lhsT=wt[:, k, m * 128:(m + 1) * 128], rhs=sc[:, k, :],
                                start=(k == 0), stop=(k == 7)), reads=[wt, sc], writes=[pb])
                    P.op('dve', lambda e, l=l, cb=cb, pb=pb: e.tensor_tensor(
                        out=modT[:, l, cb * 8:(cb + 1) * 8, :],
                        in0=pb[:, 0:16].rearrange("p (m j) -> p m j", j=2),
                        in1=modb[:, l, cb * 8:(cb + 1) * 8].unsqueeze(2).to_broadcast([128, 8, 2]),
                        op=ALU.add), reads=[pb, modb], writes=[modT])
            P.op('dve', lambda e: e.tensor_tensor(out=mdiff[:], in0=modT[:, :, :, 1], in1=modT[:, :, :, 0], op=ALU.subtract),
                 reads=[modT], writes=[mdiff])
            P.op('dve', lambda e: e.scalar_tensor_tensor(out=modT[:, :, :, 1], in0=mdiff[:], scalar=mcx[:, 0:1], in1=modT[:, :, :, 0],
                                                         op0=ALU.mult, op1=ALU.add), reads=[mdiff, mcx, modT], writes=[modT])
            P.op('dve', lambda e: e.tensor_scalar_add(out=mod1p[:], in0=modT[:], scalar1=1.0),
                 reads=[modT], writes=[mod1p])
            dl = ph.sb([1, 2, 256], F32)
            pr = ph.sb([1, 2, 2, 64], F32)
            sm = ph.sb([1, 4], F32)
            lv = ph.sb([1, 2], F32)
            P.dma('sp', dl[:], diff_lam, writes=[dl])
            dlv = dl[:].rearrange("o e (a b d) -> o e a b d", a=2, b=2)
            P.op('dve', lambda e: e.tensor_tensor(out=pr[:], in0=dlv[:, :, :, 0, :], in1=dlv[:, :, :, 1, :], op=ALU.mult),
                 reads=[dl], writes=[pr])
            P.op('dve', lambda e: e.tensor_reduce(out=sm[:], in_=pr[:].rearrange("o e a d -> o (e a) d"),
                                                  axis=AX.X, op=ALU.add), reads=[pr], writes=[sm])
            P.op('act', lambda e: e.activation(out=sm[:], in_=sm[:], func=AF.Exp), reads=[sm], writes=[sm])
            smv = sm[:].rearrange("o (e a) -> o e a", a=2)
            P.op('dve', lambda e: e.tensor_tensor(out=lv[:], in0=smv[:, :, 1], in1=smv[:, :, 0], op=ALU.subtract),
                 reads=[sm], writes=[lv])
            for ee in range(2):
                P.op('dve', lambda e, ee=ee: e.tensor_scalar_add(out=lv[:, ee:ee + 1], in0=lv[:, ee:ee + 1],
                                                                 scalar1=-lam_init(2 * ee)), reads=[lv], writes=[lv])
            P.op('pe', lambda e: e.matmul(PB[2][:, 0:2], lhsT=cmf[0:1, C_ONES, :], rhs=lv[:], start=True, stop=True),
                 reads=[cmf, lv], writes=[PB[2]])
            P.op('dve', lambda e: e.tensor_copy(out=nlam[:], in_=PB[2][:, 0:2]), reads=[PB[2]], writes=[nlam])
            for ee in range(2):
                P.op('dve', lambda e, ee=ee: e.tensor_scalar_mul(out=dnws[:, ee:ee + 1], in0=dnw[:, ee:ee + 1],
                                                                 scalar1=1.0 - lam_init(2 * ee)), reads=[dnw], writes=[dnws])

        def load_w(ph, src, K, N):
            kc = K // 128
            w = ph.sb([128, kc, N], BF16)
            for k in range(kc):
                P.dma('pool', w[:, k, :], src[k * 128:(k + 1) * 128, :], writes=[w])
            return w

        def modulate(xt, ht, n, l, which_shift, which_scale, j):
            for k in range(8):
                P.op('act', lambda e, k=k: e.activation(
                    out=ht[:, k, :n], in_=xt[:, k, :n], func=AF.Identity,
                    scale=mod1p[:, l, which_scale * 8 + k, j:j + 1], bias=modT[:, l, which_shift * 8 + k, j:j + 1]),
                    reads=[xt, mod1p, modT], writes=[ht])

        def layer_norm(ph_bufs, v, n, l, which):
            vb, vsq, mean_sb, rstd = ph_bufs
            P.op('act', lambda e: e.activation(out=vb[:, 0:8, :n], in_=v[:, :, :n], func=AF.Copy), reads=[v], writes=[vb])
            P.op('pool', lambda e: e.tensor_tensor(out=vsq[:, 0:8, :n], in0=v[:, :, :n], in1=v[:, :, :n], op=ALU.mult),
                 reads=[v], writes=[vsq])
            for k in range(8):
                P.op('pe', lambda e, k=k: e.matmul(PB[6][:, :n], lhsT=cmb[:, C_MEAN, :], rhs=vb[:, k, :n],
                                                   start=(k == 0), stop=(k == 7)), reads=[cmb, vb], writes=[PB[6]])
            for k in range(8):
                P.op('pe', lambda e, k=k: e.matmul(PB[7][:, :n], lhsT=cmb[:, C_MEAN, :], rhs=vsq[:, k, :n],
                                                   start=(k == 0), stop=(k == 7)), reads=[cmb, vsq], writes=[PB[7]])
            P.op('act', lambda e: e.activation(out=mean_sb[:, :n], in_=PB[6][:, :n], func=AF.Copy),
                 reads=[PB[6]], writes=[mean_sb])
            P.op('dve', lambda e: e.tensor_tensor(out=rstd[:, :n], in0=mean_sb[:, :n], in1=mean_sb[:, :n], op=ALU.mult),
                 reads=[mean_sb], writes=[rstd])
            P.op('dve', lambda e: e.tensor_tensor(out=rstd[:, :n], in0=PB[7][:, :n], in1=rstd[:, :n], op=ALU.subtract),
                 reads=[PB[7], rstd], writes=[rstd])
            P.op('act', lambda e: e.activation(out=rstd[:, :n], in_=rstd[:, :n], func=AF.Sqrt, bias=LN_EPS, scale=1.0),
                 reads=[rstd], writes=[rstd])
            P.op('dve', lambda e: e.reciprocal(out=rstd[:, :n], in_=rstd[:, :n]), reads=[rstd], writes=[rstd])
            P.op('dve', lambda e: e.tensor_tensor(out=v[:, :, :n], in0=v[:, :, :n],
                                                  in1=mean_sb[:, :n].unsqueeze(1).to_broadcast([128, 8, n]),
                                                  op=ALU.subtract), reads=[v, mean_sb], writes=[v])
            P.op('pool', lambda e: e.tensor_tensor(out=v[:, :, :n], in0=v[:, :, :n],
                                                   in1=rstd[:, :n].unsqueeze(1).to_broadcast([128, 8, n]),
                                                   op=ALU.mult), reads=[v, rstd], writes=[v])
            for k in range(8):
                P.op('act', lambda e, k=k: e.activation(out=v[:, k, :n], in_=v[:, k, :n], func=AF.Identity,
                                                        scale=lng[:, l, which, k:k + 1], bias=lnb[:, l, which, k:k + 1]),
                     reads=[v, lng, lnb], writes=[v])

        def modh_phase(l, Xsrc):
            with Phase() as ph:
                xb = [ph.sb([128, 8, 512], F32) for _ in range(2)]
                hb = [ph.sb([128, 8, 512], BF16) for _ in range(2)]
                for ti, (l0, n) in enumerate(own512):
                    j = 1 if l0 == 0 else 0
                    xt, ht = xb[ti % 2], hb[ti % 2]
                    P.dma('sp', xt[:, :, :n], fm(Xsrc)[:, :, l0:l0 + n], writes=[xt])
                    modulate(xt, ht, n, l, 0, 1, j)
                    P.dma('act', fm(Hloc)[:, :, l0:l0 + n], ht[:, :, :n], reads=[ht])
            P.barrier()
            for k in range(8):
                P.op('pool', lambda e, k=k: e.collective_compute("AllGather", ALU.bypass, replica_groups=GROUPS,
                                                                 ins=[Hloc[k]], outs=[Hg[k].rearrange("t p s -> (t p) s")]),
                     writes=[ccres])
            P.barrier()

        def layer_norm2(v, n, l, which, sgt, mean_sb, rstd):
            for k in range(8):
                tt = sgt[k % 2]
                P.op('act', lambda e, k=k, tt=tt: e.activation(out=tt[:, :n], in_=v[:, k, :n], func=AF.Square), reads=[v], writes=[tt])
                P.op('pe', lambda e, k=k, tt=tt: e.matmul(PB[7][:, :n], lhsT=cmb[:, C_MEAN, :], rhs=tt[:, :n],
                                                          start=(k == 0), stop=(k == 7)), reads=[cmb, tt], writes=[PB[7]])
            for k in range(8):
                tt = sgt[k % 2]
                P.op('act', lambda e, k=k, tt=tt: e.activation(out=tt[:, :n], in_=v[:, k, :n], func=AF.Copy), reads=[v], writes=[tt])
                P.op('pe', lambda e, k=k, tt=tt: e.matmul(PB[6][:, :n], lhsT=cmb[:, C_MEAN, :], rhs=tt[:, :n],
                                                          start=(k == 0), stop=(k == 7)), reads=[cmb, tt], writes=[PB[6]])
            P.op('act', lambda e: e.activation(out=mean_sb[:, :n], in_=PB[6][:, :n], func=AF.Copy),
                 reads=[PB[6]], writes=[mean_sb])
            P.op('dve', lambda e: e.tensor_tensor(out=rstd[:, :n], in0=mean_sb[:, :n], in1=mean_sb[:, :n], op=ALU.mult),
                 reads=[mean_sb], writes=[rstd])
            P.op('dve', lambda e: e.tensor_tensor(out=rstd[:, :n], in0=PB[7][:, :n], in1=rstd[:, :n], op=ALU.subtract),
                 reads=[PB[7], rstd], writes=[rstd])
            P.op('act', lambda e: e.activation(out=rstd[:, :n], in_=rstd[:, :n], func=AF.Sqrt, bias=LN_EPS, scale=1.0),
                 reads=[rstd], writes=[rstd])
            P.op('dve', lambda e: e.reciprocal(out=rstd[:, :n], in_=rstd[:, :n]), reads=[rstd], writes=[rstd])
            P.op('dve', lambda e: e.tensor_tensor(out=v[:, :, :n], in0=v[:, :, :n],
                                                  in1=mean_sb[:, :n].unsqueeze(1).to_broadcast([128, 8, n]),
                                                  op=ALU.subtract), reads=[v, mean_sb], writes=[v])
            P.op('dve', lambda e: e.tensor_tensor(out=v[:, 0:5, :n], in0=v[:, 0:5, :n],
                                                  in1=rstd[:, :n].unsqueeze(1).to_broadcast([128, 5, n]),
                                                  op=ALU.mult), reads=[v, rstd], writes=[v])
            P.op('pool', lambda e: e.tensor_tensor(out=v[:, 5:8, :n], in0=v[:, 5:8, :n],
                                                   in1=rstd[:, :n].unsqueeze(1).to_broadcast([128, 3, n]),
                                                   op=ALU.mult), reads=[v, rstd], writes=[v])
            for k in range(8):
                P.op('act', lambda e, k=k: e.activation(out=v[:, k, :n], in_=v[:, k, :n], func=AF.Identity,
                                                        scale=lng[:, l, which, k:k + 1], bias=lnb[:, l, which, k:k + 1]),
                     reads=[v, lng, lnb], writes=[v])

        def ffn_phase(l, Xsrc, Xdst, win, wout):
            with Phase() as ph:
                xt = ph.sb([128, 8, 512], F32)
                x1a = ph.sb([128, 8, 512], F32)
                hb = ph.sb([128, 8, 512], BF16)
                actb = ph.sb([128, NJ, 512], BF16)
                sg = [ph.sb([128, 512], BF16) for _ in range(2)]
                mean_sb = ph.sb([128, 512], F32)
                rstd = ph.sb([128, 512], F32)

                def prologue(ti):
                    l0, n = own512[ti]
                    j = 1 if l0 == 0 else 0
                    mo = hb
                    P.dma('sp', xt[:, :, :n], fm(Xsrc)[:, :, l0:l0 + n], writes=[xt])
                    P.dma('sp', mo[:, :, :n], mo_view(l0, n), writes=[mo])
                    P.op('act', lambda e: e.activation(out=xt[:, :, :n], in_=xt[:, :, :n], func=AF.Copy, scale=ALPHA),
                         reads=[xt], writes=[xt])
                    for m in range(8):
                        P.op('dve', lambda e, m=m: e.scalar_tensor_tensor(
                            out=xt[:, m, :n], in0=mo[:, m, :n], scalar=modT[:, l, 2 * 8 + m, j:j + 1], in1=xt[:, m, :n],
                            op0=ALU.mult, op1=ALU.add), reads=[mo, modT, xt], writes=[xt])
                    layer_norm2(xt, n, l, 0, sg, mean_sb, rstd)
                    modulate(xt, hb, n, l, 3, 4, j)

                prologue(0)
                for ti, (l0, n) in enumerate(own512):
                    j = 1 if l0 == 0 else 0
                    for jb in range(NJ):
                        pg, pu = PB[(jb % 2) * 2], PB[(jb % 2) * 2 + 1]
                        for k in range(8):
                            P.op('pe', lambda e, k=k, jb=jb, pg=pg: e.matmul(
                                pg[:, :n], lhsT=win[:, k, jb * 128:(jb + 1) * 128], rhs=hb[:, k, :n],
                                start=(k == 0), stop=(k == 7)), reads=[win, hb], writes=[pg])
                        for k in range(8):
                            P.op('pe', lambda e, k=k, jb=jb, pu=pu: e.matmul(
                                pu[:, :n], lhsT=win[:, k, FFH + jb * 128:FFH + (jb + 1) * 128], rhs=hb[:, k, :n],
                                start=(k == 0), stop=(k == 7)), reads=[win, hb], writes=[pu])
                        sgt = sg[jb % 2]
                        P.op('act', lambda e, pg=pg, sgt=sgt: e.activation(out=sgt[:, :n], in_=pg[:, :n], func=AF.Silu),
                             reads=[pg], writes=[sgt])
                        P.op('dve', lambda e, pu=pu, sgt=sgt, jb=jb: e.tensor_tensor(
                            out=actb[:, jb, :n], in0=pu[:, :n], in1=sgt[:, :n], op=ALU.mult),
                            reads=[pu, sgt], writes=[actb])
                    P.op('act', lambda e: e.activation(out=x1a[:, :, :n], in_=xt[:, :, :n], func=AF.Copy, scale=ALPHA),
                         reads=[xt], writes=[x1a])
                    if ti + 1 < len(own512):
                        prologue(ti + 1)
                    for m in range(8):
                        po = PB[4 + m % 2]
                        for jb in range(NJ):
                            P.op('pe', lambda e, m=m, jb=jb, po=po: e.matmul(
                                po[:, :n], lhsT=wout[:, jb, m * 128:(m + 1) * 128], rhs=actb[:, jb, :n],
                                start=(jb == 0), stop=(jb == NJ - 1)), reads=[wout, actb], writes=[po])
                        P.op('dve', lambda e, m=m, po=po: e.scalar_tensor_tensor(
                            out=x1a[:, m, :n], in0=po[:, :n], scalar=modT[:, l, 5 * 8 + m, j:j + 1], in1=x1a[:, m, :n],
                            op0=ALU.mult, op1=ALU.add), reads=[po, modT, x1a], writes=[x1a])
                    layer_norm2(x1a, n, l, 1, sg, mean_sb, rstd)
                    P.dma('act', fm(Xdst)[:, :, l0:l0 + n], x1a[:, :, :n], reads=[x1a])

        def load_h(ht, t, l0, n):
            P.dma('sp', ht[:, :, :n], Hg[:, t].rearrange("k p s -> p k s")[:, :, l0:l0 + n], writes=[ht])

        def even_proj_phase(ee):
            with Phase() as ph:
                w = load_w(ph, ev_w_in[ee], D, EVC)
                hb = [ph.sb([128, 8, 512], BF16) for _ in range(2)]
                st_gq = [ph.sb([128, 512], BF16) for _ in range(2)]
                st_gk = [ph.sb([128, 512], BF16) for _ in range(2)]
                st_g = [ph.sb([128, 2, 512], BF16) for _ in range(2)]
                st_r = [ph.sb([32, 512], BF16) for _ in range(2)]
                st_dq = [ph.sb([128, 2, 512], BF16) for _ in range(2)]
                st_dk = [ph.sb([128, 2, 512], BF16) for _ in range(2)]
                st_t = [ph.sb([128, 4, 640], BF16) for _ in range(2)]
                cs = [ph.sb([128, 512], F32) for _ in range(2)]
                sn = [ph.sb([128, 512], F32) for _ in range(2)]
                qb = [ph.sb([128, 512], BF16) for _ in range(2)]
                t1 = [ph.sb([128, 512], F32) for _ in range(2)]
                t2 = [ph.sb([128, 512], F32) for _ in range(2)]
                blocks = ([('gq', 0, 0, 128), ('gk', 0, 128, 128)] + [('g', i, 512 + i * 128, 128) for i in range(2)]
                          + [('r', 0, 768, 32)] + [('dq', i, 800 + i * 128, 128) for i in range(2)]
                          + [('dk', i, 1056 + i * 128, 128) for i in range(2)])
                bi = 0
                ri = 0
                for ti, (t, l0, n) in enumerate(gtiles):
                    s0 = t * Sh + l0
                    isctx = s0 < CTX
                    b2 = ti % 2
                    ht = hb[b2]
                    load_h(ht, t, l0, n)
                    if not isctx:
                        P.dma('sp', cs[b2][:, :n], cos_d[:, s0 - CTX:s0 - CTX + n], writes=[cs[b2]])
                        P.dma('sp', sn[b2][:, :n], sin_d[:, s0 - CTX:s0 - CTX + n], writes=[sn[b2]])
                    for (kind, idx, c0, M) in blocks:
                        pb = PB[bi % 4]
                        bi += 1
                        for k in range(8):
                            P.op('pe', lambda e, k=k, c0=c0, M=M, pb=pb: e.matmul(
                                pb[0:M, :n], lhsT=w[:, k, c0:c0 + M], rhs=ht[:, k, :n], start=(k == 0), stop=(k == 7)),
                                reads=[w, ht], writes=[pb])
                        if kind == 'gq':
                            dst = st_gq[b2]
                            P.op('act', lambda e, dst=dst, pb=pb: e.activation(
                                out=dst[:, :n], in_=pb[:, :n], func=AF.Copy, scale=0.125), reads=[pb], writes=[dst])
                        elif kind == 'gk':
                            dst = st_gk[b2]
                            P.op('dve', lambda e, dst=dst, pb=pb: e.tensor_copy(out=dst[:, :n], in_=pb[:, :n]),
                                 reads=[pb], writes=[dst])
                        elif kind == 'g':
                            dst = st_g[b2]
                            P.op('act', lambda e, dst=dst, idx=idx, pb=pb: e.activation(
                                out=dst[:, idx, :n], in_=pb[:, :n], func=AF.Silu), reads=[pb], writes=[dst])
                        elif kind == 'r':
                            dst = st_r[b2]
                            P.op('dve', lambda e, dst=dst, pb=pb: e.tensor_copy(out=dst[:, :n], in_=pb[0:32, :n]),
                                 reads=[pb], writes=[dst])
                        else:
                            dst = (st_dq if kind == 'dq' else st_dk)[b2]
                            if isctx:
                                P.op('dve', lambda e, dst=dst, idx=idx, pb=pb: e.tensor_copy(out=dst[:, idx, :n], in_=pb[:, :n]),
                                     reads=[pb], writes=[dst])
                            else:
                                r2 = ri % 2
                                ri += 1
                                pp = PB[4 + r2]
                                P.op('act', lambda e, r2=r2, pb=pb: e.activation(out=qb[r2][:, :n], in_=pb[:, :n], func=AF.Copy),
                                     reads=[pb], writes=[qb[r2]])
                                P.op('pe', lambda e, r2=r2, pp=pp: e.matmul(pp[:, :n], lhsT=cmb[:, C_PERM, :], rhs=qb[r2][:, :n],
                                                                          start=True, stop=True), reads=[cmb, qb[r2]], writes=[pp])
                                P.op('pool', lambda e, r2=r2: e.tensor_tensor(out=t1[r2][:, :n], in0=qb[r2][:, :n], in1=cs[b2][:, :n],
                                                                             op=ALU.mult), reads=[qb[r2], cs[b2]], writes=[t1[r2]])
                                P.op('dve', lambda e, r2=r2, pp=pp: e.tensor_tensor(out=t2[r2][:, :n], in0=pp[:, :n], in1=sn[b2][:, :n],
                                                                                   op=ALU.mult), reads=[pp, sn[b2]], writes=[t2[r2]])
                                P.op('pool', lambda e, r2=r2, dst=dst, idx=idx: e.tensor_tensor(
                                    out=dst[:, idx, :n], in0=t1[r2][:, :n], in1=t2[r2][:, :n], op=ALU.add),
                                    reads=[t1[r2], t2[r2]], writes=[dst])
                    stt = st_t[b2]
                    for sub in range(n // 128):
                        for (c0, N, o0, pbi) in ((128, 128, 0, 6), (256, 256, 128, 7), (1312, 256, 384, 6)):
                            pb = PB[pbi]
                            for k in range(8):
                                P.op('pe', lambda e, k=k, c0=c0, N=N, pb=pb, sub=sub: e.matmul(
                                    pb[:, :N], lhsT=ht[:, k, sub * 128:(sub + 1) * 128], rhs=w[:, k, c0:c0 + N],
                                    start=(k == 0), stop=(k == 7)), reads=[w, ht], writes=[pb])
                            if pbi == 7:
                                P.op('act', lambda e, pb=pb, sub=sub, o0=o0, N=N: e.activation(
                                    out=stt[:, sub, o0:o0 + N], in_=pb[:, :N], func=AF.Copy), reads=[pb], writes=[stt])
                            else:
                                P.op('dve', lambda e, pb=pb, sub=sub, o0=o0, N=N: e.tensor_copy(
                                    out=stt[:, sub, o0:o0 + N], in_=pb[:, :N]), reads=[pb], writes=[stt])
                    P.dma('pool', QG[:, s0:s0 + n], st_gq[b2][:, :n], reads=[st_gq[b2]])
                    P.dma('pool', KG[:, s0:s0 + n], st_gk[b2][:, :n], reads=[st_gk[b2]])
                    P.dma('pool', GG.rearrange("(i p) s -> p i s", p=128)[:, :, s0:s0 + n], st_g[b2][:, :, :n], reads=[st_g[b2]])
                    P.dma('pool', RR[:, s0:s0 + n], st_r[b2][:, :n], reads=[st_r[b2]])
                    P.dma('pool', QD.rearrange("(i p) s -> p i s", p=128)[:, :, s0:s0 + n], st_dq[b2][:, :, :n], reads=[st_dq[b2]])
                    P.dma('pool', KD.rearrange("(i p) s -> p i s", p=128)[:, :, s0:s0 + n], st_dk[b2][:, :, :n], reads=[st_dk[b2]])
                    c0i, nsub = s0 // 128, n // 128
                    P.dma('pool', KGt.rearrange("(c p) f -> p c f", p=128)[:, c0i:c0i + nsub, :], stt[:, :nsub, 0:128], reads=[stt])
                    P.dma('pool', VGt.rearrange("(c p) f -> p c f", p=128)[:, c0i:c0i + nsub, :], stt[:, :nsub, 128:384], reads=[stt])
                    P.dma('pool', VDt.rearrange("(c p) f -> p c f", p=128)[:, c0i:c0i + nsub, :], stt[:, :nsub, 384:640], reads=[stt])

        def gla_phase(ee, d):
            order = list(range(NCH)) if d == 0 else [1, 0] + list(range(NCH - 1, 1, -1))
            TRI, STRI, MASK = (C_TRIF, C_STRIF, C_MASKF) if d == 0 else (C_TRIB, C_STRIB, C_MASKB)
            li = 127 if d == 0 else 0
            H = 2
            with Phase() as ph:
                cmf = load_cmf(ph)
                wg = ph.sb([17, 128], BF16)
                P.dma('pool', wg[:], gla_wg[ee, :, d, :], writes=[wg])
                mask4 = ph.sb([128, H, 128], F32)
                P.op('dve', lambda e: e.tensor_copy(out=mask4[:], in_=cmf[:, MASK:MASK + 1, :].to_broadcast([128, H, 128])),
                     reads=[cmf], writes=[mask4])
                Sf = ph.sb([64, H, 128], F32)
                Sb = ph.sb([64, H, 128], BF16)
                P.op('dve', lambda e: e.memset(Sf[:], 0.0), writes=[Sf])
                P.op('dve', lambda e: e.memset(Sb[:], 0.0), writes=[Sb])
                NB = 3
                rt = [ph.sb([17, 128], BF16) for _ in range(NB)]
                for tt in rt:
                    P.op('dve', lambda e, tt=tt: e.memset(tt[:], 1.0), writes=[tt])
                qT = [ph.sb([64, H, 128], BF16) for _ in range(NB)]
                kT = [ph.sb([64, H, 128], BF16) for _ in range(NB)]
                ktok = [ph.sb([128, 128], BF16) for _ in range(NB)]
                vtok = [ph.sb([128, 256], BF16) for _ in range(NB)]
                ofw = [ph.sb([128, H, 128], F32) for _ in range(NB)]
                ez = [ph.sb([128, 128], F32) for _ in range(NB)]
                spb = [ph.sb([128, 128], BF16) for _ in range(NB)]
                Ep = [ph.sb([64, H, 128], F32) for _ in range(NB)]
                Em = [ph.sb([64, H, 128], F32) for _ in range(NB)]
                Ee = [ph.sb([128, 128], F32) for _ in range(NB)]
                qd = [ph.sb([64, H, 128], BF16) for _ in range(NB)]
                kd = [ph.sb([64, H, 128], BF16) for _ in range(NB)]
                kl = [ph.sb([128, 128], BF16) for _ in range(NB)]
                attm = [ph.sb([128, H, 128], BF16) for _ in range(NB)]
                ot = [ph.sb([128, H, 128], F32) for _ in range(NB)]
                OGv = OG.rearrange("(h p) s -> p h s", p=128)
                def stage_a(ci):
                    c = order[ci]
                    b = ci % NB
                    p0 = (ci % 2) * 4
                    tok = slice(c * 128, (c + 1) * 128)
                    P.dma('sp', rt[b][0:16, :], RR[16 * d:16 * d + 16, tok], writes=[rt[b]])
                    P.dma('sp', qT[b][:], QG.rearrange("(h p) s -> p h s", p=64)[:, :, tok], writes=[qT[b]])
                    P.dma('sp', kT[b][:], KG.rearrange("(h p) s -> p h s", p=64)[:, :, tok], writes=[kT[b]])
                    P.dma('sp', ktok[b][:], KGt[tok, :], writes=[ktok[b]])
                    P.dma('sp', vtok[b][:], VGt[tok, :], writes=[vtok[b]])
                    if d == 1:
                        P.dma('sp', ofw[b][:], OGv[:, :, tok], writes=[ofw[b]])
                    pz, pbt, patt, po = PB[p0], PB[p0 + 1], PB[p0 + 2], PB[p0 + 3]
                    P.op('pe', lambda e, b=b: e.matmul(pz[:, 0:128], lhsT=rt[b][:], rhs=wg[:], start=True, stop=True),
                         reads=[rt[b], wg], writes=[pz])
                    P.op('act', lambda e, b=b: e.activation(out=ez[b][:], in_=pz[:, 0:128], func=AF.Exp, scale=-1.0),
                         reads=[pz], writes=[ez[b]])
                    P.op('act', lambda e, b=b: e.activation(out=spb[b][:], in_=ez[b][:], func=AF.Ln, bias=1.0, scale=1.0),
                         reads=[ez[b]], writes=[spb[b]])
                    for h in range(H):
                        P.op('pe', lambda e, b=b, h=h: e.matmul(pbt[0:64, h * 128:(h + 1) * 128], lhsT=spb[b][:, h * 64:(h + 1) * 64],
                                                               rhs=cmb[:, TRI, :], start=True, stop=True),
                             reads=[spb[b], cmb], writes=[pbt])
                    P.op('pe', lambda e, b=b: e.matmul(pz[:, 128:256], lhsT=cmb[:, STRI, :], rhs=spb[b][:], start=True, stop=True),
                         reads=[spb[b], cmb], writes=[pz])
                    pb1v = pbt[0:64, 0:H * 128].rearrange("p (h t) -> p h t", h=H)
                    P.op('act', lambda e, b=b: e.activation(out=Ep[b][:], in_=pb1v, func=AF.Exp), reads=[pbt], writes=[Ep[b]])
                    P.op('act', lambda e, b=b: e.activation(out=Em[b][:], in_=pb1v, func=AF.Exp, scale=-1.0),
                         reads=[pbt], writes=[Em[b]])
                    P.op('act', lambda e, b=b: e.activation(out=Ee[b][:], in_=pz[:, 128:256], func=AF.Exp),
                         reads=[pz], writes=[Ee[b]])
                    P.op('dve', lambda e, b=b: e.tensor_tensor(out=qd[b][:], in0=Ep[b][:], in1=qT[b][:], op=ALU.mult),
                         reads=[Ep[b], qT[b]], writes=[qd[b]])
                    P.op('dve', lambda e, b=b: e.tensor_tensor(out=kd[b][:], in0=Em[b][:], in1=kT[b][:], op=ALU.mult),
                         reads=[Em[b], kT[b]], writes=[kd[b]])
                    P.op('pool', lambda e, b=b: e.tensor_tensor(out=kl[b][:], in0=Ee[b][:], in1=ktok[b][:], op=ALU.mult),
                         reads=[Ee[b], ktok[b]], writes=[kl[b]])
                    for h in range(H):
                        P.op('pe', lambda e, b=b, h=h: e.matmul(patt[:, h * 128:(h + 1) * 128], lhsT=kd[b][:, h, :], rhs=qd[b][:, h, :],
                                                               start=True, stop=True), reads=[kd[b], qd[b]], writes=[patt])
                    P.op('dve', lambda e, b=b: e.tensor_tensor(out=attm[b][:], in0=patt[:, 0:H * 128].rearrange("p (h t) -> p h t", h=H),
                                                              in1=mask4[:], op=ALU.mult), reads=[patt, mask4], writes=[attm[b]])

                def stage_b(ci):
                    c = order[ci]
                    b = ci % NB
                    p0 = (ci % 2) * 4
                    tok = slice(c * 128, (c + 1) * 128)
                    pz, po = PB[p0], PB[p0 + 3]
                    for h in range(H):
                        P.op('pe', lambda e, b=b, h=h: e.matmul(po[:, h * 128:(h + 1) * 128], lhsT=vtok[b][:, h * 128:(h + 1) * 128],
                                                               rhs=attm[b][:, h, :], start=True, stop=False),
                             reads=[vtok[b], attm[b]], writes=[po])
                        P.op('pe', lambda e, b=b, h=h: e.matmul(po[:, h * 128:(h + 1) * 128], lhsT=Sb[:, h, :],
                                                               rhs=qd[b][:, h, :], start=False, stop=True),
                             reads=[Sb, qd[b]], writes=[po])
                    pb4v = po[:, 0:H * 128].rearrange("p (h t) -> p h t", h=H)
                    if d == 0:
                        P.op('act', lambda e, b=b: e.activation(out=ot[b][:], in_=pb4v, func=AF.Copy), reads=[po], writes=[ot[b]])
                    else:
                        P.op('pool' if False else 'dve', lambda e, b=b: e.tensor_tensor(out=ot[b][:], in0=pb4v, in1=ofw[b][:], op=ALU.add),
                             reads=[po, ofw[b]], writes=[ot[b]])
                    P.dma('act', OGv[:, :, tok], ot[b][:], reads=[ot[b]])
                    for h in range(H):
                        P.op('pe', lambda e, b=b, h=h: e.matmul(pz[0:64, 256 + h * 128:256 + (h + 1) * 128], lhsT=kl[b][:, h * 64:(h + 1) * 64],
                                                               rhs=vtok[b][:, h * 128:(h + 1) * 128], start=True, stop=True),
                             reads=[kl[b], vtok[b]], writes=[pz])
                    P.op('dve', lambda e, b=b: e.tensor_tensor(out=Sf[:], in0=Sf[:],
                                                              in1=Ep[b][:, :, li:li + 1].to_broadcast([64, H, 128]), op=ALU.mult),
                         reads=[Sf, Ep[b]], writes=[Sf])
                    P.op('dve', lambda e: e.tensor_tensor(out=Sf[:], in0=Sf[:], in1=pz[0:64, 256:512].rearrange("p (h t) -> p h t", h=H),
                                                          op=ALU.add), reads=[Sf, pz], writes=[Sf])
                    P.op('act', lambda e: e.activation(out=Sb[:], in_=Sf[:], func=AF.Copy), reads=[Sf], writes=[Sb])

                stage_a(0)
                for ci in range(len(order)):
                    if ci + 1 < len(order):
                        stage_a(ci + 1)
                    stage_b(ci)

        def attn_phase(ee):
            with Phase() as ph:
                cmf = load_cmf(ph)
                kTh = [ph.sb([128, S], BF16) for _ in range(2)]
                vh = [ph.sb([128, NCH, 128], BF16) for _ in range(2)]
                qTt = [ph.sb([128, 512], BF16) for _ in range(2)]
                pt = [ph.sb([128, 2, 512], BF16) for _ in range(3)]
                acc = [ph.sb([128, 512], F32) for _ in range(2)]
                r1 = ph.sb([128, 512], F32)
                r2 = ph.sb([128, 512], F32)
                o1 = ph.sb([128, 512], F32)
                osq = ph.sb([128, 512], BF16)
                dto = [ph.sb([128, 512], BF16) for _ in range(2)]
                SB = [(0, 1), (2, 3)]
                qi = 0
                for h in range(2):
                    P.dma('sp', kTh[h][:], KD[h * 128:(h + 1) * 128, :], writes=[kTh[h]])
                    P.dma('act', vh[h][:], VDt.rearrange("(c p) f -> p c f", p=128)[:, :, h * 128:(h + 1) * 128], writes=[vh[h]])
                qtiles = [(h, t, l0, n) for h in range(2) for (t, l0, n) in gtiles]

                def issue_q(i):
                    h_, t_, l0_, n_ = qtiles[i]
                    s0_ = t_ * Sh + l0_
                    P.dma('sp', qTt[i % 2][:, :n_], QD[h_ * 128:(h_ + 1) * 128, s0_:s0_ + n_], writes=[qTt[i % 2]])

                issue_q(0)
                for qi, (h, t, l0, n) in enumerate(qtiles):
                    if True:
                        kT_, v_ = kTh[h], vh[h]
                        s0 = t * Sh + l0
                        q2 = qi % 2
                        qt = qTt[q2]
                        if qi + 1 < len(qtiles):
                            issue_q(qi + 1)
                        nkb = 2 if s0 < CTX else NCH

                        def smm(kb):
                            for m in range(2):
                                pb = PB[SB[kb % 2][m]]
                                P.op('pe', lambda e, m=m, pb=pb, kb=kb: e.matmul(
                                    pb[:, :n], lhsT=kT_[m * 64:(m + 1) * 64, kb * 128:(kb + 1) * 128],
                                    rhs=qt[m * 64:(m + 1) * 64, :n], start=True, stop=True), reads=[kT_, qt], writes=[pb])
                        smm(0)
                        if nkb > 1:
                            smm(1)
                        for kb in range(nkb):
                            p3 = kb % 3
                            b0 = SB[kb % 2][0]
                            P.op('act', lambda e, b0=b0, p3=p3: e.activation(
                                out=pt[p3][:, :, :n], in_=psum[:, b0:b0 + 2, :n], func=AF.Exp, scale=0.125),
                                reads=[PB[b0], PB[b0 + 1]], writes=[pt[p3]])
                            if kb + 2 < nkb:
                                smm(kb + 2)
                            for m in range(2):
                                P.op('pe', lambda e, m=m, p3=p3, kb=kb: e.matmul(
                                    PB[4 + m][:, :n], lhsT=v_[:, kb, :], rhs=pt[p3][:, m, :n],
                                    start=(kb == 0), stop=(kb == nkb - 1)), reads=[v_, pt[p3]], writes=[PB[4 + m]])
                            P.op('pe', lambda e, p3=p3, kb=kb: e.matmul(
                                PB[7][:, :n], lhsT=cmb[:, C_ONES, :], rhs=pt[p3][:, 0, :n],
                                start=(kb == 0), stop=(kb == nkb - 1)), reads=[cmb, pt[p3]], writes=[PB[7]])
                            hn = n // 2
                            for ai, (eng, c0, c1) in enumerate((('dve', 0, hn), ('pool', hn, n))):
                                if kb == 0:
                                    P.op(eng, lambda e, p3=p3, c0=c0, c1=c1, ai=ai: e.tensor_copy(out=acc[ai][:, c0:c1], in_=pt[p3][:, 1, c0:c1]),
                                         reads=[pt[p3]], writes=[acc[ai]])
                                else:
                                    P.op(eng, lambda e, p3=p3, c0=c0, c1=c1, ai=ai: e.tensor_tensor(
                                        out=acc[ai][:, c0:c1], in0=acc[ai][:, c0:c1], in1=pt[p3][:, 1, c0:c1], op=ALU.add),
                                        reads=[pt[p3], acc[ai]], writes=[acc[ai]])
                        for ai, (c0, c1) in enumerate(((0, n // 2), (n // 2, n))):
                            P.op('pe', lambda e, ai=ai, c0=c0, c1=c1: e.matmul(PB[6][:, c0:c1], lhsT=cmf[:, C_ONES, :], rhs=acc[ai][:, c0:c1],
                                                                              start=True, stop=True), reads=[cmf, acc[ai]], writes=[PB[6]])
                        P.op('dve', lambda e: e.reciprocal(out=r1[:, :n], in_=PB[7][:, :n]), reads=[PB[7]], writes=[r1])
                        P.op('dve', lambda e: e.reciprocal(out=r2[:, :n], in_=PB[6][:, :n]), reads=[PB[6]], writes=[r2])
                        P.op('dve', lambda e: e.tensor_tensor(out=r1[:, :n], in0=PB[4][:, :n], in1=r1[:, :n], op=ALU.mult),
                             reads=[PB[4], r1], writes=[r1])
                        P.op('dve', lambda e: e.tensor_tensor(out=r2[:, :n], in0=PB[5][:, :n], in1=r2[:, :n], op=ALU.mult),
                             reads=[PB[5], r2], writes=[r2])
                        P.op('dve', lambda e: e.scalar_tensor_tensor(out=o1[:, :n], in0=r2[:, :n], scalar=nlam[:, ee:ee + 1],
                                                                     in1=r1[:, :n], op0=ALU.mult, op1=ALU.add),
                             reads=[r1, r2, nlam], writes=[o1])
                        P.op('pool', lambda e: e.tensor_tensor(out=osq[:, :n], in0=o1[:, :n], in1=o1[:, :n], op=ALU.mult),
                             reads=[o1], writes=[osq])
                        P.op('pe', lambda e: e.matmul(PB[6][:, :n], lhsT=cmb[:, C_ONES, :], rhs=osq[:, :n], start=True, stop=True),
                             reads=[cmb, osq], writes=[PB[6]])
                        P.op('act', lambda e: e.activation(out=r1[:, :n], in_=PB[6][:, :n], func=AF.Sqrt, bias=RMS_EPS, scale=1.0 / 128),
                             reads=[PB[6]], writes=[r1])
                        P.op('dve', lambda e: e.reciprocal(out=r1[:, :n], in_=r1[:, :n]), reads=[r1], writes=[r1])
                        dd = dto[q2]
                        P.op('dve', lambda e, dd=dd: e.scalar_tensor_tensor(out=dd[:, :n], in0=o1[:, :n], scalar=dnws[:, ee:ee + 1],
                                                                            in1=r1[:, :n], op0=ALU.mult, op1=ALU.mult),
                             reads=[o1, r1, dnws], writes=[dd])
                        P.dma('sp', DT[h * 128:(h + 1) * 128, s0:s0 + n], dd[:, :n], reads=[dd])

        def even_out_phase(ee):
            with Phase() as ph:
                wo = load_w(ph, ev_w_out[ee], 512, D)
                og = [ph.sb([128, 2, 512], F32) for _ in range(2)]
                gg = [ph.sb([128, 2, 512], BF16) for _ in range(2)]
                mix = [ph.sb([128, 4, 512], BF16) for _ in range(2)]
                sq = ph.sb([128, 2, 512], BF16)
                rs = ph.sb([128, 2, 512], F32)
                ob = [ph.sb([128, 8, 512], BF16) for _ in range(2)]
                for ti, (c, t, l0, n) in enumerate(ctiles):
                    s0 = t * Sh + l0
                    b2 = ti % 2
                    P.dma('sp', og[b2][:, :, :n], OG.rearrange("(h p) s -> p h s", p=128)[:, :, s0:s0 + n], writes=[og[b2]])
                    P.dma('sp', gg[b2][:, :, :n], GG.rearrange("(h p) s -> p h s", p=128)[:, :, s0:s0 + n], writes=[gg[b2]])
                    P.dma('sp', mix[b2][:, 2:4, :n], DT.rearrange("(h p) s -> p h s", p=128)[:, :, s0:s0 + n], writes=[mix[b2]])
                    P.op('dve', lambda e, b2=b2: e.tensor_tensor(out=sq[:, :, :n], in0=og[b2][:, :, :n], in1=og[b2][:, :, :n], op=ALU.mult),
                         reads=[og[b2]], writes=[sq])
                    for h in range(2):
                        P.op('pe', lambda e, h=h: e.matmul(PB[h][:, :n], lhsT=cmb[:, C_ONES, :], rhs=sq[:, h, :n], start=True, stop=True),
                             reads=[cmb, sq], writes=[PB[h]])
                    P.op('act', lambda e: e.activation(out=rs[:, :, :n], in_=psum[:, 0:2, :n], func=AF.Sqrt, bias=RMS_EPS, scale=1.0 / 128),
                         reads=[PB[0], PB[1]], writes=[rs])
                    P.op('dve', lambda e: e.reciprocal(out=rs[:, :, :n], in_=rs[:, :, :n]), reads=[rs], writes=[rs])
                    P.op('dve', lambda e, b2=b2: e.tensor_tensor(out=rs[:, :, :n], in0=rs[:, :, :n], in1=og[b2][:, :, :n], op=ALU.mult),
                         reads=[rs, og[b2]], writes=[rs])
                    P.op('dve', lambda e, b2=b2: e.scalar_tensor_tensor(out=mix[b2][:, 0:2, :n], in0=rs[:, :, :n], scalar=gnw[:, ee:ee + 1],
                                                                       in1=gg[b2][:, :, :n], op0=ALU.mult, op1=ALU.mult),
                         reads=[rs, gnw, gg[b2]], writes=[mix[b2]])
                    o_ = ob[b2]
                    for m in range(8):
                        po = PB[4 + m % 4]
                        for k in range(4):
                            P.op('pe', lambda e, m=m, k=k, po=po, b2=b2: e.matmul(
                                po[:, :n], lhsT=wo[:, k, m * 128:(m + 1) * 128], rhs=mix[b2][:, k, :n],
                                start=(k == 0), stop=(k == 3)), reads=[wo, mix[b2]], writes=[po])
                        if m % 2 == 0:
                            P.op('act', lambda e, m=m, po=po: e.activation(out=o_[:, m, :n], in_=po[:, :n], func=AF.Copy),
                                 reads=[po], writes=[o_])
                        else:
                            P.op('dve', lambda e, m=m, po=po: e.tensor_copy(out=o_[:, m, :n], in_=po[:, :n]), reads=[po], writes=[o_])
                    P.dma('act', Ppc[c][t].rearrange("k p s -> p k s"), o_[:, :, :n], reads=[o_], writes=[ppres[c]])
                    if t == 1:
                        rs_chunk(c)

        def odd_proj_phase(oo):
            with Phase() as ph:
                w = load_w(ph, ssd_w_in[oo], D, SDC)
                hb = [ph.sb([128, 8, 512], BF16) for _ in range(2)]
                st_x = [ph.sb([128, 12, 512], BF16) for _ in range(2)]
                st_z = [ph.sb([128, 4, 1024], BF16) for _ in range(2)]
                st_dt = [ph.sb([128, 4, 32], F32) for _ in range(2)]
                dtb = ph.sb([128, 32], F32)
                P.dma('sp', dtb[:], ssd_dtb[oo], writes=[dtb])
                bi = 0
                for ti, (t, l0, n) in enumerate(gtiles):
                    s0 = t * Sh + l0
                    b2 = ti % 2
                    ht = hb[b2]
                    stx, stz, stdt = st_x[b2], st_z[b2], st_dt[b2]
                    load_h(ht, t, l0, n)
                    for i in range(12):
                        pb = PB[bi % 4]
                        bi += 1
                        for k in range(8):
                            P.op('pe', lambda e, k=k, i=i, pb=pb: e.matmul(
                                pb[:, :n], lhsT=w[:, k, 1024 + i * 128:1024 + (i + 1) * 128], rhs=ht[:, k, :n],
                                start=(k == 0), stop=(k == 7)), reads=[w, ht], writes=[pb])
                        if i % 2 == 0:
                            P.op('act', lambda e, i=i, pb=pb: e.activation(out=stx[:, i, :n], in_=pb[:, :n], func=AF.Copy),
                                 reads=[pb], writes=[stx])
                        else:
                            P.op('dve', lambda e, i=i, pb=pb: e.tensor_copy(out=stx[:, i, :n], in_=pb[:, :n]),
                                 reads=[pb], writes=[stx])
                    nsub = n // 128
                    for sub in range(nsub):
                        for zb in range(2):
                            pb = PB[4 + zb % 2]
                            for k in range(8):
                                P.op('pe', lambda e, k=k, zb=zb, pb=pb, sub=sub: e.matmul(
                                    pb[:, :512], lhsT=ht[:, k, sub * 128:(sub + 1) * 128], rhs=w[:, k, zb * 512:(zb + 1) * 512],
                                    start=(k == 0), stop=(k == 7)), reads=[w, ht], writes=[pb])
                            P.op('act', lambda e, zb=zb, pb=pb, sub=sub: e.activation(
                                out=stz[:, sub, zb * 512:(zb + 1) * 512], in_=pb[:, :512], func=AF.Silu), reads=[pb], writes=[stz])
                        for k in range(8):
                            P.op('pe', lambda e, k=k, sub=sub: e.matmul(
                                PB[6][:, 0:32], lhsT=ht[:, k, sub * 128:(sub + 1) * 128], rhs=w[:, k, 2560:2592],
                                start=(k == 0), stop=(k == 7)), reads=[w, ht], writes=[PB[6]])
                        P.op('dve', lambda e, sub=sub: e.tensor_tensor(out=stdt[:, sub, :], in0=PB[6][:, 0:32], in1=dtb[:], op=ALU.add),
                             reads=[PB[6], dtb], writes=[stdt])
                    P.op('act', lambda e: e.activation(out=stdt[:, :nsub, :], in_=stdt[:, :nsub, :], func=AF.Exp),
                         reads=[stdt], writes=[stdt])
                    P.op('act', lambda e: e.activation(out=stdt[:, :nsub, :], in_=stdt[:, :nsub, :], func=AF.Ln, bias=1.0, scale=1.0),
                         reads=[stdt], writes=[stdt])
                    c0i = s0 // 128
                    P.dma('pool', XBC.rearrange("(i p) s -> p i s", p=128)[:, :, s0:s0 + n], stx[:, :, :n], reads=[stx])
                    P.dma('pool', ZS.rearrange("(c p) f -> p c f", p=128)[:, c0i:c0i + nsub, :], stz[:, :nsub, :], reads=[stz])
                    P.dma('pool', DTt.rearrange("(c p) f -> p c f", p=128)[:, c0i:c0i + nsub, :], stdt[:, :nsub, :], reads=[stdt])

        def conv_phase(oo):
            NBK = 12
            with Phase() as ph:
                cmf = load_cmf(ph)
                cw = ph.sb([128, 5, NBK], F32)
                cb = ph.sb([128, NBK], F32)
                P.dma('sp', cw[:], ssd_cw[oo], writes=[cw])
                P.dma('sp', cb[:], ssd_cb[oo], writes=[cb])
                Dg = ph.sb([128, NBK, 5, 128], BF16)
                for i in range(NBK):
                    for jj in range(5):
                        P.op('dve' if (i + jj) % 2 == 0 else 'pool', lambda e, i=i, jj=jj: e.tensor_scalar_mul(
                            out=Dg[:, i, jj, :], in0=cmf[:, C_IDENT, :], scalar1=cw[:, jj, i:i + 1]), reads=[cmf, cw], writes=[Dg])
                xp = [ph.sb([128, NBK, 516], BF16) for _ in range(2)]
                xc = [ph.sb([128, NBK, 512], BF16) for _ in range(2)]
                stt = [ph.sb([128, 4, 1280], BF16) for _ in range(2)]
                bi = 0
                for ti, (t, l0, n) in enumerate(gtiles):
                    s0 = t * Sh + l0
                    b2 = ti % 2
                    x_p, x_c = xp[b2], xc[b2]
                    lo = 0 if (s0 == 0 or s0 == CTX) else 2
                    hi = 0 if (s0 + n == CTX or s0 + n == S) else 2
                    if lo == 0:
                        P.op('pool', lambda e, x_p=x_p: e.memset(x_p[:, :, 0:2], 0.0), writes=[x_p])
                    if hi == 0:
                        P.op('pool', lambda e, x_p=x_p: e.memset(x_p[:, :, n + 2:n + 4], 0.0), writes=[x_p])
                    P.dma('sp', x_p[:, :, 2 - lo:n + 2 + hi], XBC.rearrange("(i p) s -> p i s", p=128)[:, :, s0 - lo:s0 + n + hi],
                          writes=[x_p])
                    for i in range(NBK):
                        pb = PB[bi % 4]
                        bi += 1
                        for jj in range(5):
                            P.op('pe', lambda e, i=i, jj=jj, pb=pb: e.matmul(
                                pb[:, :n], lhsT=Dg[:, i, jj, :], rhs=x_p[:, i, jj:jj + n], start=(jj == 0), stop=(jj == 4)),
                                reads=[Dg, x_p], writes=[pb])
                        P.op('act', lambda e, i=i, pb=pb: e.activation(out=x_c[:, i, :n], in_=pb[:, :n], func=AF.Silu,
                                                                      bias=cb[:, i:i + 1], scale=1.0), reads=[pb, cb], writes=[x_c])
                    P.dma('pool', BTd.rearrange("(i p) s -> p i s", p=128)[:, :, s0:s0 + n], x_c[:, 8:10, :n], reads=[x_c])
                    P.dma('pool', CTd.rearrange("(i p) s -> p i s", p=128)[:, :, s0:s0 + n], x_c[:, 10:12, :n], reads=[x_c])
                    st = stt[b2]
                    nsub = n // 128
                    qq = 0
                    for sub in range(nsub):
                        for (i0, ni) in ((0, 4), (4, 4), (8, 2)):
                            pb = PB[4 + qq % 4]
                            qq += 1
                            for u in range(ni):
                                i = i0 + u
                                P.op('pe', lambda e, i=i, u=u, pb=pb, sub=sub: e.matmul(
                                    pb[:, u * 128:(u + 1) * 128], lhsT=x_c[:, i, sub * 128:(sub + 1) * 128], rhs=cmb[:, C_IDENT, :],
                                    start=True, stop=True), reads=[x_c, cmb], writes=[pb])
                            if qq % 2 == 0:
                                P.op('dve', lambda e, i0=i0, ni=ni, pb=pb, sub=sub: e.tensor_copy(
                                    out=st[:, sub, i0 * 128:(i0 + ni) * 128], in_=pb[:, 0:ni * 128]), reads=[pb], writes=[st])
                            else:
                                P.op('act', lambda e, i0=i0, ni=ni, pb=pb, sub=sub: e.activation(
                                    out=st[:, sub, i0 * 128:(i0 + ni) * 128], in_=pb[:, 0:ni * 128], func=AF.Copy), reads=[pb], writes=[st])
                    c0i = s0 // 128
                    P.dma('pool', XSt.rearrange("(c p) f -> p c f", p=128)[:, c0i:c0i + nsub, :], st[:, :nsub, 0:1024], reads=[st])
                    P.dma('pool', Btd.rearrange("(c p) f -> p c f", p=128)[:, c0i:c0i + nsub, :], st[:, :nsub, 1024:1280], reads=[st])

        def ssd_phase(oo, d):
            order = list(range(NCH)) if d == 0 else [1, 0] + list(range(NCH - 1, 1, -1))
            U, SL, MASK = (C_UF, C_SLF, C_MASKF) if d == 0 else (C_UB, C_SLB, C_MASKB)
            NH, NG = 16, 2
            with Phase() as ph:
                cmf = load_cmf(ph)
                abc = ph.sb([128, 32], F32)
                P.dma('sp', abc[:], ssd_alog[oo], writes=[abc])
                P.op('act', lambda e: e.activation(out=abc[:], in_=abc[:], func=AF.Exp), reads=[abc], writes=[abc])
                P.op('dve', lambda e: e.tensor_scalar_mul(out=abc[:], in0=abc[:], scalar1=-1.0), reads=[abc], writes=[abc])
                dsk = ph.sb([128, NH], F32)
                nwb = ph.sb([128, 1024], F32)
                if d == 1:
                    P.dma('sp', dsk[:], ssd_dsk[oo], writes=[dsk])
                    P.dma('sp', nwb[:], ssd_nw[oo], writes=[nwb])
                mask4 = ph.sb([128, NG, 128], F32)
                P.op('dve', lambda e: e.tensor_copy(out=mask4[:], in_=cmf[:, MASK:MASK + 1, :].to_broadcast([128, NG, 128])),
                     reads=[cmf], writes=[mask4])
                Sf = [ph.sb([128, 8, 64], F32) for _ in range(NG)]
                Sb = [ph.sb([128, 8, 64], BF16) for _ in range(NG)]
                for g in range(NG):
                    P.op('dve', lambda e, g=g: e.memset(Sf[g][:], 0.0), writes=[Sf[g]])
                    P.op('pool', lambda e, g=g: e.memset(Sb[g][:], 0.0), writes=[Sb[g]])
                NB = 3
                xtok = [ph.sb([128, NH, 64], BF16) for _ in range(NB)]
                btok = [ph.sb([128, 256], BF16) for _ in range(NB)]
                bT = [ph.sb([128, NG, 128], BF16) for _ in range(NB)]
                cT = [ph.sb([128, NG, 128], BF16) for _ in range(NB)]
                dtt = [ph.sb([128, 32], F32) for _ in range(NB)]
                dA = [ph.sb([128, NH], F32) for _ in range(NB)]
                ex = [ph.sb([128, 3 * NH], F32) for _ in range(NB)]
                wv = [ph.sb([128, NH], F32) for _ in range(NB)]
                xdt = [ph.sb([128, NH, 64], BF16) for _ in range(NB)]
                xw = [ph.sb([128, NH, 64], BF16) for _ in range(NB)]
                rhsD = [ph.sb([128, NH, 128], BF16) for _ in range(NB)]
                cbm = [ph.sb([128, NG, 128], F32) for _ in range(NB)]
                expD = [ph.sb([128, 8, 128], F32) for _ in range(2)]
                G = [ph.sb([128, 8, 128], BF16) for _ in range(2)]
                tmp = [ph.sb([128, 8, 64], F32) for _ in range(2)]
                y = [ph.sb([128, NH, 64], F32) for _ in range(NB)]
                if d == 1:
                    yfw = [ph.sb([128, NH, 64], F32) for _ in range(NB)]
                    zs = [ph.sb([128, 1024], BF16) for _ in range(NB)]
                    t3 = ph.sb([128, NH, 64], F32)
                    sq = ph.sb([128, NG, 512], F32)
                    ss = ph.sb([128, NG], F32)
                    yn = ph.sb([128, 1024], BF16)
                    ynT = [ph.sb([128, 8, 128], BF16) for _ in range(2)]
                gi = [0]

                def stage_a(ci):
                    c = order[ci]
                    b = ci % NB
                    tok = slice(c * 128, (c + 1) * 128)
                    P.dma('sp', xtok[b][:], XSt[tok, :].rearrange("t (h p) -> t h p", p=64), writes=[xtok[b]])
                    P.dma('sp', btok[b][:], Btd[tok, :], writes=[btok[b]])
                    P.dma('sp', bT[b][:], BTd.rearrange("(g p) s -> p g s", p=128)[:, :, tok], writes=[bT[b]])
                    P.dma('sp', cT[b][:], CTd.rearrange("(g p) s -> p g s", p=128)[:, :, tok], writes=[cT[b]])
                    P.dma('sp', dtt[b][:], DTt[tok, :], writes=[dtt[b]])
                    if d == 1:
                        P.dma('sp', yfw[b][:], YF[tok, :].rearrange("t (h p) -> t h p", p=64), writes=[yfw[b]])
                        P.dma('sp', zs[b][:], ZS[tok, :], writes=[zs[b]])
                    dtd = dtt[b][:, NH * d:NH * d + NH]
                    P.op('dve', lambda e, b=b: e.tensor_tensor(out=dA[b][:], in0=dtd, in1=abc[:, NH * d:NH * d + NH], op=ALU.mult),
                         reads=[dtt[b], abc], writes=[dA[b]])
                    P.op('pe', lambda e, b=b: e.matmul(PB[0][:, 0:NH], lhsT=cmf[:, U, :], rhs=dA[b][:], start=True, stop=True),
                         reads=[cmf, dA[b]], writes=[PB[0]])
                    P.op('pe', lambda e, b=b: e.matmul(PB[0][:, NH:2 * NH], lhsT=cmf[:, SL, :], rhs=dA[b][:], start=True, stop=True),
                         reads=[cmf, dA[b]], writes=[PB[0]])
                    P.op('pe', lambda e, b=b: e.matmul(PB[0][:, 2 * NH:3 * NH], lhsT=cmf[:, C_ONES, :], rhs=dA[b][:], start=True, stop=True),
                         reads=[cmf, dA[b]], writes=[PB[0]])
                    P.op('act', lambda e, b=b: e.activation(out=ex[b][:], in_=PB[0][:, 0:3 * NH], func=AF.Exp), reads=[PB[0]], writes=[ex[b]])
                    P.op('dve', lambda e, b=b: e.tensor_tensor(out=wv[b][:], in0=dtd, in1=ex[b][:, NH:2 * NH], op=ALU.mult),
                         reads=[dtt[b], ex[b]], writes=[wv[b]])
                    P.op('pool', lambda e, b=b: e.tensor_tensor(out=xdt[b][:], in0=xtok[b][:], in1=dtd.unsqueeze(2).to_broadcast([128, NH, 64]),
                                                               op=ALU.mult), reads=[xtok[b], dtt[b]], writes=[xdt[b]])
                    P.op('dve', lambda e, b=b: e.tensor_tensor(out=xw[b][:], in0=xtok[b][:], in1=wv[b][:].unsqueeze(2).to_broadcast([128, NH, 64]),
                                                              op=ALU.mult), reads=[xtok[b], wv[b]], writes=[xw[b]])
                    for hh, eng in ((0, 'dve'), (8, 'pool')):
                        P.op(eng, lambda e, b=b, hh=hh: e.tensor_tensor(
                            out=rhsD[b][:, hh:hh + 8, :], in0=cmf[:, U:U + 1, :].to_broadcast([128, 8, 128]),
                            in1=dA[b][:, hh:hh + 8].unsqueeze(2).to_broadcast([128, 8, 128]), op=ALU.mult),
                            reads=[cmf, dA[b]], writes=[rhsD[b]])
                    for g in range(NG):
                        P.op('pe', lambda e, b=b, g=g: e.matmul(PB[1][:, g * 128:(g + 1) * 128], lhsT=bT[b][:, g, :], rhs=cT[b][:, g, :],
                                                               start=True, stop=True), reads=[bT[b], cT[b]], writes=[PB[1]])
                    P.op('dve', lambda e, b=b: e.tensor_tensor(out=cbm[b][:], in0=PB[1][:, 0:NG * 128].rearrange("p (g t) -> p g t", g=NG),
                                                              in1=mask4[:], op=ALU.mult), reads=[PB[1], mask4], writes=[cbm[b]])

                def stage_b(ci):
                    c = order[ci]
                    b = ci % NB
                    tok = slice(c * 128, (c + 1) * 128)
                    g2s = []
                    for g in range(NG):
                        g2 = gi[0] % 2
                        gi[0] += 1
                        g2s.append(g2)
                        d0 = 2 if g2 == 0 else 6
                        for k2 in range(2):
                            P.op('pe', lambda e, b=b, g=g, k2=k2, d0=d0: e.matmul(
                                PB[d0 + k2][:, :], lhsT=cmb[:, SL, :],
                                rhs=rhsD[b][:, g * 8 + k2 * 4:g * 8 + k2 * 4 + 4, :], start=True, stop=True),
                                reads=[cmb, rhsD[b]], writes=[PB[d0 + k2]])
                        P.op('act', lambda e, g2=g2, d0=d0: e.activation(out=expD[g2][:], in_=psum[:, d0:d0 + 2, :].rearrange("p a (h t) -> p (a h) t", h=4),
                                                                        func=AF.Exp), reads=[PB[d0], PB[d0 + 1]], writes=[expD[g2]])
                        P.op('pool' if g % 2 == 0 else 'dve', lambda e, b=b, g=g, g2=g2: e.tensor_tensor(
                            out=G[g2][:], in0=expD[g2][:], in1=cbm[b][:, g:g + 1, :].to_broadcast([128, 8, 128]), op=ALU.mult),
                            reads=[expD[g2], cbm[b]], writes=[G[g2]])
                    for g in range(NG):
                        g2 = g2s[g]
                        P.op('pe', lambda e, b=b, g=g: e.matmul(PB[4][:, :], lhsT=cT[b][:, g, :], rhs=Sb[g][:], start=True, stop=True),
                             reads=[cT[b], Sb[g]], writes=[PB[4]])
                        for h8 in range(8):
                            P.op('pe', lambda e, b=b, g=g, g2=g2, h8=h8: e.matmul(
                                PB[5][:, h8 * 64:(h8 + 1) * 64], lhsT=G[g2][:, h8, :], rhs=xdt[b][:, g * 8 + h8, :],
                                start=True, stop=True), reads=[G[g2], xdt[b]], writes=[PB[5]])
                        P.op('dve', lambda e, b=b, g=g, g2=g2: e.tensor_tensor(
                            out=tmp[g2][:], in0=PB[4][:, :].rearrange("p (h q) -> p h q", q=64),
                            in1=ex[b][:, g * 8:(g + 1) * 8].unsqueeze(2).to_broadcast([128, 8, 64]), op=ALU.mult),
                            reads=[PB[4], ex[b]], writes=[tmp[g2]])
                        P.op('dve', lambda e, b=b, g=g, g2=g2: e.tensor_tensor(
                            out=y[b][:, g * 8:(g + 1) * 8, :], in0=PB[5][:, :].rearrange("p (h q) -> p h q", q=64),
                            in1=tmp[g2][:], op=ALU.add), reads=[PB[5], tmp[g2]], writes=[y[b]])
                        P.op('pe', lambda e, b=b, g=g: e.matmul(PB[4][:, :], lhsT=btok[b][:, g * 128:(g + 1) * 128],
                                                               rhs=xw[b][:, g * 8:(g + 1) * 8, :], start=True, stop=True),
                             reads=[btok[b], xw[b]], writes=[PB[4]])
                        P.op('pool', lambda e, b=b, g=g: e.tensor_tensor(
                            out=Sf[g][:], in0=Sf[g][:], in1=ex[b][:, 2 * NH + g * 8:2 * NH + (g + 1) * 8].unsqueeze(2).to_broadcast([128, 8, 64]),
                            op=ALU.mult), reads=[Sf[g], ex[b]], writes=[Sf[g]])
                        P.op('dve', lambda e, g=g: e.tensor_tensor(out=Sf[g][:], in0=Sf[g][:],
                                                                  in1=PB[4][:, :].rearrange("p (h q) -> p h q", q=64), op=ALU.add),
                             reads=[Sf[g], PB[4]], writes=[Sf[g]])
                        P.op('act', lambda e, g=g: e.activation(out=Sb[g][:], in_=Sf[g][:], func=AF.Copy), reads=[Sf[g]], writes=[Sb[g]])
                    if d == 0:
                        P.dma('act', YF[tok, :].rearrange("t (h p) -> t h p", p=64), y[b][:], reads=[y[b]])
                    else:
                        yb = y[b]
                        P.op('dve', lambda e, b=b: e.tensor_tensor(out=yb[:], in0=yb[:], in1=yfw[b][:], op=ALU.add),
                             reads=[yb, yfw[b]], writes=[yb])
                        P.op('pool', lambda e, b=b: e.tensor_tensor(out=t3[:], in0=xtok[b][:],
                                                                   in1=dsk[:].unsqueeze(2).to_broadcast([128, NH, 64]), op=ALU.mult),
                             reads=[xtok[b], dsk], writes=[t3])
                        P.op('dve', lambda e: e.tensor_tensor(out=yb[:], in0=yb[:], in1=t3[:], op=ALU.add), reads=[yb, t3], writes=[yb])
                        ybf = yb[:].rearrange("p h q -> p (h q)")
                        P.op('pool', lambda e, b=b: e.tensor_tensor(out=ybf, in0=ybf, in1=zs[b][:], op=ALU.mult),
                             reads=[yb, zs[b]], writes=[yb])
                        sqf = sq[:].rearrange("p g q -> p (g q)")
                        P.op('pool', lambda e: e.tensor_tensor(out=sqf, in0=ybf, in1=ybf, op=ALU.mult), reads=[yb], writes=[sq])
                        P.op('dve', lambda e: e.tensor_reduce(out=ss[:], in_=sq[:], axis=AX.X, op=ALU.add), reads=[sq], writes=[ss])
                        P.op('act', lambda e: e.activation(out=ss[:], in_=ss[:], func=AF.Ln, bias=RMS_EPS, scale=1.0 / 512),
                             reads=[ss], writes=[ss])
                        P.op('act', lambda e: e.activation(out=ss[:], in_=ss[:], func=AF.Exp, scale=-0.5), reads=[ss], writes=[ss])
                        P.op('dve', lambda e: e.tensor_tensor(out=sq[:], in0=ybf.rearrange("p (g q) -> p g q", g=NG),
                                                              in1=ss[:].unsqueeze(2).to_broadcast([128, NG, 512]), op=ALU.mult),
                             reads=[yb, ss], writes=[sq])
                        P.op('pool', lambda e: e.tensor_tensor(out=yn[:], in0=sqf, in1=nwb[:], op=ALU.mult), reads=[sq, nwb], writes=[yn])
                        yt2 = ynT[ci % 2]
                        for q4 in range(2):
                            pb = PB[1] if q4 == 0 else PB[0]
                            for u in range(4):
                                i = q4 * 4 + u
                                P.op('pe', lambda e, i=i, u=u, pb=pb: e.matmul(pb[:, u * 128:(u + 1) * 128], lhsT=yn[:, i * 128:(i + 1) * 128],
                                                                              rhs=cmb[:, C_IDENT, :], start=True, stop=True),
                                     reads=[yn, cmb], writes=[pb])
                            P.op('act', lambda e, q4=q4, pb=pb: e.activation(out=yt2[:, q4 * 4:(q4 + 1) * 4, :],
                                                                            in_=pb[:, :].rearrange("p (u t) -> p u t", u=4), func=AF.Copy),
                                 reads=[pb], writes=[yt2])
                        P.dma('act', YNT.rearrange("(i p) s -> p i s", p=128)[:, :, tok], yt2[:], reads=[yt2])

                stage_a(0)
                for ci in range(len(order)):
                    if ci + 1 < len(order):
                        stage_a(ci + 1)
                    stage_b(ci)

        def odd_out_phase(oo):
            with Phase() as ph:
                wo = load_w(ph, ssd_w_out[oo], 1024, D)
                mix = [ph.sb([128, 8, 512], BF16) for _ in range(2)]
                ob = [ph.sb([128, 8, 512], BF16) for _ in range(2)]
                for ti, (c, t, l0, n) in enumerate(ctiles):
                    s0 = t * Sh + l0
                    b2 = ti % 2
                    P.dma('sp', mix[b2][:, :, :n], YNT.rearrange("(i p) s -> p i s", p=128)[:, :, s0:s0 + n], writes=[mix[b2]])
                    o_ = ob[b2]
                    for m in range(8):
                        po = PB[4 + m % 4]
                        for k in range(8):
                            P.op('pe', lambda e, m=m, k=k, po=po, b2=b2: e.matmul(
                                po[:, :n], lhsT=wo[:, k, m * 128:(m + 1) * 128], rhs=mix[b2][:, k, :n],
                                start=(k == 0), stop=(k == 7)), reads=[wo, mix[b2]], writes=[po])
                        if m % 2 == 0:
                            P.op('act', lambda e, m=m, po=po: e.activation(out=o_[:, m, :n], in_=po[:, :n], func=AF.Copy),
                                 reads=[po], writes=[o_])
                        else:
                            P.op('dve', lambda e, m=m, po=po: e.tensor_copy(out=o_[:, m, :n], in_=po[:, :n]), reads=[po], writes=[o_])
                    P.dma('act', Ppc[c][t].rearrange("k p s -> p k s"), o_[:, :, :n], reads=[o_], writes=[ppres[c]])
                    if t == 1:
                        rs_chunk(c)

        Xcur = xin
        for l in range(DEPTH):
            Xlast = yout if l == DEPTH - 1 else X2
            modh_phase(l, Xcur)
            if l % 2 == 0:
                ee = l // 2
                even_proj_phase(ee)
                gla_phase(ee, 0)
                gla_phase(ee, 1)
                attn_phase(ee)
            else:
                oo = l // 2
                odd_proj_phase(oo)
                conv_phase(oo)
                ssd_phase(oo, 0)
                ssd_phase(oo, 1)
            with Phase() as phw:
                win = load_w(phw, ffn_w_in[l], D, 2 * FFH)
                wout = load_w(phw, ffn_w_out[l], FFH, D)
                if l % 2 == 0:
                    even_out_phase(l // 2)
                else:
                    odd_out_phase(l // 2)
                ffn_phase(l, Xcur, Xlast, win, wout)
            Xcur = Xlast
        P.barrier()
        P.emit()
    return nc


def _prep_inputs(inputs, TL):
    x = np.asarray(inputs['x'], np.float32)
    B = x.shape[0]
    S = CTX + TL
    Sh = S // 2
    g = lambda k: np.asarray(inputs[k], np.float32)
    f32 = lambda a: np.ascontiguousarray(np.asarray(a, np.float32))
    shared = {
        'mod_w': f32(g('mod_w')),
        'mod_bT': f32(g('mod_b').reshape(4, 48, 128).transpose(2, 0, 1)),
        'ln_gT': f32(g('ln_g').reshape(4, 2, 8, 128).transpose(3, 0, 1, 2)),
        'ln_bT': f32(g('ln_b').reshape(4, 2, 8, 128).transpose(3, 0, 1, 2)),
        'ffn_w_in': f32(g('ffn_w_in')),
        'ffn_w_out': f32(g('ffn_w_out')),
        'gla_nw': f32(g('gla_norm_w').T),
        'diff_nw': f32(g('diff_norm_w').T),
        'diff_lam': f32(g('diff_lambda').reshape(1, 2, 256)),
        'cmat': _const_mats(),
    }
    cos, sin = _rope_tables(TL)
    shared['ropecos'] = cos
    shared['ropesin'] = sin
    per_rank = []
    evw, evo = g('ev_w_in'), g('ev_w_out')
    sw, so = g('ssd_w_in'), g('ssd_w_out')
    cwf, cbf = g('ssd_conv_w'), g('ssd_conv_b')
    for r in range(2):
        pr = {}
        cols = np.concatenate([
            np.arange(0, 256)[r * 128:(r + 1) * 128],
            256 + np.arange(256)[r * 128:(r + 1) * 128],
            512 + np.arange(512)[r * 256:(r + 1) * 256],
            1024 + np.arange(512)[r * 256:(r + 1) * 256],
            np.arange(1536, 1568),
            1568 + np.arange(512)[r * 256:(r + 1) * 256],
            2080 + np.arange(512)[r * 256:(r + 1) * 256],
            2592 + np.arange(512)[r * 256:(r + 1) * 256],
        ])
        pr['ev_w_in'] = f32(evw[:, :, cols])
        rows = np.concatenate([np.arange(512)[r * 256:(r + 1) * 256], 512 + np.arange(512)[r * 256:(r + 1) * 256]])
        pr['ev_w_out'] = f32(evo[:, rows, :])
        wg = np.zeros((2, 17, 2, 128), np.float32)
        wg[:, 0:16] = g('gla_w_gate2').transpose(0, 2, 1, 3)[..., r * 128:(r + 1) * 128]
        wg[:, 16] = g('gla_b_gate')[..., r * 128:(r + 1) * 128]
        pr['gla_wg'] = wg
        hs = slice(r * 16, (r + 1) * 16)
        zc = np.arange(2048)[r * 1024:(r + 1) * 1024]
        xc = 2048 + np.arange(2048)[r * 1024:(r + 1) * 1024]
        bc = 4096 + np.arange(512)[r * 256:(r + 1) * 256]
        cc = 4608 + np.arange(512)[r * 256:(r + 1) * 256]
        dc = np.concatenate([5120 + np.arange(32)[hs], 5152 + np.arange(32)[hs]])
        pr['ssd_w_in'] = f32(sw[:, :, np.concatenate([zc, xc, bc, cc, dc])])
        pr['ssd_w_out'] = f32(so[:, r * 1024:(r + 1) * 1024, :])
        cch = np.concatenate([xc, bc, cc]) - 2048
        pr['ssd_cw'] = f32(cwf[:, :, cch].reshape(2, 5, 12, 128).transpose(0, 3, 1, 2))
        pr['ssd_cb'] = f32(cbf[:, cch].reshape(2, 12, 128).transpose(0, 2, 1))
        rep = lambda a: f32(np.broadcast_to(np.asarray(a, np.float32)[:, None, :], (2, 128, np.asarray(a).shape[-1])))
        pr['ssd_dtb'] = rep(g('ssd_dt_bias')[:, :, hs].reshape(2, 32))
        pr['ssd_alog'] = rep(g('ssd_a_log')[:, :, hs].reshape(2, 32))
        pr['ssd_dsk'] = rep(g('ssd_d')[:, hs])
        pr['ssd_nw'] = rep(g('ssd_norm_w')[:, r * 1024:(r + 1) * 1024])
        pr['mctx'] = np.full((128, 1), 1.0 if r == 0 else 0.0, np.float32)
        per_rank.append(pr)
    maps = []
    for core in range(8):
        b, r = core % 4, core // 4
        b = b % B
        xc_ = np.concatenate([g('ctx')[b], x[b]], 0)
        m = dict(shared)
        m.update(per_rank[r])
        m['xin'] = np.ascontiguousarray(xc_[r * Sh:(r + 1) * Sh].T.reshape(8, 128, Sh))
        cs = np.stack([g('c')[b], g('c_ctx')], -1)
        m['csil'] = np.ascontiguousarray(cs.reshape(8, 128, 2).transpose(1, 0, 2))
        maps.append(m)
    return maps


def run(inputs, TL, DEPTH, dump=()):
    nc = build(TL, DEPTH, dump)
    maps = _prep_inputs(inputs, TL)
    return run_bass_kernel_spmd(nc, maps, core_ids=list(range(8)))


def _gather(res, B, TL):
    S = CTX + TL
    Sh = S // 2
    out = np.empty((B, TL, D), np.float32)
    for b in range(B):
        full = np.concatenate([res.results[b + 4 * r]['yout'].reshape(D, Sh) for r in range(2)], axis=1)
        out[b] = full[:, CTX:].T
    return out


def kernel(**inputs):
    x = np.asarray(inputs['x'])
    B, TL, _ = x.shape
    res = run(inputs, TL, 4)
    return _gather(res, B, TL)
```

```python
import contextlib
import math
import numpy as np
import concourse.bass as bass
import concourse.mybir as mybir
from concourse.bass_utils import run_bass_kernel_spmd

F32 = mybir.dt.float32
BF16 = mybir.dt.bfloat16
AF = mybir.ActivationFunctionType
ALU = mybir.AluOpType
AX = mybir.AxisListType

ENGS = ('pe', 'act', 'dve', 'pool', 'sp')
DQS = ('sp', 'pool', 'act')
SEM_LIMIT = 30000
N_EPOCH = 12
N_DMASEM = 8


class Res:
    __slots__ = ('w', 'r')

    def __init__(self):
        self.w = None
        self.r = []


class T:
    def __init__(self, ap, res=None):
        self.ap = ap
        self.res = res if res is not None else Res()

    def __getitem__(self, idx):
        return self.ap[idx]


def _res(t):
    return t.res if isinstance(t, T) else t


class _Rec:
    def __getattr__(self, name):
        def f(*a, **k):
            self.call = (name, a, k)
        return f


class Prog:
    def __init__(self, nc, same_engine_sync=True):
        self.nc = nc
        self.same = same_engine_sync
        self.streams = {e: [] for e in ENGS}
        self.sems = {e: [nc.alloc_semaphore(name=f"s_{e}_{i}") for i in range(N_EPOCH)] for e in ENGS}
        self.ops = {e: [] for e in ENGS}
        self.seen = {e: {} for e in ENGS}
        self.dsems = {q: [nc.alloc_semaphore(name=f"d_{q}_{i}") for i in range(N_DMASEM)] for q in DQS}
        self.dcnt = {q: 0 for q in DQS}
        self.nops = 0

    def _need(self, eng, ev):
        kind, src, val = ev
        if kind == 'c':
            if src == eng and (eng == 'pe' or not self.same):
                return
            if self.seen[eng].get(('c', src), 0) >= val:
                return
            self.seen[eng][('c', src)] = val
            self.ops[src][val - 1]['signal'] = True
            self.streams[eng].append({'waitc': (src, val)})
        else:
            key = (kind, src[0], src[1])
            if self.seen[eng].get(key, 0) >= val:
                return
            self.seen[eng][key] = val
            self.streams[eng].append({'wait': (self.dsems[src[0]][src[1]], val)})

    def _collect(self, reads, writes):
        evs = []
        for t in reads:
            r = _res(t)
            if r.w is not None:
                evs.append(r.w)
        for t in writes:
            r = _res(t)
            if r.w is not None:
                evs.append(r.w)
            evs.extend(r.r)
        return evs

    def _mark(self, ev, reads, writes):
        for t in reads:
            _res(t).r.append(ev)
        for t in writes:
            r = _res(t)
            r.w = ev
            r.r = []

    def op(self, eng, fn, reads=(), writes=()):
        for ev in self._collect(reads, writes):
            self._need(eng, ev)
        r = _Rec()
        fn(r)
        rec = {'call': r.call, 'signal': False}
        self.streams[eng].append(rec)
        self.ops[eng].append(rec)
        ev = ('c', eng, len(self.ops[eng]))
        self._mark(ev, reads, writes)
        self.nops += 1
        return ev

    def dma(self, q, out, in_, reads=(), writes=(), **kw):
        evs = self._collect(reads, writes)
        k = self.dcnt[q]
        self.dcnt[q] += 1
        si, rnd = k % N_DMASEM, k // N_DMASEM
        if rnd > 0:
            evs.append(('d', (q, si), 16 * rnd))
        for ev in evs:
            self._need(q, ev)
        rec = {'dma': (out, in_, kw), 'sig': (self.dsems[q][si], 16)}
        self.streams[q].append(rec)
        ev = ('d', (q, si), 16 * (rnd + 1))
        self._mark(ev, reads, writes)
        self.nops += 1
        return ev

    def barrier(self):
        evs = []
        for e in ENGS:
            if self.ops[e]:
                evs.append(('c', e, len(self.ops[e])))
        for q in DQS:
            for si in range(N_DMASEM):
                n = (self.dcnt[q] - si + N_DMASEM - 1) // N_DMASEM if self.dcnt[q] > si else 0
                if n > 0:
                    evs.append(('d', (q, si), 16 * n))
        for e in ENGS:
            for ev in evs:
                if ev[0] == 'c' and ev[1] == e:
                    continue
                self._need(e, ev)

    def _resolve(self):
        for e in ENGS:
            epoch, c = 0, 0
            for rec in self.ops[e]:
                if rec['signal']:
                    c += 1
                    if c > SEM_LIMIT:
                        epoch += 1
                        c = 1
                        assert epoch < N_EPOCH, "out of semaphore epochs"
                    rec['semval'] = (self.sems[e][epoch], c)

    def _replay(self, eng_obj, stream):
        for rec in stream:
            if 'waitc' in rec:
                src, idx = rec['waitc']
                s, v = self.ops[src][idx - 1]['semval']
                eng_obj.wait_ge(s, v)
            elif 'wait' in rec:
                s, v = rec['wait']
                eng_obj.wait_ge(s, v)
            elif 'dma' in rec:
                out, in_, kw = rec['dma']
                ins = eng_obj.dma_start(out=out, in_=in_, **kw)
                s, v = rec['sig']
                ins.then_inc(s, v)
            else:
                name, a, k = rec['call']
                ins = getattr(eng_obj, name)(*a, **k)
                if rec['signal']:
                    ins.then_inc(rec['semval'][0], 1)

    def emit(self):
        self._resolve()
        with self.nc.Block() as block:
            @block.tensor
            def _(e):
                self._replay(e, self.streams['pe'])

            @block.scalar
            def _(e):
                self._replay(e, self.streams['act'])

            @block.vector
            def _(e):
                self._replay(e, self.streams['dve'])

            @block.gpsimd
            def _(e):
                self._replay(e, self.streams['pool'])

            @block.sync
            def _(e):
                self._replay(e, self.streams['sp'])


CTX = 256
D = 1024
FFH = 2816
NJ = FFH // 128
ALPHA = (2 * 4) ** 0.25
LN_EPS = 1e-6
RMS_EPS = 1e-6
GRID_W = 64
ROPE_BASE = 10000.0
GROUPS = [[0, 4], [1, 5], [2, 6], [3, 7]]
EVC = 1568
SDC = 2592

C_IDENT, C_TRIF, C_TRIB, C_STRIF, C_STRIB, C_MASKF, C_MASKB, C_PERM, C_ONES, C_MEAN, C_UF, C_UB, C_SLF, C_SLB = range(14)
NCMAT = 14


def _const_mats():
    j = np.arange(128)[:, None]
    i = np.arange(128)[None, :]
    m = np.zeros((128, NCMAT, 128), np.float32)
    m[:, C_IDENT] = (j == i)
    m[:, C_TRIF] = (j <= i) * (-1.0 / 16)
    m[:, C_TRIB] = (j >= i) * (-1.0 / 16)
    m[:, C_STRIF] = (j > i) * (-1.0 / 16)
    m[:, C_STRIB] = (j < i) * (-1.0 / 16)
    m[:, C_MASKF] = (j <= i)
    m[:, C_MASKB] = (j >= i)
    m[:, C_PERM] = (j == (i ^ 16))
    m[:, C_ONES] = 1.0
    m[:, C_MEAN] = 1.0 / 1024
    m[:, C_UF] = (j <= i)
    m[:, C_UB] = (j >= i)
    m[:, C_SLF] = (j > i)
    m[:, C_SLB] = (j < i)
    return m


def _rope_tables(TL):
    t = np.arange(TL)
    row = (t // GRID_W).astype(np.float32)
    col = (t % GRID_W).astype(np.float32)
    nf = 16
    inv = (ROPE_BASE ** (-np.arange(nf, dtype=np.float32) / nf)).astype(np.float32)
    ar = row[None, :] * inv[:, None]
    ac = col[None, :] * inv[:, None]
    cos64 = np.concatenate([np.cos(ar), np.cos(ar), np.cos(ac), np.cos(ac)], 0)
    sin64 = np.concatenate([-np.sin(ar), np.sin(ar), -np.sin(ac), np.sin(ac)], 0)
    cos = np.concatenate([cos64, cos64], 0).astype(np.float32)
    sin = np.concatenate([sin64, sin64], 0).astype(np.float32)
    return np.ascontiguousarray(cos), np.ascontiguousarray(sin)


def _tiles(total, first, step):
    out = [(0, first)]
    s = first
    while s < total:
        n = min(step, total - s)
        out.append((s, n))
        s += n
    return out


def build(TL, DEPTH, dump=()):
    S = CTX + TL
    Sh = S // 2
    assert Sh % 128 == 0
    NCH = S // 128
    own512 = _tiles(Sh, 256, 512)
    own256 = _tiles(Sh, 256, 256)
    gtiles = [(t, l0, n) for t in range(2) for (l0, n) in own512]
    ctiles = [(c, t, l0, n) for c, (l0, n) in enumerate(own512) for t in range(2)]
    nc = bass.Bass("TRN2", target_bir_lowering=False)
    P = Prog(nc)

    def din(name, shape, dt=F32):
        return nc.dram_tensor(name, list(shape), dt, kind="ExternalInput").ap()

    def dscr(name, shape, dt):
        kind = "ExternalOutput" if name in dump else "Internal"
        return nc.dram_tensor(name, list(shape), dt, kind=kind).ap()

    xin = din("xin", [8, 128, Sh])
    csil = din("csil", [128, 8, 2])
    mctx = din("mctx", [128, 1])
    mod_w = din("mod_w", [4, D, 6 * D])
    mod_bT = din("mod_bT", [128, 4, 48])
    ln_gT = din("ln_gT", [128, 4, 2, 8])
    ln_bT = din("ln_bT", [128, 4, 2, 8])
    ffn_w_in = din("ffn_w_in", [4, D, 2 * FFH])
    ffn_w_out = din("ffn_w_out", [4, FFH, D])
    ev_w_in = din("ev_w_in", [2, D, EVC])
    ev_w_out = din("ev_w_out", [2, 512, D])
    gla_wg = din("gla_wg", [2, 17, 2, 128])
    gla_nw = din("gla_nw", [128, 2])
    diff_nw = din("diff_nw", [128, 2])
    diff_lam = din("diff_lam", [1, 2, 256])
    ssd_w_in = din("ssd_w_in", [2, D, SDC])
    ssd_w_out = din("ssd_w_out", [2, 1024, D])
    ssd_cw = din("ssd_cw", [2, 128, 5, 12])
    ssd_cb = din("ssd_cb", [2, 128, 12])
    ssd_dtb = din("ssd_dtb", [2, 128, 32])
    ssd_alog = din("ssd_alog", [2, 128, 32])
    ssd_dsk = din("ssd_dsk", [2, 128, 16])
    ssd_nw = din("ssd_nw", [2, 128, 1024])
    cmat_d = din("cmat", [128, NCMAT, 128])
    cos_d = din("ropecos", [128, TL])
    sin_d = din("ropesin", [128, TL])
    yout = nc.dram_tensor("yout", [8, 128, Sh], F32, kind="ExternalOutput").ap()

    X2 = dscr("X2", [8, 128, Sh], F32)
    Hloc = dscr("Hloc", [8, 128, Sh], BF16)
    Hg = dscr("Hg", [8, 2, 128, Sh], BF16)
    Ppc = [dscr(f"Pp{c}", [2, 8, 128, n], BF16) for c, (l0, n) in enumerate(own512)]
    Moc = [dscr(f"Mo{c}", [8, 128, n], BF16) for c, (l0, n) in enumerate(own512)]
    QG = dscr("QG", [128, S], BF16)
    KG = dscr("KG", [128, S], BF16)
    GG = dscr("GG", [256, S], BF16)
    RR = dscr("RR", [32, S], BF16)
    QD = dscr("QD", [256, S], BF16)
    KD = dscr("KD", [256, S], BF16)
    KGt = dscr("KGt", [S, 128], BF16)
    VGt = dscr("VGt", [S, 256], BF16)
    VDt = dscr("VDt", [S, 256], BF16)
    OG = dscr("OG", [256, S], F32)
    DT = dscr("DT", [256, S], BF16)
    XBC = dscr("XBC", [1536, S], BF16)
    ZS = dscr("ZS", [S, 1024], BF16)
    DTt = dscr("DTt", [S, 32], F32)
    BTd = dscr("BTd", [256, S], BF16)
    CTd = dscr("CTd", [256, S], BF16)
    XSt = dscr("XSt", [S, 1024], BF16)
    Btd = dscr("Btd", [S, 256], BF16)
    YF = dscr("YF", [S, 1024], F32)
    YNT = dscr("YNT", [1024, S], BF16)

    def fm(x3):
        return x3.rearrange("k p s -> p k s")

    with contextlib.ExitStack() as gs:
        cnt = [0]

        def gsb(shape, dt, name=None):
            cnt[0] += 1
            return T(gs.enter_context(nc.sbuf_tensor(name or f"g{cnt[0]}", list(shape), dt)))

        psum = gs.enter_context(nc.psum_tensor("psum", [128, 8, 512], F32))
        PB = [T(psum[:, i, :]) for i in range(8)]

        class Phase:
            def __enter__(self):
                self.es = contextlib.ExitStack()
                return self

            def sb(self, shape, dt, name=None):
                cnt[0] += 1
                return T(self.es.enter_context(nc.sbuf_tensor(name or f"t{cnt[0]}", list(shape), dt)))

            def __exit__(self, *a):
                P.barrier()
                self.es.close()
                return False

        ccres = Res()
        ppres = [Res() for _ in own512]

        def rs_chunk(c):
            P.op('pool', lambda e: e.collective_compute("ReduceScatter", ALU.add, replica_groups=GROUPS,
                                                        ins=[Ppc[c].rearrange("t k p s -> (t k p) s")],
                                                        outs=[Moc[c].rearrange("k p s -> (k p) s")]),
                 reads=[ppres[c]], writes=[ccres])

        def mo_view(l0, n):
            for c, (c0, cn) in enumerate(own512):
                if c0 <= l0 and l0 + n <= c0 + cn:
                    return Moc[c].rearrange("k p s -> p k s")[:, :, l0 - c0:l0 - c0 + n]
            raise AssertionError((l0, n))

        def all_gather(src2d, dst2d):
            P.barrier()
            P.op('pool', lambda e: e.collective_compute("AllGather", ALU.bypass, replica_groups=GROUPS,
                                                        ins=[src2d], outs=[dst2d]), writes=[ccres])
            P.barrier()

        def reduce_scatter(src2d, dst2d):
            P.barrier()
            P.op('pool', lambda e: e.collective_compute("ReduceScatter", ALU.add, replica_groups=GROUPS,
                                                        ins=[src2d], outs=[dst2d]), writes=[ccres])
            P.barrier()

        cmb = gsb([128, NCMAT, 128], BF16, "cmb")
        modT = gsb([128, 4, 48, 2], F32, "modT")
        mod1p = gsb([128, 4, 48, 2], F32, "mod1p")
        lng = gsb([128, 4, 2, 8], F32, "lng")
        lnb = gsb([128, 4, 2, 8], F32, "lnb")
        gnw = gsb([128, 2], F32, "gnw")
        dnw = gsb([128, 2], F32, "dnw")
        dnws = gsb([128, 2], F32, "dnws")
        nlam = gsb([128, 2], F32, "nlam")
        mcx = gsb([128, 1], F32, "mcx")

        def load_cmf(ph):
            t = ph.sb([128, NCMAT, 128], F32)
            P.dma('sp', t[:], cmat_d, writes=[t])
            return t
        P.dma('pool', cmb[:], cmat_d, writes=[cmb])
        P.dma('sp', lng[:], ln_gT, writes=[lng])
        P.dma('sp', lnb[:], ln_bT, writes=[lnb])
        P.dma('sp', gnw[:], gla_nw, writes=[gnw])
        P.dma('sp', dnw[:], diff_nw, writes=[dnw])
        P.dma('sp', mcx[:], mctx, writes=[mcx])

        def lam_init(layer):
            return 0.8 - 0.6 * math.exp(-0.3 * layer)

        with Phase() as ph:
            cmf = load_cmf(ph)
            sc = ph.sb([128, 8, 2], F32)
            modb = ph.sb([128, 4, 48], F32)
            mdiff = ph.sb([128, 4, 48], F32)
            P.dma('sp', sc[:], csil, writes=[sc])
            P.dma('sp', modb[:], mod_bT, writes=[modb])
            P.op('act', lambda e: e.activation(out=sc[:], in_=sc[:], func=AF.Silu), reads=[sc], writes=[sc])
            wb = [ph.sb([128, 8, 1024], F32) for _ in range(2)]
            it = 0
            for l in range(DEPTH):
                for cb in range(6):
                    wt = wb[it % 2]
                    pb = PB[it % 2]
                    it += 1
                    for k in range(8):
                        P.dma('sp' if k % 2 == 0 else 'act', wt[:, k, :],
                              mod_w[l, k * 128:(k + 1) * 128, cb * 1024:(cb + 1) * 1024], writes=[wt])
                    for m in range(8):
                        for k in range(8):
                            P.op('pe', lambda e, m=m, k=k, wt=wt, pb=pb: e.matmul(
                                pb[:, 2 * m:2 * m + 2], lhsT=wt[:, k, m * 128:(m + 1) * 128], rhs=sc[:, k, :],
                                start=(k == 0), stop=(k == 7)), reads=[wt, sc], writes=[pb])
                    P.op('dve', lambda e, l=l, cb=cb, pb=pb: e.tensor_tensor(
                        out=modT[:, l, cb * 8:(cb + 1) * 8, :],
                        in0=pb[:, 0:16].rearrange("p (m j) -> p m j", j=2),
                        in1=modb[:, l, cb * 8:(cb + 1) * 8].unsqueeze(2).to_broadcast([128, 8, 2]),
                        op=ALU.add), reads=[pb, modb], writes=[modT])
            P.op('dve', lambda e: e.tensor_tensor(out=mdiff[:], in0=modT[:, :, :, 1], in1=modT[:, :, :, 0], op=ALU.subtract),
                 reads=[modT], writes=[mdiff])
            P.op('dve', lambda e: e.scalar_tensor_tensor(out=modT[:, :, :, 1], in0=mdiff[:], scalar=mcx[:, 0:1], in1=modT[:, :, :, 0],
                                                         op0=ALU.mult, op1=ALU.add), reads=[mdiff, mcx, modT], writes=[modT])
            P.op('dve', lambda e: e.tensor_scalar_add(out=mod1p[:], in0=modT[:], scalar1=1.0),
                 reads=[modT], writes=[mod1p])
            dl = ph.sb([1, 2, 256], F32)
            pr = ph.sb([1, 2, 2, 64], F32)
            sm = ph.sb([1, 4], F32)
            lv = ph.sb([1, 2], F32)
            P.dma('sp', dl[:], diff_lam, writes=[dl])
            dlv = dl[:].rearrange("o e (a b d) -> o e a b d", a=2, b=2)
            P.op('dve', lambda e: e.tensor_tensor(out=pr[:], in0=dlv[:, :, :, 0, :], in1=dlv[:, :, :, 1, :], op=ALU.mult),
                 reads=[dl], writes=[pr])
            P.op('dve', lambda e: e.tensor_reduce(out=sm[:], in_=pr[:].rearrange("o e a d -> o (e a) d"),
                                                  axis=AX.X, op=ALU.add), reads=[pr], writes=[sm])
            P.op('act', lambda e: e.activation(out=sm[:], in_=sm[:], func=AF.Exp), reads=[sm], writes=[sm])
            smv = sm[:].rearrange("o (e a) -> o e a", a=2)
            P.op('dve', lambda e: e.tensor_tensor(out=lv[:], in0=smv[:, :, 1], in1=smv[:, :, 0], op=ALU.subtract),
                 reads=[sm], writes=[lv])
            for ee in range(2):
                P.op('dve', lambda e, ee=ee: e.tensor_scalar_add(out=lv[:, ee:ee + 1], in0=lv[:, ee:ee + 1],
                                                                 scalar1=-lam_init(2 * ee)), reads=[lv], writes=[lv])
            P.op('pe', lambda e: e.matmul(PB[2][:, 0:2], lhsT=cmf[0:1, C_ONES, :], rhs=lv[:], start=True, stop=True),
                 reads=[cmf, lv], writes=[PB[2]])
            P.op('dve', lambda e: e.tensor_copy(out=nlam[:], in_=PB[2][:, 0:2]), reads=[PB[2]], writes=[nlam])
            for ee in range(2):
                P.op('dve', lambda e, ee=ee: e.tensor_scalar_mul(out=dnws[:, ee:ee + 1], in0=dnw[:, ee:ee + 1],
                                                                 scalar1=1.0 - lam_init(2 * ee)), reads=[dnw], writes=[dnws])

        def load_w(ph, src, K, N):
            kc = K // 128
            w = ph.sb([128, kc, N], BF16)
            for k in range(kc):
                P.dma('pool', w[:, k, :], src[k * 128:(k + 1) * 128, :], writes=[w])
            return w

        def modulate(xt, ht, n, l, which_shift, which_scale, j):
            for k in range(8):
                P.op('act', lambda e, k=k: e.activation(
                    out=ht[:, k, :n], in_=xt[:, k, :n], func=AF.Identity,
                    scale=mod1p[:, l, which_scale * 8 + k, j:j + 1], bias=modT[:, l, which_shift * 8 + k, j:j + 1]),
                    reads=[xt, mod1p, modT], writes=[ht])

        def layer_norm(ph_bufs, v, n, l, which):
            vb, vsq, mean_sb, rstd = ph_bufs
            P.op('act', lambda e: e.activation(out=vb[:, 0:8, :n], in_=v[:, :, :n], func=AF.Copy), reads=[v], writes=[vb])
            P.op('pool', lambda e: e.tensor_tensor(out=vsq[:, 0:8, :n], in0=v[:, :, :n], in1=v[:, :, :n], op=ALU.mult),
                 reads=[v], writes=[vsq])
            for k in range(8):
                P.op('pe', lambda e, k=k: e.matmul(PB[6][:, :n], lhsT=cmb[:, C_MEAN, :], rhs=vb[:, k, :n],
                                                   start=(k == 0), stop=(k == 7)), reads=[cmb, vb], writes=[PB[6]])
            for k in range(8):
                P.op('pe', lambda e, k=k: e.matmul(PB[7][:, :n], lhsT=cmb[:, C_MEAN, :], rhs=vsq[:, k, :n],
                                                   start=(k == 0), stop=(k == 7)), reads=[cmb, vsq], writes=[PB[7]])
            P.op('act', lambda e: e.activation(out=mean_sb[:, :n], in_=PB[6][:, :n], func=AF.Copy),
                 reads=[PB[6]], writes=[mean_sb])
            P.op('dve', lambda e: e.tensor_tensor(out=rstd[:, :n], in0=mean_sb[:, :n], in1=mean_sb[:, :n], op=ALU.mult),
                 reads=[mean_sb], writes=[rstd])
            P.op('dve', lambda e: e.tensor_tensor(out=rstd[:, :n], in0=PB[7][:, :n], in1=rstd[:, :n], op=ALU.subtract),
                 reads=[PB[7], rstd], writes=[rstd])
            P.op('act', lambda e: e.activation(out=rstd[:, :n], in_=rstd[:, :n], func=AF.Sqrt, bias=LN_EPS, scale=1.0),
                 reads=[rstd], writes=[rstd])
            P.op('dve', lambda e: e.reciprocal(out=rstd[:, :n], in_=rstd[:, :n]), reads=[rstd], writes=[rstd])
            P.op('dve', lambda e: e.tensor_tensor(out=v[:, :, :n], in0=v[:, :, :n],
                                                  in1=mean_sb[:, :n].unsqueeze(1).to_broadcast([128, 8, n]),
                                                  op=ALU.subtract), reads=[v, mean_sb], writes=[v])
            P.op('pool', lambda e: e.tensor_tensor(out=v[:, :, :n], in0=v[:, :, :n],
                                                   in1=rstd[:, :n].unsqueeze(1).to_broadcast([128, 8, n]),
                                                   op=ALU.mult), reads=[v, rstd], writes=[v])
            for k in range(8):
                P.op('act', lambda e, k=k: e.activation(out=v[:, k, :n], in_=v[:, k, :n], func=AF.Identity,
                                                        scale=lng[:, l, which, k:k + 1], bias=lnb[:, l, which, k:k + 1]),
                     reads=[v, lng, lnb], writes=[v])

        def modh_phase(l, Xsrc):
            with Phase() as ph:
                xb = [ph.sb([128, 8, 512], F32) for _ in range(2)]
                hb = [ph.sb([128, 8, 512], BF16) for _ in range(2)]
                for ti, (l0, n) in enumerate(own512):
                    j = 1 if l0 == 0 else 0
                    xt, ht = xb[ti % 2], hb[ti % 2]
                    P.dma('sp', xt[:, :, :n], fm(Xsrc)[:, :, l0:l0 + n], writes=[xt])
                    modulate(xt, ht, n, l, 0, 1, j)
                    P.dma('act', fm(Hloc)[:, :, l0:l0 + n], ht[:, :, :n], reads=[ht])
            P.barrier()
            for k in range(8):
                P.op('pool', lambda e, k=k: e.collective_compute("AllGather", ALU.bypass, replica_groups=GROUPS,
                                                                 ins=[Hloc[k]], outs=[Hg[k].rearrange("t p s -> (t p) s")]),
                     writes=[ccres])
            P.barrier()

        def layer_norm2(v, n, l, which, sgt, mean_sb, rstd):
            for k in range(8):
                tt = sgt[k % 2]
                P.op('act', lambda e, k=k, tt=tt: e.activation(out=tt[:, :n], in_=v[:, k, :n], func=AF.Square), reads=[v], writes=[tt])
                P.op('pe', lambda e, k=k, tt=tt: e.matmul(PB[7][:, :n], lhsT=cmb[:, C_MEAN, :], rhs=tt[:, :n],
                                                          start=(k == 0), stop=(k == 7)), reads=[cmb, tt], writes=[PB[7]])
            for k in range(8):
                tt = sgt[k % 2]
                P.op('act', lambda e, k=k, tt=tt: e.activation(out=tt[:, :n], in_=v[:, k, :n], func=AF.Copy), reads=[v], writes=[tt])
                P.op('pe', lambda e, k=k, tt=tt: e.matmul(PB[6][:, :n], lhsT=cmb[:, C_MEAN, :], rhs=tt[:, :n],
                                                          start=(k == 0), stop=(k == 7)), reads=[cmb, tt], writes=[PB[6]])
            P.op('act', lambda e: e.activation(out=mean_sb[:, :n], in_=PB[6][:, :n], func=AF.Copy),
                 reads=[PB[6]], writes=[mean_sb])
            P.op('dve', lambda e: e.tensor_tensor(out=rstd[:, :n], in0=mean_sb[:, :n], in1=mean_sb[:, :n], op=ALU.mult),
                 reads=[mean_sb], writes=[rstd])
            P.op('dve', lambda e: e.tensor_tensor(out=rstd[:, :n], in0=PB[7][:, :n], in1=rstd[:, :n], op=ALU.subtract),
                 reads=[PB[7], rstd], writes=[rstd])
            P.op('act', lambda e: e.activation(out=rstd[:, :n], in_=rstd[:, :n], func=AF.Sqrt, bias=LN_EPS, scale=1.0),
                 reads=[rstd], writes=[rstd])
            P.op('dve', lambda e: e.reciprocal(out=rstd[:, :n], in_=rstd[:, :n]), reads=[rstd], writes=[rstd])
            P.op('dve', lambda e: e.tensor_tensor(out=v[:, :, :n], in0=v[:, :, :n],
                                                  in1=mean_sb[:, :n].unsqueeze(1).to_broadcast([128, 8, n]),
                                                  op=ALU.subtract), reads=[v, mean_sb], writes=[v])
            P.op('dve', lambda e: e.tensor_tensor(out=v[:, 0:5, :n], in0=v[:, 0:5, :n],
                                                  in1=rstd[:, :n].unsqueeze(1).to_broadcast([128, 5, n]),
                                                  op=ALU.mult), reads=[v, rstd], writes=[v])
            P.op('pool', lambda e: e.tensor_tensor(out=v[:, 5:8, :n], in0=v[:, 5:8, :n],
                                                   in1=rstd[:, :n].unsqueeze(1).to_broadcast([128, 3, n]),
                                                   op=ALU.mult), reads=[v, rstd], writes=[v])
            for k in range(8):
                P.op('act', lambda e, k=k: e.activation(out=v[:, k, :n], in_=v[:, k, :n], func=AF.Identity,
                                                        scale=lng[:, l, which, k:k + 1], bias=lnb[:, l, which, k:k + 1]),
                     reads=[v, lng, lnb], writes=[v])

        def ffn_phase(l, Xsrc, Xdst, win, wout):
            with Phase() as ph:
                xt = ph.sb([128, 8, 512], F32)
                x1a = ph.sb([128, 8, 512], F32)
                hb = ph.sb([128, 8, 512], BF16)
                actb = ph.sb([128, NJ, 512], BF16)
                sg = [ph.sb([128, 512], BF16) for _ in range(2)]
                mean_sb = ph.sb([128, 512], F32)
                rstd = ph.sb([128, 512], F32)

                def prologue(ti):
                    l0, n = own512[ti]
                    j = 1 if l0 == 0 else 0
                    mo = hb
                    P.dma('sp', xt[:, :, :n], fm(Xsrc)[:, :, l0:l0 + n], writes=[xt])
                    P.dma('sp', mo[:, :, :n], mo_view(l0, n), writes=[mo])
                    P.op('act', lambda e: e.activation(out=xt[:, :, :n], in_=xt[:, :, :n], func=AF.Copy, scale=ALPHA),
                         reads=[xt], writes=[xt])
                    for m in range(8):
                        P.op('dve', lambda e, m=m: e.scalar_tensor_tensor(
                            out=xt[:, m, :n], in0=mo[:, m, :n], scalar=modT[:, l, 2 * 8 + m, j:j + 1], in1=xt[:, m, :n],
                            op0=ALU.mult, op1=ALU.add), reads=[mo, modT, xt], writes=[xt])
                    layer_norm2(xt, n, l, 0, sg, mean_sb, rstd)
                    modulate(xt, hb, n, l, 3, 4, j)

                prologue(0)
                for ti, (l0, n) in enumerate(own512):
                    j = 1 if l0 == 0 else 0
                    for jb in range(NJ):
                        pg, pu = PB[(jb % 2) * 2], PB[(jb % 2) * 2 + 1]
                        for k in range(8):
                            P.op('pe', lambda e, k=k, jb=jb, pg=pg: e.matmul(
                                pg[:, :n], lhsT=win[:, k, jb * 128:(jb + 1) * 128], rhs=hb[:, k, :n],
                                start=(k == 0), stop=(k == 7)), reads=[win, hb], writes=[pg])
                        for k in range(8):
                            P.op('pe', lambda e, k=k, jb=jb, pu=pu: e.matmul(
                                pu[:, :n], lhsT=win[:, k, FFH + jb * 128:FFH + (jb + 1) * 128], rhs=hb[:, k, :n],
                                start=(k == 0), stop=(k == 7)), reads=[win, hb], writes=[pu])
                        sgt = sg[jb % 2]
                        P.op('act', lambda e, pg=pg, sgt=sgt: e.activation(out=sgt[:, :n], in_=pg[:, :n], func=AF.Silu),
                             reads=[pg], writes=[sgt])
                        P.op('dve', lambda e, pu=pu, sgt=sgt, jb=jb: e.tensor_tensor(
                            out=actb[:, jb, :n], in0=pu[:, :n], in1=sgt[:, :n], op=ALU.mult),
                            reads=[pu, sgt], writes=[actb])
                    P.op('act', lambda e: e.activation(out=x1a[:, :, :n], in_=xt[:, :, :n], func=AF.Copy, scale=ALPHA),
                         reads=[xt], writes=[x1a])
                    if ti + 1 < len(own512):
                        prologue(ti + 1)
                    for m in range(8):
                        po = PB[4 + m % 2]
                        for jb in range(NJ):
                            P.op('pe', lambda e, m=m, jb=jb, po=po: e.matmul(
                                po[:, :n], lhsT=wout[:, jb, m * 128:(m + 1) * 128], rhs=actb[:, jb, :n],
                                start=(jb == 0), stop=(jb == NJ - 1)), reads=[wout, actb], writes=[po])
                        P.op('dve', lambda e, m=m, po=po: e.scalar_tensor_tensor(
                            out=x1a[:, m, :n], in0=po[:, :n], scalar=modT[:, l, 5 * 8 + m, j:j + 1], in1=x1a[:, m, :n],
                            op0=ALU.mult, op1=ALU.add), reads=[po, modT, x1a], writes=[x1a])
                    layer_norm2(x1a, n, l, 1, sg, mean_sb, rstd)
                    P.dma('act', fm(Xdst)[:, :, l0:l0 + n], x1a[:, :, :n], reads=[x1a])

        def load_h(ht, t, l0, n):
            P.dma('sp', ht[:, :, :n], Hg[:, t].rearrange("k p s -> p k s")[:, :, l0:l0 + n], writes=[ht])

        def even_proj_phase(ee):
            with Phase() as ph:
                w = load_w(ph, ev_w_in[ee], D, EVC)
                hb = [ph.sb([128, 8, 512], BF16) for _ in range(2)]
                st_gq = [ph.sb([128, 512], BF16) for _ in range(2)]
                st_gk = [ph.sb([128, 512], BF16) for _ in range(2)]
                st_g = [ph.sb([128, 2, 512], BF16) for _ in range(2)]
                st_r = [ph.sb([32, 512], BF16) for _ in range(2)]
                st_dq = [ph.sb([128, 2, 512], BF16) for _ in range(2)]
                st_dk = [ph.sb([128, 2, 512], BF16) for _ in range(2)]
                st_t = [ph.sb([128, 4, 640], BF16) for _ in range(2)]
                cs = [ph.sb([128, 512], F32) for _ in range(2)]
                sn = [ph.sb([128, 512], F32) for _ in range(2)]
                qb = [ph.sb([128, 512], BF16) for _ in range(2)]
                t1 = [ph.sb([128, 512], F32) for _ in range(2)]
                t2 = [ph.sb([128, 512], F32) for _ in range(2)]
                blocks = ([('gq', 0, 0, 128), ('gk', 0, 128, 128)] + [('g', i, 512 + i * 128, 128) for i in range(2)]
                          + [('r', 0, 768, 32)] + [('dq', i, 800 + i * 128, 128) for i in range(2)]
                          + [('dk', i, 1056 + i * 128, 128) for i in range(2)])
                bi = 0
                ri = 0
                for ti, (t, l0, n) in enumerate(gtiles):
                    s0 = t * Sh + l0
                    isctx = s0 < CTX
                    b2 = ti % 2
                    ht = hb[b2]
                    load_h(ht, t, l0, n)
                    if not isctx:
                        P.dma('sp', cs[b2][:, :n], cos_d[:, s0 - CTX:s0 - CTX + n], writes=[cs[b2]])
                        P.dma('sp', sn[b2][:, :n], sin_d[:, s0 - CTX:s0 - CTX + n], writes=[sn[b2]])
                    for (kind, idx, c0, M) in blocks:
                        pb = PB[bi % 4]
                        bi += 1
                        for k in range(8):
                            P.op('pe', lambda e, k=k, c0=c0, M=M, pb=pb: e.matmul(
                                pb[0:M, :n], lhsT=w[:, k, c0:c0 + M], rhs=ht[:, k, :n], start=(k == 0), stop=(k == 7)),
                                reads=[w, ht], writes=[pb])
                        if kind == 'gq':
                            dst = st_gq[b2]
                            P.op('act', lambda e, dst=dst, pb=pb: e.activation(
                                out=dst[:, :n], in_=pb[:, :n], func=AF.Copy, scale=0.125), reads=[pb], writes=[dst])
                        elif kind == 'gk':
                            dst = st_gk[b2]
                            P.op('dve', lambda e, dst=dst, pb=pb: e.tensor_copy(out=dst[:, :n], in_=pb[:, :n]),
                                 reads=[pb], writes=[dst])
                        elif kind == 'g':
                            dst = st_g[b2]
                            P.op('act', lambda e, dst=dst, idx=idx, pb=pb: e.activation(
                                out=dst[:, idx, :n], in_=pb[:, :n], func=AF.Silu), reads=[pb], writes=[dst])
                        elif kind == 'r':
                            dst = st_r[b2]
                            P.op('dve', lambda e, dst=dst, pb=pb: e.tensor_copy(out=dst[:, :n], in_=pb[0:32, :n]),
                                 reads=[pb], writes=[dst])
                        else:
                            dst = (st_dq if kind == 'dq' else st_dk)[b2]
                            if isctx:
                                P.op('dve', lambda e, dst=dst, idx=idx, pb=pb: e.tensor_copy(out=dst[:, idx, :n], in_=pb[:, :n]),
                                     reads=[pb], writes=[dst])
                            else:
                                r2 = ri % 2
                                ri += 1
                                pp = PB[4 + r2]
                                P.op('act', lambda e, r2=r2, pb=pb: e.activation(out=qb[r2][:, :n], in_=pb[:, :n], func=AF.Copy),
                                     reads=[pb], writes=[qb[r2]])
                                P.op('pe', lambda e, r2=r2, pp=pp: e.matmul(pp[:, :n], lhsT=cmb[:, C_PERM, :], rhs=qb[r2][:, :n],
                                                                          start=True, stop=True), reads=[cmb, qb[r2]], writes=[pp])
                                P.op('pool', lambda e, r2=r2: e.tensor_tensor(out=t1[r2][:, :n], in0=qb[r2][:, :n], in1=cs[b2][:, :n],
                                                                             op=ALU.mult), reads=[qb[r2], cs[b2]], writes=[t1[r2]])
                                P.op('dve', lambda e, r2=r2, pp=pp: e.tensor_tensor(out=t2[r2][:, :n], in0=pp[:, :n], in1=sn[b2][:, :n],
                                                                                   op=ALU.mult), reads=[pp, sn[b2]], writes=[t2[r2]])
                                P.op('pool', lambda e, r2=r2, dst=dst, idx=idx: e.tensor_tensor(
                                    out=dst[:, idx, :n], in0=t1[r2][:, :n], in1=t2[r2][:, :n], op=ALU.add),
                                    reads=[t1[r2], t2[r2]], writes=[dst])
                    stt = st_t[b2]
                    for sub in range(n // 128):
                        for (c0, N, o0, pbi) in ((128, 128, 0, 6), (256, 256, 128, 7), (1312, 256, 384, 6)):
                            pb = PB[pbi]
                            for k in range(8):
                                P.op('pe', lambda e, k=k, c0=c0, N=N, pb=pb, sub=sub: e.matmul(
                                    pb[:, :N], lhsT=ht[:, k, sub * 128:(sub + 1) * 128], rhs=w[:, k, c0:c0 + N],
                                    start=(k == 0), stop=(k == 7)), reads=[w, ht], writes=[pb])
                            if pbi == 7:
                                P.op('act', lambda e, pb=pb, sub=sub, o0=o0, N=N: e.activation(
                                    out=stt[:, sub, o0:o0 + N], in_=pb[:, :N], func=AF.Copy), reads=[pb], writes=[stt])
                            else:
                                P.op('dve', lambda e, pb=pb, sub=sub, o0=o0, N=N: e.tensor_copy(
                                    out=stt[:, sub, o0:o0 + N], in_=pb[:, :N]), reads=[pb], writes=[stt])
                    P.dma('pool', QG[:, s0:s0 + n], st_gq[b2][:, :n], reads=[st_gq[b2]])
                    P.dma('pool', KG[:, s0:s0 + n], st_gk[b2][:, :n], reads=[st_gk[b2]])
                    P.dma('pool', GG.rearrange("(i p) s -> p i s", p=128)[:, :, s0:s0 + n], st_g[b2][:, :, :n], reads=[st_g[b2]])
                    P.dma('pool', RR[:, s0:s0 + n], st_r[b2][:, :n], reads=[st_r[b2]])
                    P.dma('pool', QD.rearrange("(i p) s -> p i s", p=128)[:, :, s0:s0 + n], st_dq[b2][:, :, :n], reads=[st_dq[b2]])
                    P.dma('pool', KD.rearrange("(i p) s -> p i s", p=128)[:, :, s0:s0 + n], st_dk[b2][:, :, :n], reads=[st_dk[b2]])
                    c0i, nsub = s0 // 128, n // 128
                    P.dma('pool', KGt.rearrange("(c p) f -> p c f", p=128)[:, c0i:c0i + nsub, :], stt[:, :nsub, 0:128], reads=[stt])
                    P.dma('pool', VGt.rearrange("(c p) f -> p c f", p=128)[:, c0i:c0i + nsub, :], stt[:, :nsub, 128:384], reads=[stt])
                    P.dma('pool', VDt.rearrange("(c p) f -> p c f", p=128)[:, c0i:c0i + nsub, :], stt[:, :nsub, 384:640], reads=[stt])

        def gla_phase(ee, d):
            order = list(range(NCH)) if d == 0 else [1, 0] + list(range(NCH - 1, 1, -1))
            TRI, STRI, MASK = (C_TRIF, C_STRIF, C_MASKF) if d == 0 else (C_TRIB, C_STRIB, C_MASKB)
            li = 127 if d == 0 else 0
            H = 2
            with Phase() as ph:
                cmf = load_cmf(ph)
                wg = ph.sb([17, 128], BF16)
                P.dma('pool', wg[:], gla_wg[ee, :, d, :], writes=[wg])
                mask4 = ph.sb([128, H, 128], F32)
                P.op('dve', lambda e: e.tensor_copy(out=mask4[:], in_=cmf[:, MASK:MASK + 1, :].to_broadcast([128, H, 128])),
                     reads=[cmf], writes=[mask4])
                Sf = ph.sb([64, H, 128], F32)
                Sb = ph.sb([64, H, 128], BF16)
                P.op('dve', lambda e: e.memset(Sf[:], 0.0), writes=[Sf])
                P.op('dve', lambda e: e.memset(Sb[:], 0.0), writes=[Sb])
                NB = 3
                rt = [ph.sb([17, 128], BF16) for _ in range(NB)]
                for tt in rt:
                    P.op('dve', lambda e, tt=tt: e.memset(tt[:], 1.0), writes=[tt])
                qT = [ph.sb([64, H, 128], BF16) for _ in range(NB)]
                kT = [ph.sb([64, H, 128], BF16) for _ in range(NB)]
                ktok = [ph.sb([128, 128], BF16) for _ in range(NB)]
                vtok = [ph.sb([128, 256], BF16) for _ in range(NB)]
                ofw = [ph.sb([128, H, 128], F32) for _ in range(NB)]
                ez = [ph.sb([128, 128], F32) for _ in range(NB)]
                spb = [ph.sb([128, 128], BF16) for _ in range(NB)]
                Ep = [ph.sb([64, H, 128], F32) for _ in range(NB)]
                Em = [ph.sb([64, H, 128], F32) for _ in range(NB)]
                Ee = [ph.sb([128, 128], F32) for _ in range(NB)]
                qd = [ph.sb([64, H, 128], BF16) for _ in range(NB)]
                kd = [ph.sb([64, H, 128], BF16) for _ in range(NB)]
                kl = [ph.sb([128, 128], BF16) for _ in range(NB)]
                attm = [ph.sb([128, H, 128], BF16) for _ in range(NB)]
                ot = [ph.sb([128, H, 128], F32) for _ in range(NB)]
                OGv = OG.rearrange("(h p) s -> p h s", p=128)
                def stage_a(ci):
                    c = order[ci]
                    b = ci % NB
                    p0 = (ci % 2) * 4
                    tok = slice(c * 128, (c + 1) * 128)
                    P.dma('sp', rt[b][0:16, :], RR[16 * d:16 * d + 16, tok], writes=[rt[b]])
                    P.dma('sp', qT[b][:], QG.rearrange("(h p) s -> p h s", p=64)[:, :, tok], writes=[qT[b]])
                    P.dma('sp', kT[b][:], KG.rearrange("(h p) s -> p h s", p=64)[:, :, tok], writes=[kT[b]])
                    P.dma('sp', ktok[b][:], KGt[tok, :], writes=[ktok[b]])
                    P.dma('sp', vtok[b][:], VGt[tok, :], writes=[vtok[b]])
                    if d == 1:
                        P.dma('sp', ofw[b][:], OGv[:, :, tok], writes=[ofw[b]])
                    pz, pbt, patt, po = PB[p0], PB[p0 + 1], PB[p0 + 2], PB[p0 + 3]
                    P.op('pe', lambda e, b=b: e.matmul(pz[:, 0:128], lhsT=rt[b][:], rhs=wg[:], start=True, stop=True),
                         reads=[rt[b], wg], writes=[pz])
                    P.op('act', lambda e, b=b: e.activation(out=ez[b][:], in_=pz[:, 0:128], func=AF.Exp, scale=-1.0),
                         reads=[pz], writes=[ez[b]])
                    P.op('act', lambda e, b=b: e.activation(out=spb[b][:], in_=ez[b][:], func=AF.Ln, bias=1.0, scale=1.0),
                         reads=[ez[b]], writes=[spb[b]])
                    for h in range(H):
                        P.op('pe', lambda e, b=b, h=h: e.matmul(pbt[0:64, h * 128:(h + 1) * 128], lhsT=spb[b][:, h * 64:(h + 1) * 64],
                                                               rhs=cmb[:, TRI, :], start=True, stop=True),
                             reads=[spb[b], cmb], writes=[pbt])
                    P.op('pe', lambda e, b=b: e.matmul(pz[:, 128:256], lhsT=cmb[:, STRI, :], rhs=spb[b][:], start=True, stop=True),
                         reads=[spb[b], cmb], writes=[pz])
                    pb1v = pbt[0:64, 0:H * 128].rearrange("p (h t) -> p h t", h=H)
                    P.op('act', lambda e, b=b: e.activation(out=Ep[b][:], in_=pb1v, func=AF.Exp), reads=[pbt], writes=[Ep[b]])
                    P.op('act', lambda e, b=b: e.activation(out=Em[b][:], in_=pb1v, func=AF.Exp, scale=-1.0),
                         reads=[pbt], writes=[Em[b]])
                    P.op('act', lambda e, b=b: e.activation(out=Ee[b][:], in_=pz[:, 128:256], func=AF.Exp),
                         reads=[pz], writes=[Ee[b]])
                    P.op('dve', lambda e, b=b: e.tensor_tensor(out=qd[b][:], in0=Ep[b][:], in1=qT[b][:], op=ALU.mult),
                         reads=[Ep[b], qT[b]], writes=[qd[b]])
                    P.op('dve', lambda e, b=b: e.tensor_tensor(out=kd[b][:], in0=Em[b][:], in1=kT[b][:], op=ALU.mult),
                         reads=[Em[b], kT[b]], writes=[kd[b]])
                    P.op('pool', lambda e, b=b: e.tensor_tensor(out=kl[b][:], in0=Ee[b][:], in1=ktok[b][:], op=ALU.mult),
                         reads=[Ee[b], ktok[b]], writes=[kl[b]])
                    for h in range(H):
                        P.op('pe', lambda e, b=b, h=h: e.matmul(patt[:, h * 128:(h + 1) * 128], lhsT=kd[b][:, h, :], rhs=qd[b][:, h, :],
                                                               start=True, stop=True), reads=[kd[b], qd[b]], writes=[patt])
                    P.op('dve', lambda e, b=b: e.tensor_tensor(out=attm[b][:], in0=patt[:, 0:H * 128].rearrange("p (h t) -> p h t", h=H),
                                                              in1=mask4[:], op=ALU.mult), reads=[patt, mask4], writes=[attm[b]])

                def stage_b(ci):
                    c = order[ci]
                    b = ci % NB
                    p0 = (ci % 2) * 4
                    tok = slice(c * 128, (c + 1) * 128)
                    pz, po = PB[p0], PB[p0 + 3]
                    for h in range(H):
                        P.op('pe', lambda e, b=b, h=h: e.matmul(po[:, h * 128:(h + 1) * 128], lhsT=vtok[b][:, h * 128:(h + 1) * 128],
                                                               rhs=attm[b][:, h, :], start=True, stop=False),
                             reads=[vtok[b], attm[b]], writes=[po])
                        P.op('pe', lambda e, b=b, h=h: e.matmul(po[:, h * 128:(h + 1) * 128], lhsT=Sb[:, h, :],
                                                               rhs=qd[b][:, h, :], start=False, stop=True),
                             reads=[Sb, qd[b]], writes=[po])
                    pb4v = po[:, 0:H * 128].rearrange("p (h t) -> p h t", h=H)
                    if d == 0:
                        P.op('act', lambda e, b=b: e.activation(out=ot[b][:], in_=pb4v, func=AF.Copy), reads=[po], writes=[ot[b]])
                    else:
                        P.op('pool' if False else 'dve', lambda e, b=b: e.tensor_tensor(out=ot[b][:], in0=pb4v, in1=ofw[b][:], op=ALU.add),
                             reads=[po, ofw[b]], writes=[ot[b]])
                    P.dma('act', OGv[:, :, tok], ot[b][:], reads=[ot[b]])
                    for h in range(H):
                        P.op('pe', lambda e, b=b, h=h: e.matmul(pz[0:64, 256 + h * 128:256 + (h + 1) * 128], lhsT=kl[b][:, h * 64:(h + 1) * 64],
                                                               rhs=vtok[b][:, h * 128:(h + 1) * 128], start=True, stop=True),
                             reads=[kl[b], vtok[b]], writes=[pz])
                    P.op('dve', lambda e, b=b: e.tensor_tensor(out=Sf[:], in0=Sf[:],
                                                              in1=Ep[b][:, :, li:li + 1].to_broadcast([64, H, 128]), op=ALU.mult),
                         reads=[Sf, Ep[b]], writes=[Sf])
                    P.op('dve', lambda e: e.tensor_tensor(out=Sf[:], in0=Sf[:], in1=pz[0:64, 256:512].rearrange("p (h t) -> p h t", h=H),
                                                          op=ALU.add), reads=[Sf, pz], writes=[Sf])
                    P.op('act', lambda e: e.activation(out=Sb[:], in_=Sf[:], func=AF.Copy), reads=[Sf], writes=[Sb])

                stage_a(0)
                for ci in range(len(order)):
                    if ci + 1 < len(order):
                        stage_a(ci + 1)
                    stage_b(ci)

        def attn_phase(ee):
            with Phase() as ph:
                cmf = load_cmf(ph)
                kTh = [ph.sb([128, S], BF16) for _ in range(2)]
                vh = [ph.sb([128, NCH, 128], BF16) for _ in range(2)]
                qTt = [ph.sb([128, 512], BF16) for _ in range(2)]
                pt = [ph.sb([128, 2, 512], BF16) for _ in range(3)]
                acc = [ph.sb([128, 512], F32) for _ in range(2)]
                r1 = ph.sb([128, 512], F32)
                r2 = ph.sb([128, 512], F32)
                o1 = ph.sb([128, 512], F32)
                oa = ph.sb([128, 512], F32)
                ob_ = ph.sb([128, 512], F32)
                osq = ph.sb([128, 512], BF16)
                dto = [ph.sb([128, 512], BF16) for _ in range(2)]
                SB = [(0, 1), (2, 3)]
                qi = 0
                for h in range(2):
                    P.dma('sp', kTh[h][:], KD[h * 128:(h + 1) * 128, :], writes=[kTh[h]])
                    P.dma('act', vh[h][:], VDt.rearrange("(c p) f -> p c f", p=128)[:, :, h * 128:(h + 1) * 128], writes=[vh[h]])
                qtiles = [(h, t, l0, n) for h in range(2) for (t, l0, n) in gtiles]

                def issue_q(i):
                    h_, t_, l0_, n_ = qtiles[i]
                    s0_ = t_ * Sh + l0_
                    P.dma('sp', qTt[i % 2][:, :n_], QD[h_ * 128:(h_ + 1) * 128, s0_:s0_ + n_], writes=[qTt[i % 2]])

                issue_q(0)
                for qi, (h, t, l0, n) in enumerate(qtiles):
                    if True:
                        kT_, v_ = kTh[h], vh[h]
                        s0 = t * Sh + l0
                        q2 = qi % 2
                        qt = qTt[q2]
                        if qi + 1 < len(qtiles):
                            issue_q(qi + 1)
                        nkb = 2 if s0 < CTX else NCH

                        def smm(kb):
                            for m in range(2):
                                pb = PB[SB[kb % 2][m]]
                                P.op('pe', lambda e, m=m, pb=pb, kb=kb: e.matmul(
                                    pb[:, :n], lhsT=kT_[m * 64:(m + 1) * 64, kb * 128:(kb + 1) * 128],
                                    rhs=qt[m * 64:(m + 1) * 64, :n], start=True, stop=True), reads=[kT_, qt], writes=[pb])
                        smm(0)
                        if nkb > 1:
                            smm(1)
                        for kb in range(nkb):
                            p3 = kb % 3
                            b0 = SB[kb % 2][0]
                            P.op('act', lambda e, b0=b0, p3=p3: e.activation(
                                out=pt[p3][:, :, :n], in_=psum[:, b0:b0 + 2, :n], func=AF.Exp, scale=0.125),
                                reads=[PB[b0], PB[b0 + 1]], writes=[pt[p3]])
                            if kb + 2 < nkb:
                                smm(kb + 2)
                            for m in range(2):
                                P.op('pe', lambda e, m=m, p3=p3, kb=kb: e.matmul(
                                    PB[4 + m][:, :n], lhsT=v_[:, kb, :], rhs=pt[p3][:, m, :n],
                                    start=(kb == 0), stop=(kb == nkb - 1)), reads=[v_, pt[p3]], writes=[PB[4 + m]])
                            P.op('pe', lambda e, p3=p3, kb=kb: e.matmul(
                                PB[7][:, :n], lhsT=cmb[:, C_ONES, :], rhs=pt[p3][:, 0, :n],
                                start=(kb == 0), stop=(kb == nkb - 1)), reads=[cmb, pt[p3]], writes=[PB[7]])
                            hn = n // 2
                            for ai, (eng, c0, c1) in enumerate((('dve', 0, hn), ('pool', hn, n))):
                                if kb == 0:
                                    P.op(eng, lambda e, p3=p3, c0=c0, c1=c1, ai=ai: e.tensor_copy(out=acc[ai][:, c0:c1], in_=pt[p3][:, 1, c0:c1]),
                                         reads=[pt[p3]], writes=[acc[ai]])
                                else:
                                    P.op(eng, lambda e, p3=p3, c0=c0, c1=c1, ai=ai: e.tensor_tensor(
                                        out=acc[ai][:, c0:c1], in0=acc[ai][:, c0:c1], in1=pt[p3][:, 1, c0:c1], op=ALU.add),
                                        reads=[pt[p3], acc[ai]], writes=[acc[ai]])
                        for ai, (c0, c1) in enumerate(((0, n // 2), (n // 2, n))):
                            P.op('pe', lambda e, ai=ai, c0=c0, c1=c1: e.matmul(PB[6][:, c0:c1], lhsT=cmf[:, C_ONES, :], rhs=acc[ai][:, c0:c1],
                                                                              start=True, stop=True), reads=[cmf, acc[ai]], writes=[PB[6]])
                        P.op('act', lambda e: e.activation(out=r1[:, :n], in_=PB[7][:, :n], func=AF.Copy), reads=[PB[7]], writes=[r1])
                        P.op('dve', lambda e: e.tensor_copy(out=oa[:, :n], in_=PB[4][:, :n]), reads=[PB[4]], writes=[oa])
                        P.op('act', lambda e: e.activation(out=ob_[:, :n], in_=PB[5][:, :n], func=AF.Copy), reads=[PB[5]], writes=[ob_])
                        P.op('dve', lambda e: e.reciprocal(out=r1[:, :n], in_=r1[:, :n]), reads=[r1], writes=[r1])
                        P.op('dve', lambda e: e.reciprocal(out=r2[:, :n], in_=PB[6][:, :n]), reads=[PB[6]], writes=[r2])
                        P.op('dve', lambda e: e.tensor_tensor(out=r1[:, :n], in0=oa[:, :n], in1=r1[:, :n], op=ALU.mult),
                             reads=[oa, r1], writes=[r1])
                        P.op('pool', lambda e: e.tensor_tensor(out=r2[:, :n], in0=ob_[:, :n], in1=r2[:, :n], op=ALU.mult),
                             reads=[ob_, r2], writes=[r2])
                        P.op('dve', lambda e: e.scalar_tensor_tensor(out=o1[:, :n], in0=r2[:, :n], scalar=nlam[:, ee:ee + 1],
                                                                     in1=r1[:, :n], op0=ALU.mult, op1=ALU.add),
                             reads=[r1, r2, nlam], writes=[o1])
                        P.op('pool', lambda e: e.tensor_tensor(out=osq[:, :n], in0=o1[:, :n], in1=o1[:, :n], op=ALU.mult),
                             reads=[o1], writes=[osq])
                        P.op('pe', lambda e: e.matmul(PB[6][:, :n], lhsT=cmb[:, C_ONES, :], rhs=osq[:, :n], start=True, stop=True),
                             reads=[cmb, osq], writes=[PB[6]])
                        P.op('act', lambda e: e.activation(out=r1[:, :n], in_=PB[6][:, :n], func=AF.Sqrt, bias=RMS_EPS, scale=1.0 / 128),
                             reads=[PB[6]], writes=[r1])
                        P.op('dve', lambda e: e.reciprocal(out=r1[:, :n], in_=r1[:, :n]), reads=[r1], writes=[r1])
                        dd = dto[q2]
                        P.op('dve', lambda e, dd=dd: e.scalar_tensor_tensor(out=dd[:, :n], in0=o1[:, :n], scalar=dnws[:, ee:ee + 1],
                                                                            in1=r1[:, :n], op0=ALU.mult, op1=ALU.mult),
                             reads=[o1, r1, dnws], writes=[dd])
                        P.dma('sp', DT[h * 128:(h + 1) * 128, s0:s0 + n], dd[:, :n], reads=[dd])

        def even_out_phase(ee):
            with Phase() as ph:
                wo = load_w(ph, ev_w_out[ee], 512, D)
                og = [ph.sb([128, 2, 512], F32) for _ in range(2)]
                gg = [ph.sb([128, 2, 512], BF16) for _ in range(2)]
                mix = [ph.sb([128, 4, 512], BF16) for _ in range(2)]
                sq = ph.sb([128, 2, 512], BF16)
                rs = ph.sb([128, 2, 512], F32)
                ob = [ph.sb([128, 8, 512], BF16) for _ in range(2)]
                for ti, (c, t, l0, n) in enumerate(ctiles):
                    s0 = t * Sh + l0
                    b2 = ti % 2
                    P.dma('sp', og[b2][:, :, :n], OG.rearrange("(h p) s -> p h s", p=128)[:, :, s0:s0 + n], writes=[og[b2]])
                    P.dma('sp', gg[b2][:, :, :n], GG.rearrange("(h p) s -> p h s", p=128)[:, :, s0:s0 + n], writes=[gg[b2]])
                    P.dma('sp', mix[b2][:, 2:4, :n], DT.rearrange("(h p) s -> p h s", p=128)[:, :, s0:s0 + n], writes=[mix[b2]])
                    P.op('dve', lambda e, b2=b2: e.tensor_tensor(out=sq[:, :, :n], in0=og[b2][:, :, :n], in1=og[b2][:, :, :n], op=ALU.mult),
                         reads=[og[b2]], writes=[sq])
                    for h in range(2):
                        P.op('pe', lambda e, h=h: e.matmul(PB[h][:, :n], lhsT=cmb[:, C_ONES, :], rhs=sq[:, h, :n], start=True, stop=True),
                             reads=[cmb, sq], writes=[PB[h]])
                    P.op('act', lambda e: e.activation(out=rs[:, :, :n], in_=psum[:, 0:2, :n], func=AF.Sqrt, bias=RMS_EPS, scale=1.0 / 128),
                         reads=[PB[0], PB[1]], writes=[rs])
                    P.op('dve', lambda e: e.reciprocal(out=rs[:, :, :n], in_=rs[:, :, :n]), reads=[rs], writes=[rs])
                    P.op('dve', lambda e, b2=b2: e.tensor_tensor(out=rs[:, :, :n], in0=rs[:, :, :n], in1=og[b2][:, :, :n], op=ALU.mult),
                         reads=[rs, og[b2]], writes=[rs])
                    P.op('dve', lambda e, b2=b2: e.scalar_tensor_tensor(out=mix[b2][:, 0:2, :n], in0=rs[:, :, :n], scalar=gnw[:, ee:ee + 1],
                                                                       in1=gg[b2][:, :, :n], op0=ALU.mult, op1=ALU.mult),
                         reads=[rs, gnw, gg[b2]], writes=[mix[b2]])
                    o_ = ob[b2]
                    for m in range(8):
                        po = PB[4 + m % 4]
                        for k in range(4):
                            P.op('pe', lambda e, m=m, k=k, po=po, b2=b2: e.matmul(
                                po[:, :n], lhsT=wo[:, k, m * 128:(m + 1) * 128], rhs=mix[b2][:, k, :n],
                                start=(k == 0), stop=(k == 3)), reads=[wo, mix[b2]], writes=[po])
                        if m % 2 == 0:
                            P.op('act', lambda e, m=m, po=po: e.activation(out=o_[:, m, :n], in_=po[:, :n], func=AF.Copy),
                                 reads=[po], writes=[o_])
                        else:
                            P.op('dve', lambda e, m=m, po=po: e.tensor_copy(out=o_[:, m, :n], in_=po[:, :n]), reads=[po], writes=[o_])
                    P.dma('act', Ppc[c][t].rearrange("k p s -> p k s"), o_[:, :, :n], reads=[o_], writes=[ppres[c]])
                    if t == 1:
                        rs_chunk(c)

        def odd_proj_phase(oo):
            with Phase() as ph:
                w = load_w(ph, ssd_w_in[oo], D, SDC)
                hb = [ph.sb([128, 8, 512], BF16) for _ in range(2)]
                st_x = [ph.sb([128, 12, 512], BF16) for _ in range(2)]
                st_z = [ph.sb([128, 4, 1024], BF16) for _ in range(2)]
                st_dt = [ph.sb([128, 4, 32], F32) for _ in range(2)]
                dtb = ph.sb([128, 32], F32)
                P.dma('sp', dtb[:], ssd_dtb[oo], writes=[dtb])
                bi = 0
                for ti, (t, l0, n) in enumerate(gtiles):
                    s0 = t * Sh + l0
                    b2 = ti % 2
                    ht = hb[b2]
                    stx, stz, stdt = st_x[b2], st_z[b2], st_dt[b2]
                    load_h(ht, t, l0, n)
                    for i in range(12):
                        pb = PB[bi % 4]
                        bi += 1
                        for k in range(8):
                            P.op('pe', lambda e, k=k, i=i, pb=pb: e.matmul(
                                pb[:, :n], lhsT=w[:, k, 1024 + i * 128:1024 + (i + 1) * 128], rhs=ht[:, k, :n],
                                start=(k == 0), stop=(k == 7)), reads=[w, ht], writes=[pb])
                        if i % 2 == 0:
                            P.op('act', lambda e, i=i, pb=pb: e.activation(out=stx[:, i, :n], in_=pb[:, :n], func=AF.Copy),
                                 reads=[pb], writes=[stx])
                        else:
                            P.op('dve', lambda e, i=i, pb=pb: e.tensor_copy(out=stx[:, i, :n], in_=pb[:, :n]),
                                 reads=[pb], writes=[stx])
                    nsub = n // 128
                    for sub in range(nsub):
                        for zb in range(2):
                            pb = PB[4 + zb % 2]
                            for k in range(8):
                                P.op('pe', lambda e, k=k, zb=zb, pb=pb, sub=sub: e.matmul(
                                    pb[:, :512], lhsT=ht[:, k, sub * 128:(sub + 1) * 128], rhs=w[:, k, zb * 512:(zb + 1) * 512],
                                    start=(k == 0), stop=(k == 7)), reads=[w, ht], writes=[pb])
                            P.op('act', lambda e, zb=zb, pb=pb, sub=sub: e.activation(
                                out=stz[:, sub, zb * 512:(zb + 1) * 512], in_=pb[:, :512], func=AF.Silu), reads=[pb], writes=[stz])
                        for k in range(8):
                            P.op('pe', lambda e, k=k, sub=sub: e.matmul(
                                PB[6][:, 0:32], lhsT=ht[:, k, sub * 128:(sub + 1) * 128], rhs=w[:, k, 2560:2592],
                                start=(k == 0), stop=(k == 7)), reads=[w, ht], writes=[PB[6]])
                        P.op('dve', lambda e, sub=sub: e.tensor_tensor(out=stdt[:, sub, :], in0=PB[6][:, 0:32], in1=dtb[:], op=ALU.add),
                             reads=[PB[6], dtb], writes=[stdt])
                    P.op('act', lambda e: e.activation(out=stdt[:, :nsub, :], in_=stdt[:, :nsub, :], func=AF.Exp),
                         reads=[stdt], writes=[stdt])
                    P.op('act', lambda e: e.activation(out=stdt[:, :nsub, :], in_=stdt[:, :nsub, :], func=AF.Ln, bias=1.0, scale=1.0),
                         reads=[stdt], writes=[stdt])
                    c0i = s0 // 128
                    P.dma('pool', XBC.rearrange("(i p) s -> p i s", p=128)[:, :, s0:s0 + n], stx[:, :, :n], reads=[stx])
                    P.dma('pool', ZS.rearrange("(c p) f -> p c f", p=128)[:, c0i:c0i + nsub, :], stz[:, :nsub, :], reads=[stz])
                    P.dma('pool', DTt.rearrange("(c p) f -> p c f", p=128)[:, c0i:c0i + nsub, :], stdt[:, :nsub, :], reads=[stdt])

        def conv_phase(oo):
            NBK = 12
            with Phase() as ph:
                cmf = load_cmf(ph)
                cw = ph.sb([128, 5, NBK], F32)
                cb = ph.sb([128, NBK], F32)
                P.dma('sp', cw[:], ssd_cw[oo], writes=[cw])
                P.dma('sp', cb[:], ssd_cb[oo], writes=[cb])
                Dg = ph.sb([128, NBK, 5, 128], BF16)
                for i in range(NBK):
                    for jj in range(5):
                        P.op('dve' if (i + jj) % 2 == 0 else 'pool', lambda e, i=i, jj=jj: e.tensor_scalar_mul(
                            out=Dg[:, i, jj, :], in0=cmf[:, C_IDENT, :], scalar1=cw[:, jj, i:i + 1]), reads=[cmf, cw], writes=[Dg])
                xp = [ph.sb([128, NBK, 516], BF16) for _ in range(2)]
                xc = [ph.sb([128, NBK, 512], BF16) for _ in range(2)]
                stt = [ph.sb([128, 4, 1280], BF16) for _ in range(2)]
                bi = 0
                for ti, (t, l0, n) in enumerate(gtiles):
                    s0 = t * Sh + l0
                    b2 = ti % 2
                    x_p, x_c = xp[b2], xc[b2]
                    lo = 0 if (s0 == 0 or s0 == CTX) else 2
                    hi = 0 if (s0 + n == CTX or s0 + n == S) else 2
                    if lo == 0:
                        P.op('pool', lambda e, x_p=x_p: e.memset(x_p[:, :, 0:2], 0.0), writes=[x_p])
                    if hi == 0:
                        P.op('pool', lambda e, x_p=x_p: e.memset(x_p[:, :, n + 2:n + 4], 0.0), writes=[x_p])
                    P.dma('sp', x_p[:, :, 2 - lo:n + 2 + hi], XBC.rearrange("(i p) s -> p i s", p=128)[:, :, s0 - lo:s0 + n + hi],
                          writes=[x_p])
                    for i in range(NBK):
                        pb = PB[bi % 4]
                        bi += 1
                        for jj in range(5):
                            P.op('pe', lambda e, i=i, jj=jj, pb=pb: e.matmul(
                                pb[:, :n], lhsT=Dg[:, i, jj, :], rhs=x_p[:, i, jj:jj + n], start=(jj == 0), stop=(jj == 4)),
                                reads=[Dg, x_p], writes=[pb])
                        P.op('act', lambda e, i=i, pb=pb: e.activation(out=x_c[:, i, :n], in_=pb[:, :n], func=AF.Silu,
                                                                      bias=cb[:, i:i + 1], scale=1.0), reads=[pb, cb], writes=[x_c])
                    P.dma('pool', BTd.rearrange("(i p) s -> p i s", p=128)[:, :, s0:s0 + n], x_c[:, 8:10, :n], reads=[x_c])
                    P.dma('pool', CTd.rearrange("(i p) s -> p i s", p=128)[:, :, s0:s0 + n], x_c[:, 10:12, :n], reads=[x_c])
                    st = stt[b2]
                    nsub = n // 128
                    qq = 0
                    for sub in range(nsub):
                        for (i0, ni) in ((0, 4), (4, 4), (8, 2)):
                            pb = PB[4 + qq % 4]
                            qq += 1
                            for u in range(ni):
                                i = i0 + u
                                P.op('pe', lambda e, i=i, u=u, pb=pb, sub=sub: e.matmul(
                                    pb[:, u * 128:(u + 1) * 128], lhsT=x_c[:, i, sub * 128:(sub + 1) * 128], rhs=cmb[:, C_IDENT, :],
                                    start=True, stop=True), reads=[x_c, cmb], writes=[pb])
                            if qq % 2 == 0:
                                P.op('dve', lambda e, i0=i0, ni=ni, pb=pb, sub=sub: e.tensor_copy(
                                    out=st[:, sub, i0 * 128:(i0 + ni) * 128], in_=pb[:, 0:ni * 128]), reads=[pb], writes=[st])
                            else:
                                P.op('act', lambda e, i0=i0, ni=ni, pb=pb, sub=sub: e.activation(
                                    out=st[:, sub, i0 * 128:(i0 + ni) * 128], in_=pb[:, 0:ni * 128], func=AF.Copy), reads=[pb], writes=[st])
                    c0i = s0 // 128
                    P.dma('pool', XSt.rearrange("(c p) f -> p c f", p=128)[:, c0i:c0i + nsub, :], st[:, :nsub, 0:1024], reads=[st])
                    P.dma('pool', Btd.rearrange("(c p) f -> p c f", p=128)[:, c0i:c0i + nsub, :], st[:, :nsub, 1024:1280], reads=[st])

        def ssd_phase(oo, d):
            order = list(range(NCH)) if d == 0 else [1, 0] + list(range(NCH - 1, 1, -1))
            U, SL, MASK = (C_UF, C_SLF, C_MASKF) if d == 0 else (C_UB, C_SLB, C_MASKB)
            NH, NG = 16, 2
            with Phase() as ph:
                cmf = load_cmf(ph)
                abc = ph.sb([128, 32], F32)
                P.dma('sp', abc[:], ssd_alog[oo], writes=[abc])
                P.op('act', lambda e: e.activation(out=abc[:], in_=abc[:], func=AF.Exp), reads=[abc], writes=[abc])
                P.op('dve', lambda e: e.tensor_scalar_mul(out=abc[:], in0=abc[:], scalar1=-1.0), reads=[abc], writes=[abc])
                dsk = ph.sb([128, NH], F32)
                nwb = ph.sb([128, 1024], F32)
                if d == 1:
                    P.dma('sp', dsk[:], ssd_dsk[oo], writes=[dsk])
                    P.dma('sp', nwb[:], ssd_nw[oo], writes=[nwb])
                mask4 = ph.sb([128, NG, 128], F32)
                P.op('dve', lambda e: e.tensor_copy(out=mask4[:], in_=cmf[:, MASK:MASK + 1, :].to_broadcast([128, NG, 128])),
                     reads=[cmf], writes=[mask4])
                Sf = [ph.sb([128, 8, 64], F32) for _ in range(NG)]
                Sb = [ph.sb([128, 8, 64], BF16) for _ in range(NG)]
                for g in range(NG):
                    P.op('dve', lambda e, g=g: e.memset(Sf[g][:], 0.0), writes=[Sf[g]])
                    P.op('pool', lambda e, g=g: e.memset(Sb[g][:], 0.0), writes=[Sb[g]])
                NB = 3
                xtok = [ph.sb([128, NH, 64], BF16) for _ in range(NB)]
                btok = [ph.sb([128, 256], BF16) for _ in range(NB)]
                bT = [ph.sb([128, NG, 128], BF16) for _ in range(NB)]
                cT = [ph.sb([128, NG, 128], BF16) for _ in range(NB)]
                dtt = [ph.sb([128, 32], F32) for _ in range(NB)]
                dA = [ph.sb([128, NH], F32) for _ in range(NB)]
                ex = [ph.sb([128, 3 * NH], F32) for _ in range(NB)]
                wv = [ph.sb([128, NH], F32) for _ in range(NB)]
                xdt = [ph.sb([128, NH, 64], BF16) for _ in range(NB)]
                xw = [ph.sb([128, NH, 64], BF16) for _ in range(NB)]
                rhsD = [ph.sb([128, NH, 128], BF16) for _ in range(NB)]
                cbm = [ph.sb([128, NG, 128], F32) for _ in range(NB)]
                expD = [ph.sb([128, 8, 128], F32) for _ in range(2)]
                G = [ph.sb([128, 8, 128], BF16) for _ in range(2)]
                tmp = [ph.sb([128, 8, 64], F32) for _ in range(2)]
                y = [ph.sb([128, NH, 64], F32) for _ in range(NB)]
                if d == 1:
                    yfw = [ph.sb([128, NH, 64], F32) for _ in range(NB)]
                    zs = [ph.sb([128, 1024], BF16) for _ in range(NB)]
                    t3 = ph.sb([128, NH, 64], F32)
                    sq = ph.sb([128, NG, 512], F32)
                    ss = ph.sb([128, NG], F32)
                    yn = ph.sb([128, 1024], BF16)
                    ynT = [ph.sb([128, 8, 128], BF16) for _ in range(2)]
                gi = [0]

                def stage_a(ci):
                    c = order[ci]
                    b = ci % NB
                    tok = slice(c * 128, (c + 1) * 128)
                    P.dma('sp', xtok[b][:], XSt[tok, :].rearrange("t (h p) -> t h p", p=64), writes=[xtok[b]])
                    P.dma('sp', btok[b][:], Btd[tok, :], writes=[btok[b]])
                    P.dma('sp', bT[b][:], BTd.rearrange("(g p) s -> p g s", p=128)[:, :, tok], writes=[bT[b]])
                    P.dma('sp', cT[b][:], CTd.rearrange("(g p) s -> p g s", p=128)[:, :, tok], writes=[cT[b]])
                    P.dma('sp', dtt[b][:], DTt[tok, :], writes=[dtt[b]])
                    if d == 1:
                        P.dma('sp', yfw[b][:], YF[tok, :].rearrange("t (h p) -> t h p", p=64), writes=[yfw[b]])
                        P.dma('sp', zs[b][:], ZS[tok, :], writes=[zs[b]])
                    dtd = dtt[b][:, NH * d:NH * d + NH]
                    P.op('dve', lambda e, b=b: e.tensor_tensor(out=dA[b][:], in0=dtd, in1=abc[:, NH * d:NH * d + NH], op=ALU.mult),
                         reads=[dtt[b], abc], writes=[dA[b]])
                    P.op('pe', lambda e, b=b: e.matmul(PB[0][:, 0:NH], lhsT=cmf[:, U, :], rhs=dA[b][:], start=True, stop=True),
                         reads=[cmf, dA[b]], writes=[PB[0]])
                    P.op('pe', lambda e, b=b: e.matmul(PB[0][:, NH:2 * NH], lhsT=cmf[:, SL, :], rhs=dA[b][:], start=True, stop=True),
                         reads=[cmf, dA[b]], writes=[PB[0]])
                    P.op('pe', lambda e, b=b: e.matmul(PB[0][:, 2 * NH:3 * NH], lhsT=cmf[:, C_ONES, :], rhs=dA[b][:], start=True, stop=True),
                         reads=[cmf, dA[b]], writes=[PB[0]])
                    P.op('act', lambda e, b=b: e.activation(out=ex[b][:], in_=PB[0][:, 0:3 * NH], func=AF.Exp), reads=[PB[0]], writes=[ex[b]])
                    P.op('dve', lambda e, b=b: e.tensor_tensor(out=wv[b][:], in0=dtd, in1=ex[b][:, NH:2 * NH], op=ALU.mult),
                         reads=[dtt[b], ex[b]], writes=[wv[b]])
                    P.op('pool', lambda e, b=b: e.tensor_tensor(out=xdt[b][:], in0=xtok[b][:], in1=dtd.unsqueeze(2).to_broadcast([128, NH, 64]),
                                                               op=ALU.mult), reads=[xtok[b], dtt[b]], writes=[xdt[b]])
                    P.op('dve', lambda e, b=b: e.tensor_tensor(out=xw[b][:], in0=xtok[b][:], in1=wv[b][:].unsqueeze(2).to_broadcast([128, NH, 64]),
                                                              op=ALU.mult), reads=[xtok[b], wv[b]], writes=[xw[b]])
                    for hh, eng in ((0, 'dve'), (8, 'pool')):
                        P.op(eng, lambda e, b=b, hh=hh: e.tensor_tensor(
                            out=rhsD[b][:, hh:hh + 8, :], in0=cmf[:, U:U + 1, :].to_broadcast([128, 8, 128]),
                            in1=dA[b][:, hh:hh + 8].unsqueeze(2).to_broadcast([128, 8, 128]), op=ALU.mult),
                            reads=[cmf, dA[b]], writes=[rhsD[b]])
                    for g in range(NG):
                        P.op('pe', lambda e, b=b, g=g: e.matmul(PB[1][:, g * 128:(g + 1) * 128], lhsT=bT[b][:, g, :], rhs=cT[b][:, g, :],
                                                               start=True, stop=True), reads=[bT[b], cT[b]], writes=[PB[1]])
                    P.op('dve', lambda e, b=b: e.tensor_tensor(out=cbm[b][:], in0=PB[1][:, 0:NG * 128].rearrange("p (g t) -> p g t", g=NG),
                                                              in1=mask4[:], op=ALU.mult), reads=[PB[1], mask4], writes=[cbm[b]])

                def stage_b(ci):
                    c = order[ci]
                    b = ci % NB
                    tok = slice(c * 128, (c + 1) * 128)
                    g2s = []
                    for g in range(NG):
                        g2 = gi[0] % 2
                        gi[0] += 1
                        g2s.append(g2)
                        d0 = 2 if g2 == 0 else 6
                        for k2 in range(2):
                            P.op('pe', lambda e, b=b, g=g, k2=k2, d0=d0: e.matmul(
                                PB[d0 + k2][:, :], lhsT=cmb[:, SL, :],
                                rhs=rhsD[b][:, g * 8 + k2 * 4:g * 8 + k2 * 4 + 4, :], start=True, stop=True),
                                reads=[cmb, rhsD[b]], writes=[PB[d0 + k2]])
                        P.op('act', lambda e, g2=g2, d0=d0: e.activation(out=expD[g2][:], in_=psum[:, d0:d0 + 2, :].rearrange("p a (h t) -> p (a h) t", h=4),
                                                                        func=AF.Exp), reads=[PB[d0], PB[d0 + 1]], writes=[expD[g2]])
                        P.op('pool' if g % 2 == 0 else 'dve', lambda e, b=b, g=g, g2=g2: e.tensor_tensor(
                            out=G[g2][:], in0=expD[g2][:], in1=cbm[b][:, g:g + 1, :].to_broadcast([128, 8, 128]), op=ALU.mult),
                            reads=[expD[g2], cbm[b]], writes=[G[g2]])
                    for g in range(NG):
                        g2 = g2s[g]
                        P.op('pe', lambda e, b=b, g=g: e.matmul(PB[4][:, :], lhsT=cT[b][:, g, :], rhs=Sb[g][:], start=True, stop=True),
                             reads=[cT[b], Sb[g]], writes=[PB[4]])
                        for h8 in range(8):
                            P.op('pe', lambda e, b=b, g=g, g2=g2, h8=h8: e.matmul(
                                PB[5][:, h8 * 64:(h8 + 1) * 64], lhsT=G[g2][:, h8, :], rhs=xdt[b][:, g * 8 + h8, :],
                                start=True, stop=True), reads=[G[g2], xdt[b]], writes=[PB[5]])
                        P.op('dve', lambda e, b=b, g=g, g2=g2: e.tensor_tensor(
                            out=tmp[g2][:], in0=PB[4][:, :].rearrange("p (h q) -> p h q", q=64),
                            in1=ex[b][:, g * 8:(g + 1) * 8].unsqueeze(2).to_broadcast([128, 8, 64]), op=ALU.mult),
                            reads=[PB[4], ex[b]], writes=[tmp[g2]])
                        P.op('dve', lambda e, b=b, g=g, g2=g2: e.tensor_tensor(
                            out=y[b][:, g * 8:(g + 1) * 8, :], in0=PB[5][:, :].rearrange("p (h q) -> p h q", q=64),
                            in1=tmp[g2][:], op=ALU.add), reads=[PB[5], tmp[g2]], writes=[y[b]])
                        P.op('pe', lambda e, b=b, g=g: e.matmul(PB[4][:, :], lhsT=btok[b][:, g * 128:(g + 1) * 128],
                                                               rhs=xw[b][:, g * 8:(g + 1) * 8, :], start=True, stop=True),
                             reads=[btok[b], xw[b]], writes=[PB[4]])
                        P.op('pool', lambda e, b=b, g=g: e.tensor_tensor(
                            out=Sf[g][:], in0=Sf[g][:], in1=ex[b][:, 2 * NH + g * 8:2 * NH + (g + 1) * 8].unsqueeze(2).to_broadcast([128, 8, 64]),
                            op=ALU.mult), reads=[Sf[g], ex[b]], writes=[Sf[g]])
                        P.op('dve', lambda e, g=g: e.tensor_tensor(out=Sf[g][:], in0=Sf[g][:],
                                                                  in1=PB[4][:, :].rearrange("p (h q) -> p h q", q=64), op=ALU.add),
                             reads=[Sf[g], PB[4]], writes=[Sf[g]])
                        P.op('act', lambda e, g=g: e.activation(out=Sb[g][:], in_=Sf[g][:], func=AF.Copy), reads=[Sf[g]], writes=[Sb[g]])
                    if d == 0:
                        P.dma('act', YF[tok, :].rearrange("t (h p) -> t h p", p=64), y[b][:], reads=[y[b]])
                    else:
                        yb = y[b]
                        P.op('dve', lambda e, b=b: e.tensor_tensor(out=yb[:], in0=yb[:], in1=yfw[b][:], op=ALU.add),
                             reads=[yb, yfw[b]], writes=[yb])
                        P.op('pool', lambda e, b=b: e.tensor_tensor(out=t3[:], in0=xtok[b][:],
                                                                   in1=dsk[:].unsqueeze(2).to_broadcast([128, NH, 64]), op=ALU.mult),
                             reads=[xtok[b], dsk], writes=[t3])
                        P.op('dve', lambda e: e.tensor_tensor(out=yb[:], in0=yb[:], in1=t3[:], op=ALU.add), reads=[yb, t3], writes=[yb])
                        ybf = yb[:].rearrange("p h q -> p (h q)")
                        P.op('pool', lambda e, b=b: e.tensor_tensor(out=ybf, in0=ybf, in1=zs[b][:], op=ALU.mult),
                             reads=[yb, zs[b]], writes=[yb])
                        sqf = sq[:].rearrange("p g q -> p (g q)")
                        P.op('pool', lambda e: e.tensor_tensor(out=sqf, in0=ybf, in1=ybf, op=ALU.mult), reads=[yb], writes=[sq])
                        P.op('dve', lambda e: e.tensor_reduce(out=ss[:], in_=sq[:], axis=AX.X, op=ALU.add), reads=[sq], writes=[ss])
                        P.op('act', lambda e: e.activation(out=ss[:], in_=ss[:], func=AF.Ln, bias=RMS_EPS, scale=1.0 / 512),
                             reads=[ss], writes=[ss])
                        P.op('act', lambda e: e.activation(out=ss[:], in_=ss[:], func=AF.Exp, scale=-0.5), reads=[ss], writes=[ss])
                        P.op('dve', lambda e: e.tensor_tensor(out=sq[:], in0=ybf.rearrange("p (g q) -> p g q", g=NG),
                                                              in1=ss[:].unsqueeze(2).to_broadcast([128, NG, 512]), op=ALU.mult),
                             reads=[yb, ss], writes=[sq])
                        P.op('pool', lambda e: e.tensor_tensor(out=yn[:], in0=sqf, in1=nwb[:], op=ALU.mult), reads=[sq, nwb], writes=[yn])
                        yt2 = ynT[ci % 2]
                        for q4 in range(2):
                            pb = PB[1] if q4 == 0 else PB[0]
                            for u in range(4):
                                i = q4 * 4 + u
                                P.op('pe', lambda e, i=i, u=u, pb=pb: e.matmul(pb[:, u * 128:(u + 1) * 128], lhsT=yn[:, i * 128:(i + 1) * 128],
                                                                              rhs=cmb[:, C_IDENT, :], start=True, stop=True),
                                     reads=[yn, cmb], writes=[pb])
                            P.op('act', lambda e, q4=q4, pb=pb: e.activation(out=yt2[:, q4 * 4:(q4 + 1) * 4, :],
                                                                            in_=pb[:, :].rearrange("p (u t) -> p u t", u=4), func=AF.Copy),
                                 reads=[pb], writes=[yt2])
                        P.dma('act', YNT.rearrange("(i p) s -> p i s", p=128)[:, :, tok], yt2[:], reads=[yt2])

                stage_a(0)
                for ci in range(len(order)):
                    if ci + 1 < len(order):
                        stage_a(ci + 1)
                    stage_b(ci)

        def odd_out_phase(oo):
            with Phase() as ph:
                wo = load_w(ph, ssd_w_out[oo], 1024, D)
                mix = [ph.sb([128, 8, 512], BF16) for _ in range(2)]
                ob = [ph.sb([128, 8, 512], BF16) for _ in range(2)]
                for ti, (c, t, l0, n) in enumerate(ctiles):
                    s0 = t * Sh + l0
                    b2 = ti % 2
                    P.dma('sp', mix[b2][:, :, :n], YNT.rearrange("(i p) s -> p i s", p=128)[:, :, s0:s0 + n], writes=[mix[b2]])
                    o_ = ob[b2]
                    for m in range(8):
                        po = PB[4 + m % 4]
                        for k in range(8):
                            P.op('pe', lambda e, m=m, k=k, po=po, b2=b2: e.matmul(
                                po[:, :n], lhsT=wo[:, k, m * 128:(m + 1) * 128], rhs=mix[b2][:, k, :n],
                                start=(k == 0), stop=(k == 7)), reads=[wo, mix[b2]], writes=[po])
                        if m % 2 == 0:
                            P.op('act', lambda e, m=m, po=po: e.activation(out=o_[:, m, :n], in_=po[:, :n], func=AF.Copy),
                                 reads=[po], writes=[o_])
                        else:
                            P.op('dve', lambda e, m=m, po=po: e.tensor_copy(out=o_[:, m, :n], in_=po[:, :n]), reads=[po], writes=[o_])
                    P.dma('act', Ppc[c][t].rearrange("k p s -> p k s"), o_[:, :, :n], reads=[o_], writes=[ppres[c]])
                    if t == 1:
                        rs_chunk(c)

        Xcur = xin
        for l in range(DEPTH):
            Xlast = yout if l == DEPTH - 1 else X2
            modh_phase(l, Xcur)
            if l % 2 == 0:
                ee = l // 2
                even_proj_phase(ee)
                gla_phase(ee, 0)
                gla_phase(ee, 1)
                attn_phase(ee)
            else:
                oo = l // 2
                odd_proj_phase(oo)
                conv_phase(oo)
                ssd_phase(oo, 0)
                ssd_phase(oo, 1)
            with Phase() as phw:
                win = load_w(phw, ffn_w_in[l], D, 2 * FFH)
                wout = load_w(phw, ffn_w_out[l], FFH, D)
                if l % 2 == 0:
                    even_out_phase(l // 2)
                else:
                    odd_out_phase(l // 2)
                ffn_phase(l, Xcur, Xlast, win, wout)
            Xcur = Xlast
        P.barrier()
        P.emit()
    return nc


def _prep_inputs(inputs, TL):
    x = np.asarray(inputs['x'], np.float32)
    B = x.shape[0]
    S = CTX + TL
    Sh = S // 2
    g = lambda k: np.asarray(inputs[k], np.float32)
    f32 = lambda a: np.ascontiguousarray(np.asarray(a, np.float32))
    shared = {
        'mod_w': f32(g('mod_w')),
        'mod_bT': f32(g('mod_b').reshape(4, 48, 128).transpose(2, 0, 1)),
        'ln_gT': f32(g('ln_g').reshape(4, 2, 8, 128).transpose(3, 0, 1, 2)),
        'ln_bT': f32(g('ln_b').reshape(4, 2, 8, 128).transpose(3, 0, 1, 2)),
        'ffn_w_in': f32(g('ffn_w_in')),
        'ffn_w_out': f32(g('ffn_w_out')),
        'gla_nw': f32(g('gla_norm_w').T),
        'diff_nw': f32(g('diff_norm_w').T),
        'diff_lam': f32(g('diff_lambda').reshape(1, 2, 256)),
        'cmat': _const_mats(),
    }
    cos, sin = _rope_tables(TL)
    shared['ropecos'] = cos
    shared['ropesin'] = sin
    per_rank = []
    evw, evo = g('ev_w_in'), g('ev_w_out')
    sw, so = g('ssd_w_in'), g('ssd_w_out')
    cwf, cbf = g('ssd_conv_w'), g('ssd_conv_b')
    for r in range(2):
        pr = {}
        cols = np.concatenate([
            np.arange(0, 256)[r * 128:(r + 1) * 128],
            256 + np.arange(256)[r * 128:(r + 1) * 128],
            512 + np.arange(512)[r * 256:(r + 1) * 256],
            1024 + np.arange(512)[r * 256:(r + 1) * 256],
            np.arange(1536, 1568),
            1568 + np.arange(512)[r * 256:(r + 1) * 256],
            2080 + np.arange(512)[r * 256:(r + 1) * 256],
            2592 + np.arange(512)[r * 256:(r + 1) * 256],
        ])
        pr['ev_w_in'] = f32(evw[:, :, cols])
        rows = np.concatenate([np.arange(512)[r * 256:(r + 1) * 256], 512 + np.arange(512)[r * 256:(r + 1) * 256]])
        pr['ev_w_out'] = f32(evo[:, rows, :])
        wg = np.zeros((2, 17, 2, 128), np.float32)
        wg[:, 0:16] = g('gla_w_gate2').transpose(0, 2, 1, 3)[..., r * 128:(r + 1) * 128]
        wg[:, 16] = g('gla_b_gate')[..., r * 128:(r + 1) * 128]
        pr['gla_wg'] = wg
        hs = slice(r * 16, (r + 1) * 16)
        zc = np.arange(2048)[r * 1024:(r + 1) * 1024]
        xc = 2048 + np.arange(2048)[r * 1024:(r + 1) * 1024]
        bc = 4096 + np.arange(512)[r * 256:(r + 1) * 256]
        cc = 4608 + np.arange(512)[r * 256:(r + 1) * 256]
        dc = np.concatenate([5120 + np.arange(32)[hs], 5152 + np.arange(32)[hs]])
        pr['ssd_w_in'] = f32(sw[:, :, np.concatenate([zc, xc, bc, cc, dc])])
        pr['ssd_w_out'] = f32(so[:, r * 1024:(r + 1) * 1024, :])
        cch = np.concatenate([xc, bc, cc]) - 2048
        pr['ssd_cw'] = f32(cwf[:, :, cch].reshape(2, 5, 12, 128).transpose(0, 3, 1, 2))
        pr['ssd_cb'] = f32(cbf[:, cch].reshape(2, 12, 128).transpose(0, 2, 1))
        rep = lambda a: f32(np.broadcast_to(np.asarray(a, np.float32)[:, None, :], (2, 128, np.asarray(a).shape[-1])))
        pr['ssd_dtb'] = rep(g('ssd_dt_bias')[:, :, hs].reshape(2, 32))
        pr['ssd_alog'] = rep(g('ssd_a_log')[:, :, hs].reshape(2, 32))
        pr['ssd_dsk'] = rep(g('ssd_d')[:, hs])
        pr['ssd_nw'] = rep(g('ssd_norm_w')[:, r * 1024:(r + 1) * 1024])
        pr['mctx'] = np.full((128, 1), 1.0 if r == 0 else 0.0, np.float32)
        per_rank.append(pr)
    maps = []
    for core in range(8):
        b, r = core % 4, core // 4
        b = b % B
        xc_ = np.concatenate([g('ctx')[b], x[b]], 0)
        m = dict(shared)
        m.update(per_rank[r])
        m['xin'] = np.ascontiguousarray(xc_[r * Sh:(r + 1) * Sh].T.reshape(8, 128, Sh))
        cs = np.stack([g('c')[b], g('c_ctx')], -1)
        m['csil'] = np.ascontiguousarray(cs.reshape(8, 128, 2).transpose(1, 0, 2))
        maps.append(m)
    return maps


def run(inputs, TL, DEPTH, dump=()):
    nc = build(TL, DEPTH, dump)
    maps = _prep_inputs(inputs, TL)
    return run_bass_kernel_spmd(nc, maps, core_ids=list(range(8)))


def _gather(res, B, TL):
    S = CTX + TL
    Sh = S // 2
    out = np.empty((B, TL, D), np.float32)
    for b in range(B):
        full = np.concatenate([res.results[b + 4 * r]['yout'].reshape(D, Sh) for r in range(2)], axis=1)
        out[b] = full[:, CTX:].T
    return out


def kernel(**inputs):
    x = np.asarray(inputs['x'])
    B, TL, _ = x.shape
    res = run(inputs, TL, 4)
    return _gather(res, B, TL)
```
